# Optimizing a Trainium2 kernel written in Bass

```python
import math
import jax
import jax.numpy as jnp
from jax import lax
import numpy as np

D_MODEL = 2048
BATCH = 4
SEQ = 2048
DEPTH = 2
DEC_BATCH = 128
DEC_SEQ = 8
PAST_LEN = 16384
PAGE_SIZE = 128

MIX_W = D_MODEL
N_BRANCH = 3
SSD_HEAD_DIM = 64
SSD_HEADS = MIX_W // SSD_HEAD_DIM
SSD_GROUPS = 4
SSD_STATE = 128
SSD_CONV = 4
SSD_CONV_DIM = MIX_W + 2 * SSD_GROUPS * SSD_STATE
SSD_CHUNK = 64
ML_HEADS = 4
ML_DV = MIX_W // ML_HEADS
ML_DK = ML_DV // 2
ML_CHUNK = 64
HG_EXPAND = 128
HG_HEADS = MIX_W // HG_EXPAND
HG_DK = HG_EXPAND
HG_DV = MIX_W // HG_HEADS
HG_CHUNK = 64
MEM_LEN = 256
X_HEADS = 4
X_HEAD_DIM = D_MODEL // X_HEADS
D_FF = ((8 * D_MODEL // 3 + 127) // 128) * 128
DN_ALPHA = (2.0 * DEPTH) ** 0.25
DN_BETA = (8.0 * DEPTH) ** -0.25
LN_EPS = 1e-5
RMS_EPS = 1e-6
IN_SPLITS = (MIX_W, SSD_CONV_DIM, SSD_HEADS,
             ML_HEADS * ML_DK, ML_HEADS * ML_DK, MIX_W, MIX_W,
             ML_HEADS, ML_HEADS,
             MIX_W, MIX_W, MIX_W, MIX_W,
             N_BRANCH * D_MODEL)
N_IN = sum(IN_SPLITS)

kernel_name = 'hybrid_ssd_mlstm_hgrn2_step'


def split_cols(h, sizes):
    parts, start = [], 0
    for s in sizes:
        parts.append(h[..., start:start + s])
        start += s
    return parts


def layer_norm(x, g, b):
    xf = x.astype(jnp.float32)
    mu = jnp.mean(xf, -1, keepdims=True)
    var = jnp.mean(jnp.square(xf - mu), -1, keepdims=True)
    return ((xf - mu) * lax.rsqrt(var + LN_EPS) * g.astype(jnp.float32) + b.astype(jnp.float32)).astype(x.dtype)


def rms_norm(x, g):
    xf = x.astype(jnp.float32)
    return xf * lax.rsqrt(jnp.mean(xf * xf, -1, keepdims=True) + RMS_EPS) * g.astype(jnp.float32)


def swiglu(x, w1, w3, w2):
    return (jax.nn.silu(x @ w1) * (x @ w3)) @ w2


def to_chunks(a, cs):
    nb, L = a.shape[0], a.shape[1]
    return jnp.moveaxis(a.reshape(nb, L // cs, cs, *a.shape[2:]), 1, 0)


def from_chunks(a):
    n, nb, cs = a.shape[0], a.shape[1], a.shape[2]
    return jnp.moveaxis(a, 0, 1).reshape(nb, n * cs, *a.shape[3:])


def causal_conv(u, buf, w, b):
    L = u.shape[1]
    up = jnp.concatenate([buf.astype(u.dtype), u], axis=1)
    out = b + up[:, 0:L] * w[0]
    for j in range(1, SSD_CONV):
        out = out + up[:, j:j + L] * w[j]
    return out, up[:, L:]


def ssd_scan(x, dt, a, bm, cm, h0):
    cs = math.gcd(x.shape[1], SSD_CHUNK)
    mask = jnp.tril(jnp.ones((cs, cs), bool))

    def step(h, inp):
        xc, dtc, ac, bc, cc = inp
        acum = jnp.cumsum(ac, axis=1)
        seg = acum[:, :, None] - acum[:, None, :]
        decay = jnp.exp(jnp.where(mask[None, :, :, None, None], seg, -jnp.inf))
        cb = jnp.einsum('btgn,bsgn->btsg', cc, bc)
        w = cb[..., None] * decay * dtc[:, None]
        y = jnp.einsum('btsgh,bsghp->btghp', w, xc)
        y = y + jnp.einsum('btgn,bghpn->btghp', cc, h) * jnp.exp(acum)[..., None]
        tail = jnp.exp(acum[:, -1:] - acum) * dtc
        h = h * jnp.exp(acum[:, -1])[..., None, None] + jnp.einsum('bsgh,bsgn,bsghp->bghpn', tail, bc, xc)
        return h, y

    xs = (to_chunks(x, cs), to_chunks(dt, cs), to_chunks(a, cs), to_chunks(bm, cs), to_chunks(cm, cs))
    h, ys = lax.scan(step, h0, xs)
    return from_chunks(ys), h


def mlstm_scan(q, k, v, logi, logf, c0, n0, m0):
    cs = math.gcd(q.shape[1], ML_CHUNK)
    mask = jnp.tril(jnp.ones((cs, cs), bool))

    def step(carry, inp):
        c, n, m = carry
        qc, kc, vc, ic, fc = inp
        b = jnp.cumsum(fc, axis=1)
        dmat = b[:, :, None] - b[:, None, :] + ic[:, None, :]
        dmat = jnp.where(mask[None, :, :, None], dmat, -jnp.inf)
        prev = b + m[:, None]
        mt = jnp.maximum(prev, jnp.max(dmat, axis=2))
        wts = jnp.exp(dmat - mt[:, :, None])
        sprev = jnp.exp(prev - mt)
        qk = jnp.einsum('bthd,bshd->btsh', qc, kc) * wts
        num = jnp.einsum('btsh,bshv->bthv', qk, vc) + jnp.einsum('bthd,bhdv->bthv', qc, c) * sprev[..., None]
        den = jnp.sum(qk, axis=2) + jnp.einsum('bthd,bhd->bth', qc, n) * sprev
        h = num / jnp.maximum(jnp.abs(den), jnp.exp(-mt))[..., None]
        m_new = mt[:, -1]
        dec_prev = jnp.exp(b[:, -1] + m - m_new)
        wk = jnp.exp(b[:, -1:] - b + ic - m_new[:, None])
        c = c * dec_prev[..., None, None] + jnp.einsum('bsh,bshd,bshv->bhdv', wk, kc, vc)
        n = n * dec_prev[..., None] + jnp.einsum('bsh,bshd->bhd', wk, kc)
        return (c, n, m_new), h

    xs = (to_chunks(q, cs), to_chunks(k, cs), to_chunks(v, cs), to_chunks(logi, cs), to_chunks(logf, cs))
    (c, n, m), hs = lax.scan(step, (c0, n0, m0), xs)
    return from_chunks(hs), c, n, m


def hgrn_scan(q, k, v, logf, s0):
    cs = math.gcd(q.shape[1], HG_CHUNK)
    mask = jnp.tril(jnp.ones((cs, cs), bool))

    def step(s, inp):
        qc, kc, vc, fc = inp
        b = jnp.cumsum(fc, axis=1)
        seg = b[:, :, None] - b[:, None, :]
        decay = jnp.exp(jnp.where(mask[None, :, :, None, None], seg, -jnp.inf))
        att = jnp.einsum('bthd,bshd,btshd->btsh', qc, kc, decay)
        y = jnp.einsum('btsh,bshv->bthv', att, vc) + jnp.einsum('bthd,bhdv->bthv', qc * jnp.exp(b), s)
        s = s * jnp.exp(b[:, -1])[..., None] + jnp.einsum('bshd,bshv->bhdv', kc * jnp.exp(b[:, -1:] - b), vc)
        return s, y

    xs = (to_chunks(q, cs), to_chunks(k, cs), to_chunks(v, cs), to_chunks(logf, cs))
    s, ys = lax.scan(step, s0, xs)
    return from_chunks(ys), s


def hgrn_lower_bounds(logits):
    p = jax.nn.softmax(logits.astype(jnp.float32), axis=0)
    c = jnp.cumsum(p, axis=0)
    return c - c[0:1]


def token_mixers(u, st, p, lb):
    f32 = jnp.float32
    nb, L = u.shape[0], u.shape[1]
    conv_buf, ssd_h, ml_c, ml_n, ml_m, hg_s = st
    hid = u @ p['w_in']
    (z, xbc, dt_raw, mq, mkey, mval, mo, mi, mf, hq, hf, hi, hgt, gates) = split_cols(hid, IN_SPLITS)

    xbc, conv_new = causal_conv(xbc, conv_buf, p['ssd_conv_w'], p['ssd_conv_b'])
    xbc = jax.nn.silu(xbc)
    xs, bm, cm = split_cols(xbc, (MIX_W, SSD_GROUPS * SSD_STATE, SSD_GROUPS * SSD_STATE))
    hpg = SSD_HEADS // SSD_GROUPS
    xs = xs.astype(f32).reshape(nb, L, SSD_GROUPS, hpg, SSD_HEAD_DIM)
    bm = bm.astype(f32).reshape(nb, L, SSD_GROUPS, SSD_STATE)
    cm = cm.astype(f32).reshape(nb, L, SSD_GROUPS, SSD_STATE)
    dt = jax.nn.softplus((dt_raw + p['ssd_dt_bias']).astype(f32)).reshape(nb, L, SSD_GROUPS, hpg)
    a_neg = -jnp.exp(p['ssd_a_log'].astype(f32)).reshape(SSD_GROUPS, hpg)
    h0 = ssd_h.astype(f32).reshape(nb, SSD_GROUPS, hpg, SSD_HEAD_DIM, SSD_STATE)
    ys, h1 = ssd_scan(xs, dt, dt * a_neg, bm, cm, h0)
    ys = ys + xs * p['ssd_d'].astype(f32).reshape(SSD_GROUPS, hpg, 1)
    ys = ys.reshape(nb, L, MIX_W) * jax.nn.silu(z.astype(f32))
    gw = MIX_W // SSD_GROUPS
    y_ssd = rms_norm(ys.reshape(nb, L, SSD_GROUPS, gw), p['ssd_norm'].reshape(SSD_GROUPS, gw)).reshape(nb, L, MIX_W)

    q = mq.astype(f32).reshape(nb, L, ML_HEADS, ML_DK)
    k = mkey.astype(f32).reshape(nb, L, ML_HEADS, ML_DK) * (ML_DK ** -0.5)
    v = mval.astype(f32).reshape(nb, L, ML_HEADS, ML_DV)
    logi = (mi + p['ml_gate_bias'][0]).astype(f32)
    logf = jax.nn.log_sigmoid((mf + p['ml_gate_bias'][1]).astype(f32))
    hm, c1, n1, m1 = mlstm_scan(q, k, v, logi, logf, ml_c.astype(f32), ml_n.astype(f32), ml_m.astype(f32))
    hm = rms_norm(hm, p['ml_norm'].reshape(ML_HEADS, ML_DV))
    y_ml = (hm * jax.nn.sigmoid(mo.astype(f32)).reshape(nb, L, ML_HEADS, ML_DV)).reshape(nb, L, MIX_W)

    lbh = lb.reshape(HG_HEADS, HG_DK)
    fr = hf.astype(f32).reshape(nb, L, HG_HEADS, HG_DK)
    logf_h = jnp.logaddexp(jnp.log(lbh), jnp.log1p(-lbh) + jax.nn.log_sigmoid(fr))
    k_h = (1.0 - lbh) * jax.nn.sigmoid(-fr)
    q_h = jax.nn.silu(hq.astype(f32)).reshape(nb, L, HG_HEADS, HG_DK)
    v_h = hi.astype(f32).reshape(nb, L, HG_HEADS, HG_DV)
    yh, s1 = hgrn_scan(q_h, k_h, v_h, logf_h, hg_s.astype(f32))
    yh = rms_norm(yh, p['hg_norm'].reshape(HG_HEADS, HG_DV))
    y_hg = (yh * jax.nn.sigmoid(hgt.astype(f32)).reshape(nb, L, HG_HEADS, HG_DV)).reshape(nb, L, MIX_W)

    branches = jnp.stack([y_ssd, y_ml, y_hg], axis=2).astype(u.dtype)
    proj = jnp.einsum('blnc,ncd->blnd', branches, p['w_branch'])
    gate = jax.nn.sigmoid(gates.reshape(nb, L, N_BRANCH, D_MODEL))
    out = jnp.sum(gate * proj, axis=2) @ p['w_mix_out']
    dt_out = u.dtype
    new_st = (conv_new.astype(dt_out),
              h1.reshape(nb, SSD_HEADS, SSD_HEAD_DIM, SSD_STATE).astype(dt_out),
              c1.astype(dt_out), n1.astype(dt_out), m1.astype(dt_out), s1.astype(dt_out))
    return out, new_st


def mem_cross_attention(x, mk, mv, wq, wo):
    nb, L = x.shape[0], x.shape[1]
    q = (x @ wq).reshape(nb, L, X_HEADS, X_HEAD_DIM)
    s = jnp.einsum('blhd,bmhd->bhlm', q, mk.astype(x.dtype)).astype(jnp.float32) * (X_HEAD_DIM ** -0.5)
    pr = jax.nn.softmax(s, axis=-1).astype(x.dtype)
    o = jnp.einsum('bhlm,bmhd->blhd', pr, mv.astype(x.dtype)).reshape(nb, L, D_MODEL)
    return o @ wo


def decoder_layer(x, mk, mv, st, p, lb):
    x = layer_norm(DN_ALPHA * x + 0.5 * swiglu(x, p['ffn_w1'][0], p['ffn_w3'][0], p['ffn_w2'][0]), p['ln_g'][0], p['ln_b'][0])
    mix, new_st = token_mixers(x, st, p, lb)
    x = layer_norm(DN_ALPHA * x + mix, p['ln_g'][1], p['ln_b'][1])
    x = layer_norm(DN_ALPHA * x + mem_cross_attention(x, mk, mv, p['x_wq'], p['x_wo']), p['ln_g'][2], p['ln_b'][2])
    x = layer_norm(DN_ALPHA * x + 0.5 * swiglu(x, p['ffn_w1'][1], p['ffn_w3'][1], p['ffn_w2'][1]), p['ln_g'][3], p['ln_b'][3])
    return x, new_st


def zero_states(nb, dtype):
    return (jnp.zeros((nb, SSD_CONV - 1, SSD_CONV_DIM), dtype),
            jnp.zeros((nb, SSD_HEADS, SSD_HEAD_DIM, SSD_STATE), dtype),
            jnp.zeros((nb, ML_HEADS, ML_DK, ML_DV), dtype),
            jnp.zeros((nb, ML_HEADS, ML_DK), dtype),
            jnp.zeros((nb, ML_HEADS), dtype),
            jnp.zeros((nb, HG_HEADS, HG_DK, HG_DV), dtype))


def setup_inputs(seed: int = 0) -> dict:
    key = jax.random.key(seed)
    keys = iter(jax.random.split(key, 48))

    def nrm(shape, scale):
        return jax.random.normal(next(keys), shape, jnp.float32) * scale

    def unif(shape, lo, hi):
        return jax.random.uniform(next(keys), shape, jnp.float32, lo, hi)

    x_prompt = nrm((BATCH, SEQ, D_MODEL), 1.0)
    x_sample = nrm((DEC_BATCH, DEC_SEQ, D_MODEL), 1.0)
    mem_prompt = nrm((BATCH, MEM_LEN, D_MODEL), 1.0)
    state_ssd_conv = nrm((DEPTH, DEC_BATCH, SSD_CONV - 1, SSD_CONV_DIM), 1.0)
    state_ssd = nrm((DEPTH, DEC_BATCH, SSD_HEADS, SSD_HEAD_DIM, SSD_STATE), 0.5)
    state_mlstm_c = nrm((DEPTH, DEC_BATCH, ML_HEADS, ML_DK, ML_DV), 0.1)
    state_mlstm_n = nrm((DEPTH, DEC_BATCH, ML_HEADS, ML_DK), 0.1)
    state_mlstm_m = unif((DEPTH, DEC_BATCH, ML_HEADS), -1.0, 1.0)
    state_hgrn = nrm((DEPTH, DEC_BATCH, HG_HEADS, HG_DK, HG_DV), 0.3)
    cache_mem_k = nrm((DEPTH, DEC_BATCH, MEM_LEN, X_HEADS, X_HEAD_DIM), 1.0)
    cache_mem_v = nrm((DEPTH, DEC_BATCH, MEM_LEN, X_HEADS, X_HEAD_DIM), DN_BETA)

    ln_g = 1.0 + nrm((DEPTH, 4, D_MODEL), 0.02)
    ln_b = nrm((DEPTH, 4, D_MODEL), 0.02)
    ffn_w1 = nrm((DEPTH, 2, D_MODEL, D_FF), D_MODEL ** -0.5)
    ffn_w3 = nrm((DEPTH, 2, D_MODEL, D_FF), D_MODEL ** -0.5)
    ffn_w2 = nrm((DEPTH, 2, D_FF, D_MODEL), D_FF ** -0.5 * DN_BETA)
    w_in = nrm((DEPTH, D_MODEL, N_IN), D_MODEL ** -0.5)
    ssd_conv_w = nrm((DEPTH, SSD_CONV, SSD_CONV_DIM), SSD_CONV ** -0.5)
    ssd_conv_b = nrm((DEPTH, SSD_CONV_DIM), 0.02)
    dt0 = jnp.exp(unif((DEPTH, SSD_HEADS), math.log(1e-3), math.log(1e-1)))
    ssd_dt_bias = dt0 + jnp.log(-jnp.expm1(-dt0))
    ssd_a_log = jnp.log(unif((DEPTH, SSD_HEADS), 1.0, 16.0))
    ssd_d = 1.0 + nrm((DEPTH, SSD_HEADS), 0.02)
    ssd_norm = 1.0 + nrm((DEPTH, MIX_W), 0.02)
    ml_i_bias = nrm((DEPTH, 1, ML_HEADS), 0.1)
    ml_f_bias = jnp.linspace(3.0, 6.0, ML_HEADS)[None, None, :] + nrm((DEPTH, 1, ML_HEADS), 0.1)
    ml_gate_bias = jnp.concatenate([ml_i_bias, ml_f_bias], axis=1)
    ml_norm = 1.0 + nrm((DEPTH, MIX_W), 0.02)
    hg_lb_logits = nrm((DEPTH, MIX_W), 0.5)
    hg_norm = 1.0 + nrm((DEPTH, MIX_W), 0.02)
    w_branch = nrm((DEPTH, N_BRANCH, MIX_W, D_MODEL), MIX_W ** -0.5)
    w_mix_out = nrm((DEPTH, D_MODEL, D_MODEL), D_MODEL ** -0.5 * DN_BETA)
    x_wq = nrm((DEPTH, D_MODEL, D_MODEL), D_MODEL ** -0.5)
    x_wk = nrm((DEPTH, D_MODEL, D_MODEL), D_MODEL ** -0.5)
    x_wv = nrm((DEPTH, D_MODEL, D_MODEL), D_MODEL ** -0.5 * DN_BETA)
    x_wo = nrm((DEPTH, D_MODEL, D_MODEL), D_MODEL ** -0.5 * DN_BETA)
    return {'x_prompt': x_prompt, 'x_sample': x_sample, 'mem_prompt': mem_prompt,
            'state_ssd_conv': state_ssd_conv, 'state_ssd': state_ssd,
            'state_mlstm_c': state_mlstm_c, 'state_mlstm_n': state_mlstm_n, 'state_mlstm_m': state_mlstm_m,
            'state_hgrn': state_hgrn, 'cache_mem_k': cache_mem_k, 'cache_mem_v': cache_mem_v,
            'ln_g': ln_g, 'ln_b': ln_b, 'ffn_w1': ffn_w1, 'ffn_w3': ffn_w3, 'ffn_w2': ffn_w2,
            'w_in': w_in, 'ssd_conv_w': ssd_conv_w, 'ssd_conv_b': ssd_conv_b, 'ssd_dt_bias': ssd_dt_bias,
            'ssd_a_log': ssd_a_log, 'ssd_d': ssd_d, 'ssd_norm': ssd_norm,
            'ml_gate_bias': ml_gate_bias, 'ml_norm': ml_norm, 'hg_lb_logits': hg_lb_logits, 'hg_norm': hg_norm,
            'w_branch': w_branch, 'w_mix_out': w_mix_out,
            'x_wq': x_wq, 'x_wk': x_wk, 'x_wv': x_wv, 'x_wo': x_wo}


def reference(x_prompt, x_sample, mem_prompt, state_ssd_conv, state_ssd, state_mlstm_c, state_mlstm_n,
              state_mlstm_m, state_hgrn, cache_mem_k, cache_mem_v, ln_g, ln_b, ffn_w1, ffn_w3, ffn_w2,
              w_in, ssd_conv_w, ssd_conv_b, ssd_dt_bias, ssd_a_log, ssd_d, ssd_norm, ml_gate_bias, ml_norm,
              hg_lb_logits, hg_norm, w_branch, w_mix_out, x_wq, x_wk, x_wv, x_wo):
    params = dict(ln_g=ln_g, ln_b=ln_b, ffn_w1=ffn_w1, ffn_w3=ffn_w3, ffn_w2=ffn_w2, w_in=w_in,
                  ssd_conv_w=ssd_conv_w, ssd_conv_b=ssd_conv_b, ssd_dt_bias=ssd_dt_bias, ssd_a_log=ssd_a_log,
                  ssd_d=ssd_d, ssd_norm=ssd_norm, ml_gate_bias=ml_gate_bias, ml_norm=ml_norm, hg_norm=hg_norm,
                  w_branch=w_branch, w_mix_out=w_mix_out, x_wq=x_wq, x_wo=x_wo)
    lbs = hgrn_lower_bounds(hg_lb_logits)

    def run(x, mem_k, mem_v, init):
        new = []
        for l in range(DEPTH):
            p = {name: arr[l] for name, arr in params.items()}
            x, st = decoder_layer(x, mem_k[l], mem_v[l], init(l), p, lbs[l])
            new.append(st)
        stacked = [jnp.stack([s[i] for s in new]) for i in range(6)]
        return x, stacked

    nb = x_prompt.shape[0]
    p_mem_k = jnp.stack([(mem_prompt @ x_wk[l]).reshape(nb, MEM_LEN, X_HEADS, X_HEAD_DIM) for l in range(DEPTH)])
    p_mem_v = jnp.stack([(mem_prompt @ x_wv[l]).reshape(nb, MEM_LEN, X_HEADS, X_HEAD_DIM) for l in range(DEPTH)])
    zeros = zero_states(nb, x_prompt.dtype)
    y_prompt, (p_conv, p_ssd, p_mlstm_c, p_mlstm_n, p_mlstm_m, p_hgrn) = run(
        x_prompt, p_mem_k, p_mem_v, lambda l: zeros)

    y_sample, (s_conv, s_ssd, s_mlstm_c, s_mlstm_n, s_mlstm_m, s_hgrn) = run(
        x_sample, cache_mem_k, cache_mem_v,
        lambda l: (state_ssd_conv[l], state_ssd[l], state_mlstm_c[l], state_mlstm_n[l], state_mlstm_m[l], state_hgrn[l]))

    return (y_prompt, y_sample, p_conv, p_ssd, p_mlstm_c, p_mlstm_n, p_mlstm_m, p_hgrn, p_mem_k, p_mem_v,
            s_conv, s_ssd, s_mlstm_c, s_mlstm_n, s_mlstm_m, s_hgrn)
```

```python
import numpy as np
from contextlib import ExitStack
import concourse.bass as bass
import concourse.mybir as mybir
from concourse.bass_utils import run_bass_kernel_spmd

F32 = mybir.dt.float32
BF16 = mybir.dt.bfloat16
AF = mybir.ActivationFunctionType
ALU = mybir.AluOpType
AX = mybir.AxisListType

D = 2048
DFF = 5504
NIN = 25640
DEPTH = 2
NPR = 2048
NSEQ_S = 16
LS = 8
NSM = NSEQ_S * LS
NT = NPR + NSM
NTILE = NT // 128
TBS = [(0, 512), (512, 512), (1024, 512), (1536, 512), (2048, 128)]
ALPHA = (2.0 * DEPTH) ** 0.25
XBC_ROWS = 3 + NPR + NSEQ_S * (3 + LS)

O_Z, O_XBC, O_DT, O_MQ, O_MK, O_MV, O_MO, O_MI, O_MF, O_HQ, O_HF, O_HI, O_HG, O_GT = (
    0, 2048, 5120, 5152, 6176, 7200, 9248, 11296, 11300, 11304, 13352, 15400, 17448, 19496)

C_ID, C_TRI, C_L, C_MNEG, C_SEL128, C_SEL8, C_ONES, C_END = 0, 128, 256, 384, 512, 640, 768, 896
PCS = 128


def make_consts():
    c = np.zeros((128, C_END), np.float32)
    c[:, C_ID:C_ID + 128] = np.eye(128, dtype=np.float32)
    k = np.arange(128)
    c[:, C_TRI:C_TRI + 128] = (k[:, None] <= k[None, :]).astype(np.float32)
    c[:, C_L:C_L + 128] = (k[:, None] > k[None, :]).astype(np.float32)
    c[:, C_MNEG:C_MNEG + 128] = np.where(k[None, :] <= k[:, None], 0.0, -1e30)
    c[127, C_SEL128:C_SEL128 + 128] = 1.0
    c[7, C_SEL8:C_SEL8 + 128] = 1.0
    c[:, C_ONES:C_ONES + 128] = 1.0
    return c


class Sem:
    __slots__ = ("h", "id", "target")

    def __init__(self, h, i):
        self.h = h
        self.id = i
        self.target = 0


class Tl:
    __slots__ = ("h", "w", "r")

    def __init__(self, h):
        self.h = h
        self.w = None
        self.r = {}

    def __getitem__(self, k):
        return self.h[k]


class Eng:
    def __init__(self, e, sem, name):
        self.e = e
        self.sem = sem
        self.n = 0
        self.known = {}
        self.name = name
        self.dsems = []
        self.di = 0


class K:
    def __init__(self, debug=False, stop=None):
        self.debug = debug
        self.stop = stop
        self.nc = nc = bass.Bass("TRN2", target_bir_lowering=False)
        self.uid = 0
        self.semc = 0
        self.pe = Eng(nc.tensor, self.newsem(), "pe")
        self.act = Eng(nc.scalar, self.newsem(), "act")
        self.dve = Eng(nc.vector, self.newsem(), "dve")
        self.sp = Eng(nc.sync, None, "sp")
        self.gq = Eng(nc.gpsimd, None, "gq")
        self.sp.dsems = [self.newsem() for _ in range(40)]
        self.gq.dsems = [self.newsem() for _ in range(24)]
        self.engs = [self.pe, self.act, self.dve, self.sp, self.gq]
        self.stk = None
        self.psb = [Tl(nc.alloc_psum_tensor(f"psb{i}", [128, 512], F32)) for i in range(8)]
        self.psi = 0
        self.evi = 0

    def newsem(self):
        self.semc += 1
        return Sem(self.nc.alloc_semaphore(f"s{self.semc}"), self.semc)

    def name(self, p="t"):
        self.uid += 1
        return f"{p}{self.uid}"

    def sb(self, shape, dt):
        return Tl(self.stk.enter_context(self.nc.sbuf_tensor(self.name("sb"), list(shape), dt)))

    def dram(self, nm, shape, dt):
        kind = "ExternalOutput" if self.debug else "Internal"
        return self.nc.dram_tensor(nm, list(shape), dt, kind=kind).ap()

    def ps(self):
        t = self.psb[self.psi % 8]
        self.psi += 1
        return t

    def _deps(self, eng, rd, wr):
        need = {}

        def add(ev):
            if ev is None:
                return
            s, v = ev
            o = need.get(s.id)
            if o is None or o[1] < v:
                need[s.id] = (s, v)

        for t in rd:
            add(t.w)
        for t in wr:
            add(t.w)
            for ev in t.r.values():
                add(ev)
        for sid, (s, v) in need.items():
            if eng is self.pe and s is self.pe.sem:
                continue
            if eng.known.get(sid, 0) < v:
                eng.e.wait_ge(s.h, v)
                eng.known[sid] = v

    def _mark(self, ev, rd, wr):
        s, v = ev
        for t in rd:
            t.r[s.id] = ev
        for t in wr:
            t.w = ev
            t.r = {}

    def op(self, eng, fn, rd=(), wr=()):
        self._deps(eng, rd, wr)
        ins = fn()
        eng.n += 1
        ins.then_inc(eng.sem.h, 1)
        self._mark((eng.sem, eng.n), rd, wr)

    def A(self, fn, rd=(), wr=()):
        self.op(self.act, fn, rd, wr)

    def V(self, fn, rd=(), wr=()):
        self.op(self.dve, fn, rd, wr)

    def ev(self, fn_act, fn_dve, rd=(), wr=()):
        self.evi += 1
        if self.evi % 2:
            self.A(fn_act, rd, wr)
        else:
            self.V(fn_dve, rd, wr)

    def mm(self, pst, groups, rd):
        pe = self.pe
        self._deps(pe, rd, [pst])
        last = None
        for out, pairs in groups:
            n = len(pairs)
            for i, (l, r) in enumerate(pairs):
                last = self.nc.tensor.matmul(out, lhsT=l, rhs=r, start=(i == 0), stop=(i == n - 1))
        pe.n += 1
        last.then_inc(pe.sem.h, 1)
        self._mark((pe.sem, pe.n), rd, [pst])

    def dma(self, q, out, in_, rd=(), wr=()):
        self._deps(q, rd, wr)
        s = q.dsems[q.di % len(q.dsems)]
        q.di += 1
        if q.known.get(s.id, 0) < s.target:
            q.e.wait_ge(s.h, s.target)
            q.known[s.id] = s.target
        q.e.dma_start(out=out, in_=in_).then_inc(s.h, 16)
        s.target += 16
        self._mark((s, s.target), rd, wr)

    def ld(self, t, out, in_):
        self.dma(self.sp, out, in_, wr=[t])

    def ldc(self, t, out, in_):
        self.dma(self.gq, out, in_, wr=[t])

    def st(self, out, t, in_):
        self.dma(self.sp, out, in_, rd=[t])

    def barrier(self):
        allv = [(e.sem, e.n) for e in (self.pe, self.act, self.dve)]
        for q in (self.sp, self.gq):
            allv += [(s, s.target) for s in q.dsems]
        for eng in self.engs:
            for s, v in allv:
                if v > 0 and eng.known.get(s.id, 0) < v:
                    eng.e.wait_ge(s.h, v)
                    eng.known[s.id] = v

    class _Phase:
        def __init__(self, k):
            self.k = k

        def __enter__(self):
            self.prev = self.k.stk
            self.k.stk = ExitStack()
            self.k.stk.__enter__()
            return self

        def __exit__(self, *a):
            self.k.barrier()
            self.k.stk.__exit__(None, None, None)
            self.k.stk = self.prev
            return False

    def phase(self):
        return K._Phase(self)

    def load_w(self, wap):
        sl = self.wslots[self.wsi % len(self.wslots)]
        self.wsi += 1
        kc = wap.shape[0] // 128
        nco = wap.shape[1]
        self.ldc(sl, sl[:, :kc, :nco], wap.rearrange("(kc p) n -> p kc n", p=128))
        return sl

    def run_jobs(self, jobs, nslots=4):
        self.wslots = [self.sb([128, 16, 512], BF16) for _ in range(nslots)]
        self.wsi = 0
        n = len(jobs)
        loaded = {}
        if n:
            loaded[0] = [self.load_w(w) for w in jobs[0][0]]
        for i in range(n):
            if i + 1 < n:
                loaded[i + 1] = [self.load_w(w) for w in jobs[i + 1][0]]
            jobs[i][1](loaded.pop(i))

    def bc(self, ap, shape, axis):
        return ap.unsqueeze(axis).to_broadcast(list(shape))

    def load_bc(self, dram_row, n, dt=F32):
        t = self.sb([128, n], dt)
        self.ld(t, t[:], dram_row.partition_broadcast(128))
        return t

    def transpose_to_aT(self, src, tt):
        nc = self.nc
        for g in range(4):
            p = self.ps()
            self.mm(p, [(p[:, j * 128:(j + 1) * 128],
                         [(src[:, (g * 4 + j) * 128:(g * 4 + j + 1) * 128], self.identb[:])]) for j in range(4)],
                    [src, self.identb])
            o = self.aTh[:, g * 4:(g + 1) * 4, tt * 128:(tt + 1) * 128]
            i = p[:].rearrange("p (a b) -> p a b", a=4)
            self.ev(lambda: nc.scalar.copy(out=o, in_=i), lambda: nc.vector.tensor_copy(out=o, in_=i),
                    rd=[p], wr=[self.aT[tt]])

    def alloc_aT(self):
        t = self.stk.enter_context(self.nc.sbuf_tensor(self.name("aT"), [128, 16, NT], BF16))
        self.aTh = t
        self.aT = [Tl(t) for _ in range(NTILE)]

    def aT_tiles(self, t0, n):
        return [self.aT[i] for i in range(t0 // 128, (t0 + n + 127) // 128)]

    def rstd(self, ss, np_, ncol, inv_n, eps):
        nc = self.nc
        a = ss[:np_, :ncol]
        self.V(lambda: nc.vector.tensor_scalar(out=a, in0=a, scalar1=inv_n, scalar2=eps, op0=ALU.mult, op1=ALU.add),
               rd=[ss], wr=[ss])
        self.A(lambda: nc.scalar.activation(out=a, in_=a, func=AF.Ln), rd=[ss], wr=[ss])
        self.A(lambda: nc.scalar.activation(out=a, in_=a, func=AF.Exp, scale=-0.5), rd=[ss], wr=[ss])

    def build(self):
        nc = self.nc
        I = {}

        def inp(nm, shape):
            I[nm] = nc.dram_tensor(nm, list(shape), F32, kind="ExternalInput").ap()

        inp("xp", [NPR, D]); inp("xsm", [NSM, D]); inp("memp", [256, D])
        inp("st_conv", [DEPTH, NSEQ_S, 3, 3072]); inp("st_ssd", [DEPTH, NSEQ_S, 2048, 128])
        inp("st_mc", [DEPTH, NSEQ_S, 4, 256, 512]); inp("st_mn", [DEPTH, NSEQ_S, 8, 128])
        inp("st_mm", [DEPTH, NSEQ_S, 4]); inp("st_hg", [DEPTH, NSEQ_S, 16, 128, 128])
        inp("ck", [DEPTH, NSEQ_S, 256, D]); inp("cv", [DEPTH, NSEQ_S, 256, D])
        inp("ln_g", [DEPTH, 4, D]); inp("ln_b", [DEPTH, 4, D])
        inp("ffn_w1", [DEPTH, 2, D, DFF]); inp("ffn_w3", [DEPTH, 2, D, DFF]); inp("ffn_w2", [DEPTH, 2, DFF, D])
        inp("w_in", [DEPTH, D, NIN]); inp("ssd_conv_w", [DEPTH, 4, 3072]); inp("ssd_conv_b", [DEPTH, 3072])
        inp("ssd_dt_bias", [DEPTH, 32]); inp("ssd_a_log", [DEPTH, 32]); inp("ssd_d", [DEPTH, 32])
        inp("ssd_norm", [DEPTH, D]); inp("ml_gate_bias", [DEPTH, 8]); inp("ml_norm", [DEPTH, D])
        inp("hg_lb_logits", [DEPTH, D]); inp("hg_norm", [DEPTH, D])
        inp("w_branch", [DEPTH, 3, D, D]); inp("w_mix_out", [DEPTH, D, D])
        inp("x_wq", [DEPTH, D, D]); inp("x_wk", [DEPTH, D, D]); inp("x_wv", [DEPTH, D, D]); inp("x_wo", [DEPTH, D, D])
        inp("consts", [128, C_END])
        self.I = I
        O = {}

        def outp(nm, shape):
            O[nm] = nc.dram_tensor(nm, list(shape), F32, kind="ExternalOutput").ap()

        outp("y", [NT, D]); outp("o_conv", [DEPTH, 17, 3, 3072]); outp("o_ssd", [DEPTH, 17, 2048, 128])
        outp("o_mc", [DEPTH, 17, 4, 256, 512]); outp("o_mn", [DEPTH, 17, 8, 128]); outp("o_mm", [DEPTH, 17, 4])
        outp("o_hg", [DEPTH, 17, 16, 128, 128]); outp("o_mk", [DEPTH, 256, D]); outp("o_mv", [DEPTH, 256, D])
        self.O = O
        S = {}
        S["X"] = self.dram("X", [NT, D], F32)
        S["Y"] = self.dram("Y", [NT, D], F32)
        S["G"] = self.dram("G", [DFF, NT], BF16)
        S["ZS"] = self.dram("ZS", [NT, D], BF16)
        S["XBC"] = self.dram("XBC", [XBC_ROWS, 3072], F32)
        S["XA"] = self.dram("XA", [NT, 3072], BF16)
        S["BCT"] = self.dram("BCT", [1024, NT], BF16)
        S["DTA"] = self.dram("DTA", [NT, 64], F32)
        S["MQ"] = self.dram("MQ", [1024, NT], BF16)
        S["MKT"] = self.dram("MKT", [1024, NT], BF16)
        S["MK"] = self.dram("MK", [NT, 1024], BF16)
        S["MV"] = self.dram("MV", [NT, D], BF16)
        S["MO"] = self.dram("MO", [NT, D], BF16)
        S["MIF"] = self.dram("MIF", [NT, 8], F32)
        S["HQ"] = self.dram("HQ", [D, NT], BF16)
        S["KT"] = self.dram("KT", [D, NT], BF16)
        S["LF"] = self.dram("LF", [NT, D], F32)
        S["HI"] = self.dram("HI", [NT, D], BF16)
        S["HG"] = self.dram("HG", [NT, D], BF16)
        S["GT"] = self.dram("GT", [3 * D, NT], BF16)
        S["YB"] = self.dram("YB", [3, NT, D], BF16)
        S["MB"] = self.dram("MB", [3, D, NT], BF16)
        S["QT"] = self.dram("QT", [D, NT], BF16)
        S["KTM"] = self.dram("KTM", [DEPTH, D, 256], BF16)
        S["VM"] = self.dram("VM", [DEPTH, 256, D], BF16)
        self.S = S

        with ExitStack() as gst:
            self.stk = gst
            self.cf = self.sb([128, C_END], F32)
            self.ld(self.cf, self.cf[:], I["consts"])
            self.identb = self.sb([128, 128], BF16)
            self.onesb = self.sb([128, 128], BF16)
            self.V(lambda: nc.vector.tensor_copy(out=self.identb[:], in_=self.cf[:, C_ID:C_ID + 128]),
                   rd=[self.cf], wr=[self.identb])
            self.V(lambda: nc.vector.tensor_copy(out=self.onesb[:], in_=self.cf[:, C_ONES:C_ONES + 128]),
                   rd=[self.cf], wr=[self.onesb])
            self.barrier()
            self.program()
            self.barrier()
        return nc

    def stopped(self, tag):
        return self.stop is not None and tag >= self.stop

    def program(self):
        I, S, O = self.I, self.S, self.O
        if self.stopped(0):
            return
        self.phase_memkv()
        if self.stopped(1):
            return
        for l in range(DEPTH):
            with self.phase():
                self.alloc_aT()
                if l == 0:
                    self.phase_init_xT()
                else:
                    self.phase_ln(l - 1, 3, S["X"], S["X"], final=False)
                self.phase_ffn_up(l, 0)
            if self.stopped(2):
                return
            self.phase_ffn_down(l, 0)
            with self.phase():
                self.alloc_aT()
                self.phase_ln(l, 0, None if l == 0 else S["X"], S["X"])
                if self.stopped(3):
                    return
                self.phase_win(l)
            if self.stopped(4):
                return
            self.phase_conv(l)
            if self.stopped(5):
                return
            self.phase_ssd(l)
            if self.stopped(6):
                return
            self.phase_mlstm(l)
            if self.stopped(7):
                return
            self.phase_hgrn(l)
            if self.stopped(8):
                return
            for b in range(3):
                with self.phase():
                    self.alloc_aT()
                    self.phase_branch(l, b)
            with self.phase():
                self.alloc_aT()
                self.phase_mixout(l)
            with self.phase():
                self.alloc_aT()
                self.phase_ln(l, 1, S["X"], S["X"])
                self.phase_q(l)
            if self.stopped(9):
                return
            with self.phase():
                self.alloc_aT()
                self.phase_attn(l)
                self.phase_wo(l)
            with self.phase():
                self.alloc_aT()
                self.phase_ln(l, 2, S["X"], S["X"])
                self.phase_ffn_up(l, 1)
            self.phase_ffn_down(l, 1)
            if self.stopped(10 + l):
                return
        with self.phase():
            self.phase_ln(DEPTH - 1, 3, S["X"], O["y"], final=True)

    def xsrc(self, tt):
        if tt < 16:
            return self.I["xp"][tt * 128:(tt + 1) * 128, :]
        return self.I["xsm"][:, :]

    def phase_init_xT(self):
        nc = self.nc
        with self.phase():
            xt = [self.sb([128, D], F32) for _ in range(2)]
            xb = [self.sb([128, D], BF16) for _ in range(2)]
            for tt in range(NTILE):
                a, b = xt[tt % 2], xb[tt % 2]
                self.ld(a, a[:], self.xsrc(tt))
                self.A(lambda: nc.scalar.copy(out=b[:], in_=a[:]), rd=[a], wr=[b])
                self.transpose_to_aT(b, tt)

    def phase_ln(self, l, idx, xres, xdst, final=False):
        nc = self.nc
        with self.phase():
            gb = self.load_bc(self.I["ln_g"][l, idx:idx + 1, :], D)
            bb = self.load_bc(self.I["ln_b"][l, idx:idx + 1, :], D)
            xt = [self.sb([128, D], F32) for _ in range(2)]
            yt = [self.sb([128, D], F32) for _ in range(2)]
            vt = [self.sb([128, D], F32) for _ in range(2)]
            sq = self.sb([128, D], F32)
            xb = [self.sb([128, D], BF16) for _ in range(2)]
            st = [self.sb([128, 4], F32) for _ in range(2)]
            for tt in range(NTILE):
                x, y, v, b, s = xt[tt % 2], yt[tt % 2], vt[tt % 2], xb[tt % 2], st[tt % 2]
                src = self.xsrc(tt) if xres is None else xres[tt * 128:(tt + 1) * 128, :]
                self.ld(x, x[:], src)
                self.ld(y, y[:], self.S["Y"][tt * 128:(tt + 1) * 128, :])
                self.V(lambda: nc.vector.scalar_tensor_tensor(out=v[:], in0=x[:], scalar=ALPHA, in1=y[:],
                                                              op0=ALU.mult, op1=ALU.add), rd=[x, y], wr=[v])
                self.V(lambda: nc.vector.tensor_reduce(out=s[:, 0:1], in_=v[:], axis=AX.X, op=ALU.add), rd=[v], wr=[s])
                self.V(lambda: nc.vector.tensor_scalar(out=s[:, 1:2], in0=s[:, 0:1], scalar1=-1.0 / D, scalar2=None,
                                                       op0=ALU.mult), rd=[s], wr=[s])
                self.A(lambda: nc.scalar.activation(out=v[:], in_=v[:], func=AF.Identity, bias=s[:, 1:2], scale=1.0),
                       rd=[v, s], wr=[v])
                self.A(lambda: nc.scalar.activation(out=sq[:], in_=v[:], func=AF.Square), rd=[v], wr=[sq])
                self.V(lambda: nc.vector.tensor_reduce(out=s[:, 2:3], in_=sq[:], axis=AX.X, op=ALU.add), rd=[sq], wr=[s])
                self.V(lambda: nc.vector.tensor_scalar(out=s[:, 2:3], in0=s[:, 2:3], scalar1=1.0 / D, scalar2=1e-5,
                                                       op0=ALU.mult, op1=ALU.add), rd=[s], wr=[s])
                self.A(lambda: nc.scalar.activation(out=s[:, 2:3], in_=s[:, 2:3], func=AF.Ln), rd=[s], wr=[s])
                self.A(lambda: nc.scalar.activation(out=s[:, 3:4], in_=s[:, 2:3], func=AF.Exp, scale=-0.5), rd=[s], wr=[s])
                self.V(lambda: nc.vector.scalar_tensor_tensor(out=v[:], in0=v[:], scalar=s[:, 3:4], in1=gb[:],
                                                              op0=ALU.mult, op1=ALU.mult), rd=[v, s, gb], wr=[v])
                self.V(lambda: nc.vector.tensor_tensor(out=v[:], in0=v[:], in1=bb[:], op=ALU.add), rd=[v, bb], wr=[v])
                self.st(xdst[tt * 128:(tt + 1) * 128, :], v, v[:])
                if not final:
                    self.A(lambda: nc.scalar.copy(out=b[:], in_=v[:]), rd=[v], wr=[b])
                    self.transpose_to_aT(b, tt)

    def phase_ffn_up(self, l, f):
        nc = self.nc
        W1 = self.I["ffn_w1"][l, f]
        W3 = self.I["ffn_w3"][l, f]
        G = self.S["G"]
        with self.phase():
            gch = [self.sb([128, NT], BF16) for _ in range(2)]
            tmp = [self.sb([128, 512], F32) for _ in range(2)]
            cnt = [0]

            def mk(hs, nco):
                def fn(slots):
                    s1, s3 = slots
                    for j in range(nco // 128):
                        g = gch[cnt[0] % 2]
                        cnt[0] += 1
                        for (t0, tn) in TBS:
                            at = self.aT_tiles(t0, tn)
                            p1 = self.ps()
                            self.mm(p1, [(p1[:, :tn], [(s1[:, kc, j * 128:(j + 1) * 128], self.aTh[:, kc, t0:t0 + tn])
                                                       for kc in range(16)])], [s1] + at)
                            p3 = self.ps()
                            self.mm(p3, [(p3[:, :tn], [(s3[:, kc, j * 128:(j + 1) * 128], self.aTh[:, kc, t0:t0 + tn])
                                                       for kc in range(16)])], [s3] + at)
                            tm = tmp[cnt[0] % 2]
                            cnt[0] += 1
                            self.A(lambda: nc.scalar.activation(out=tm[:, :tn], in_=p1[:, :tn], func=AF.Silu),
                                   rd=[p1], wr=[tm])
                            self.V(lambda: nc.vector.tensor_tensor(out=g[:, t0:t0 + tn], in0=tm[:, :tn], in1=p3[:, :tn],
                                                                   op=ALU.mult), rd=[tm, p3], wr=[g])
                        r0 = hs * 512 + j * 128
                        self.st(G[r0:r0 + 128, :], g, g[:])
                return fn

            jobs = []
            for hs in range(11):
                c0 = hs * 512
                nco = min(512, DFF - c0)
                jobs.append(([W1[:, c0:c0 + nco], W3[:, c0:c0 + nco]], mk(hs, nco)))
            self.run_jobs(jobs, nslots=4)

    def phase_ffn_down(self, l, f):
        nc = self.nc
        W2 = self.I["ffn_w2"][l, f]
        G = self.S["G"]
        Y = self.S["Y"]
        groups = [(i * 256, 256) for i in range(8)] + [(2048, 128)]
        with self.phase():
            gb = [self.sb([128, 43, 256], BF16) for _ in range(2)]
            yo = [self.sb([128, 512], F32) for _ in range(3)]
            cnt = [0]

            def ldg(gi):
                t0, tn = groups[gi]
                t = gb[gi % 2]
                self.ld(t, t[:, :, :tn], G[:, t0:t0 + tn].rearrange("(j p) t -> p j t", p=128))
                return t

            def mk(cb):
                def fn(slots):
                    cur = ldg(0)
                    for gi, (t0, tn) in enumerate(groups):
                        nxt = ldg(gi + 1) if gi + 1 < len(groups) else None
                        for ti in range(tn // 128):
                            p = self.ps()
                            pairs = [(cur[:, j, ti * 128:(ti + 1) * 128], slots[j // 16][:, j % 16, :]) for j in range(43)]
                            self.mm(p, [(p[:, :], pairs)], list(slots) + [cur])
                            o = yo[cnt[0] % 3]
                            cnt[0] += 1
                            self.ev(lambda: nc.scalar.mul(out=o[:], in_=p[:], mul=0.5),
                                    lambda: nc.vector.tensor_scalar(out=o[:], in0=p[:], scalar1=0.5, scalar2=None,
                                                                    op0=ALU.mult), rd=[p], wr=[o])
                            r0 = t0 + ti * 128
                            self.st(Y[r0:r0 + 128, cb * 512:(cb + 1) * 512], o, o[:])
                        cur = nxt
                return fn

            jobs = []
            for cb in range(4):
                cs_ = slice(cb * 512, (cb + 1) * 512)
                jobs.append(([W2[0:2048, cs_], W2[2048:4096, cs_], W2[4096:5504, cs_]], mk(cb)))
            self.run_jobs(jobs, nslots=6)

    def job_T(self, wap, evac):
        nco = wap.shape[1]

        def fn(slots):
            s = slots[0]
            for tt in range(NTILE):
                p = self.ps()
                self.mm(p, [(p[:, :nco], [(self.aTh[:, kc, tt * 128:(tt + 1) * 128], s[:, kc, :nco]) for kc in range(16)])],
                        [s, self.aT[tt]])
                evac(tt, p, nco)
        return ([wap], fn)

    def job_F(self, wap, evac):
        nco = wap.shape[1]

        def fn(slots):
            s = slots[0]
            for j in range(nco // 128):
                for (t0, tn) in TBS:
                    p = self.ps()
                    self.mm(p, [(p[:, :tn], [(s[:, kc, j * 128:(j + 1) * 128], self.aTh[:, kc, t0:t0 + tn])
                                             for kc in range(16)])], [s] + self.aT_tiles(t0, tn))
                    evac(j, t0, tn, p)
        return ([wap], fn)

    def xbc_rows(self, tt):
        if tt < 16:
            return [(3 + tt * 128, 128, 0)]
        return [(3 + NPR + s * 11 + 3, 8, s * 8) for s in range(NSEQ_S)]

    def phase_win(self, l):
        nc = self.nc
        I, S = self.I, self.S
        W = I["w_in"][l]
        with self.phase():
            dtb = self.load_bc(I["ssd_dt_bias"][l:l + 1, :], 32)
            alog = self.load_bc(I["ssd_a_log"][l:l + 1, :], 32)
            aneg = self.sb([128, 32], F32)
            self.A(lambda: nc.scalar.activation(out=aneg[:], in_=alog[:], func=AF.Exp), rd=[alog], wr=[aneg])
            self.V(lambda: nc.vector.tensor_scalar(out=aneg[:], in0=aneg[:], scalar1=-1.0, scalar2=None, op0=ALU.mult),
                   rd=[aneg], wr=[aneg])
            gbias = self.load_bc(I["ml_gate_bias"][l:l + 1, :], 8)
            lb = self.sb([128, D], F32)
            oml = self.sb([128, D], F32)
            lbT = self.sb([128, 16], F32)
            omlT = self.sb([128, 16], F32)
            if l == 0:
                self.V(lambda: nc.vector.memset(lb[:], 0.0), wr=[lb])
                self.V(lambda: nc.vector.memset(lbT[:], 0.0), wr=[lbT])
            else:
                l0 = self.load_bc(I["hg_lb_logits"][0:1, :], D)
                l1 = self.load_bc(I["hg_lb_logits"][1:2, :], D)
                self.V(lambda: nc.vector.tensor_tensor(out=lb[:], in0=l1[:], in1=l0[:], op=ALU.subtract), rd=[l0, l1], wr=[lb])
                self.A(lambda: nc.scalar.activation(out=lb[:], in_=lb[:], func=AF.Sigmoid), rd=[lb], wr=[lb])
                t0_ = self.sb([128, 16], F32)
                t1_ = self.sb([128, 16], F32)
                with nc.allow_non_contiguous_dma(reason="tiny param transpose"):
                    self.ld(t0_, t0_[:], I["hg_lb_logits"][0, :].rearrange("(c p) -> p c", p=128))
                    self.ld(t1_, t1_[:], I["hg_lb_logits"][1, :].rearrange("(c p) -> p c", p=128))
                self.V(lambda: nc.vector.tensor_tensor(out=lbT[:], in0=t1_[:], in1=t0_[:], op=ALU.subtract), rd=[t0_, t1_], wr=[lbT])
                self.A(lambda: nc.scalar.activation(out=lbT[:], in_=lbT[:], func=AF.Sigmoid), rd=[lbT], wr=[lbT])
            self.V(lambda: nc.vector.tensor_scalar(out=oml[:], in0=lb[:], scalar1=-1.0, scalar2=1.0, op0=ALU.mult, op1=ALU.add),
                   rd=[lb], wr=[oml])
            self.V(lambda: nc.vector.tensor_scalar(out=omlT[:], in0=lbT[:], scalar1=-1.0, scalar2=1.0, op0=ALU.mult, op1=ALU.add),
                   rd=[lbT], wr=[omlT])

            ob16 = [self.sb([128, 512], BF16) for _ in range(3)]
            of32 = [self.sb([128, 512], F32) for _ in range(3)]
            tmpf = [self.sb([128, 512], F32) for _ in range(2)]
            cnt = [0]

            def nb16():
                cnt[0] += 1
                return ob16[cnt[0] % 3]

            def nf32():
                cnt[0] += 1
                return of32[cnt[0] % 3]

            def ntmp():
                cnt[0] += 1
                return tmpf[cnt[0] % 2]

            jobs = []

            def T_act(dst, c0, func, scale=1.0):
                def evac(tt, p, nco):
                    o = nb16()
                    self.A(lambda: nc.scalar.activation(out=o[:, :nco], in_=p[:, :nco], func=func, scale=scale), rd=[p], wr=[o])
                    self.st(dst[tt * 128:(tt + 1) * 128, c0:c0 + nco], o, o[:, :nco])
                return evac

            def T_copy(dst, c0, scale=1.0):
                def evac(tt, p, nco):
                    o = nb16()
                    self.ev(lambda: nc.scalar.mul(out=o[:, :nco], in_=p[:, :nco], mul=scale),
                            lambda: nc.vector.tensor_scalar(out=o[:, :nco], in0=p[:, :nco], scalar1=scale, scalar2=None,
                                                            op0=ALU.mult), rd=[p], wr=[o])
                    self.st(dst[tt * 128:(tt + 1) * 128, c0:c0 + nco], o, o[:, :nco])
                return evac

            def F_act(dst, r0, func, scale=1.0):
                def evac(j, t0, tn, p):
                    o = nb16()
                    self.A(lambda: nc.scalar.activation(out=o[:, :tn], in_=p[:, :tn], func=func, scale=scale), rd=[p], wr=[o])
                    self.st(dst[r0 + j * 128:r0 + (j + 1) * 128, t0:t0 + tn], o, o[:, :tn])
                return evac

            def F_copy(dst, r0, scale=1.0):
                def evac(j, t0, tn, p):
                    o = nb16()
                    self.ev(lambda: nc.scalar.mul(out=o[:, :tn], in_=p[:, :tn], mul=scale),
                            lambda: nc.vector.tensor_scalar(out=o[:, :tn], in0=p[:, :tn], scalar1=scale, scalar2=None,
                                                            op0=ALU.mult), rd=[p], wr=[o])
                    self.st(dst[r0 + j * 128:r0 + (j + 1) * 128, t0:t0 + tn], o, o[:, :tn])
                return evac

            def blocks(n):
                return [(c, min(512, n - c)) for c in range(0, n, 512)]

            for c0, n in blocks(2048):
                jobs.append(self.job_T(W[:, O_Z + c0:O_Z + c0 + n], T_act(S["ZS"], c0, AF.Silu)))

            def xbc_evac(c0):
                def evac(tt, p, nco):
                    o = nf32()
                    self.ev(lambda: nc.scalar.copy(out=o[:, :nco], in_=p[:, :nco]),
                            lambda: nc.vector.tensor_copy(out=o[:, :nco], in_=p[:, :nco]), rd=[p], wr=[o])
                    for (r0, nr, p0) in self.xbc_rows(tt):
                        self.st(S["XBC"][r0:r0 + nr, c0:c0 + nco], o, o[p0:p0 + nr, :nco])
                return evac
            for c0, n in blocks(3072):
                jobs.append(self.job_T(W[:, O_XBC + c0:O_XBC + c0 + n], xbc_evac(c0)))

            def dt_evac(tt, p, nco):
                o = nf32()
                self.V(lambda: nc.vector.tensor_tensor(out=o[:, 0:32], in0=p[:, 0:32], in1=dtb[:], op=ALU.add), rd=[p, dtb], wr=[o])
                self.A(lambda: nc.scalar.activation(out=o[:, 0:32], in_=o[:, 0:32], func=AF.Exp), rd=[o], wr=[o])
                self.A(lambda: nc.scalar.activation(out=o[:, 0:32], in_=o[:, 0:32], func=AF.Ln, bias=1.0, scale=1.0), rd=[o], wr=[o])
                self.V(lambda: nc.vector.tensor_tensor(out=o[:, 32:64], in0=o[:, 0:32], in1=aneg[:], op=ALU.mult), rd=[o, aneg], wr=[o])
                self.st(S["DTA"][tt * 128:(tt + 1) * 128, :], o, o[:, 0:64])
            jobs.append(self.job_T(W[:, O_DT:O_DT + 32], dt_evac))

            for c0, n in blocks(1024):
                jobs.append(self.job_F(W[:, O_MQ + c0:O_MQ + c0 + n], F_copy(S["MQ"], c0)))
            for c0, n in blocks(1024):
                jobs.append(self.job_F(W[:, O_MK + c0:O_MK + c0 + n], F_copy(S["MKT"], c0, 256 ** -0.5)))
            for c0, n in blocks(1024):
                jobs.append(self.job_T(W[:, O_MK + c0:O_MK + c0 + n], T_copy(S["MK"], c0, 256 ** -0.5)))
            for c0, n in blocks(2048):
                jobs.append(self.job_T(W[:, O_MV + c0:O_MV + c0 + n], T_copy(S["MV"], c0)))
            for c0, n in blocks(2048):
                jobs.append(self.job_T(W[:, O_MO + c0:O_MO + c0 + n], T_act(S["MO"], c0, AF.Sigmoid)))

            def if_evac(tt, p, nco):
                o = nf32()
                self.V(lambda: nc.vector.tensor_tensor(out=o[:, 0:8], in0=p[:, 0:8], in1=gbias[:], op=ALU.add), rd=[p, gbias], wr=[o])
                self.st(S["MIF"][tt * 128:(tt + 1) * 128, :], o, o[:, 0:8])
            jobs.append(self.job_T(W[:, O_MI:O_MI + 8], if_evac))

            for c0, n in blocks(2048):
                jobs.append(self.job_F(W[:, O_HQ + c0:O_HQ + c0 + n], F_act(S["HQ"], c0, AF.Silu)))

            def kt_evac(c0):
                def evac(j, t0, tn, p):
                    tm = ntmp()
                    o = nb16()
                    cidx = (c0 // 128) + j
                    self.A(lambda: nc.scalar.activation(out=tm[:, :tn], in_=p[:, :tn], func=AF.Sigmoid, scale=-1.0), rd=[p], wr=[tm])
                    self.V(lambda: nc.vector.tensor_scalar(out=o[:, :tn], in0=tm[:, :tn], scalar1=omlT[:, cidx:cidx + 1],
                                                           scalar2=None, op0=ALU.mult), rd=[tm, omlT], wr=[o])
                    self.st(S["KT"][c0 + j * 128:c0 + (j + 1) * 128, t0:t0 + tn], o, o[:, :tn])
                return evac
            for c0, n in blocks(2048):
                jobs.append(self.job_F(W[:, O_HF + c0:O_HF + c0 + n], kt_evac(c0)))

            def lf_evac(c0):
                def evac(tt, p, nco):
                    tm = ntmp()
                    o = nf32()
                    self.A(lambda: nc.scalar.activation(out=tm[:, :nco], in_=p[:, :nco], func=AF.Sigmoid), rd=[p], wr=[tm])
                    self.V(lambda: nc.vector.tensor_tensor(out=tm[:, :nco], in0=tm[:, :nco], in1=oml[:, c0:c0 + nco], op=ALU.mult),
                           rd=[tm, oml], wr=[tm])
                    self.V(lambda: nc.vector.tensor_tensor(out=tm[:, :nco], in0=tm[:, :nco], in1=lb[:, c0:c0 + nco], op=ALU.add),
                           rd=[tm, lb], wr=[tm])
                    self.A(lambda: nc.scalar.activation(out=o[:, :nco], in_=tm[:, :nco], func=AF.Ln), rd=[tm], wr=[o])
                    self.st(S["LF"][tt * 128:(tt + 1) * 128, c0:c0 + nco], o, o[:, :nco])
                return evac
            for c0, n in blocks(2048):
                jobs.append(self.job_T(W[:, O_HF + c0:O_HF + c0 + n], lf_evac(c0)))
            for c0, n in blocks(2048):
                jobs.append(self.job_T(W[:, O_HI + c0:O_HI + c0 + n], T_copy(S["HI"], c0)))
            for c0, n in blocks(2048):
                jobs.append(self.job_T(W[:, O_HG + c0:O_HG + c0 + n], T_act(S["HG"], c0, AF.Sigmoid)))
            for c0, n in blocks(3 * D):
                jobs.append(self.job_F(W[:, O_GT + c0:O_GT + c0 + n], F_act(S["GT"], c0, AF.Sigmoid)))
            self.run_jobs(jobs, nslots=4)

    def phase_conv(self, l):
        nc = self.nc
        I, S, O = self.I, self.S, self.O
        XBC = S["XBC"]
        with self.phase():
            z = self.sb([48, 3072], F32)
            self.V(lambda: nc.vector.memset(z[:3, :], 0.0), wr=[z])
            self.st(XBC[0:3, :], z, z[:3, :])
            hs = self.sb([48, 3072], F32)
            self.ld(hs, hs[:], I["st_conv"][l].rearrange("s r c -> (s r) c"))
            for s in range(NSEQ_S):
                b0 = 3 + NPR + s * 11
                self.st(XBC[b0:b0 + 3, :], hs, hs[s * 3:(s + 1) * 3, :])
        with self.phase():
            t = self.sb([51, 3072], F32)
            self.ld(t, t[0:3, :], XBC[NPR:NPR + 3, :])
            self.st(O["o_conv"][l, 0], t, t[0:3, :])
            for s in range(NSEQ_S):
                b0 = 3 + NPR + s * 11
                self.ld(t, t[3 + s * 3:6 + s * 3, :], XBC[b0 + 8:b0 + 11, :])
            self.st(O["o_conv"][l, 1:17].rearrange("s r c -> (s r) c"), t, t[3:51, :])
            HW = 1536
            wj = [self.load_bc(I["ssd_conv_w"][l, j:j + 1, :], 3072) for j in range(4)]
            cb = self.load_bc(I["ssd_conv_b"][l:l + 1, :], 3072)
            u = [[self.sb([128, HW], F32) for _ in range(4)] for _ in range(2)]
            acc = [self.sb([128, HW], F32) for _ in range(2)]
            t2 = [self.sb([128, HW], F32) for _ in range(2)]
            xa = [self.sb([128, HW], BF16) for _ in range(2)]
            bct = [self.sb([128, 8, 128], BF16) for _ in range(2)]
            it = 0
            for tt in range(NTILE):
                for hf in range(2):
                    c0 = hf * HW
                    uu = u[it % 2]
                    ac, tm, xo = acc[it % 2], t2[it % 2], xa[it % 2]
                    it += 1
                    for j in range(4):
                        for (r0, nr, p0) in self.xbc_rows(tt):
                            self.ld(uu[j], uu[j][p0:p0 + nr, :], XBC[r0 - 3 + j:r0 - 3 + j + nr, c0:c0 + HW])
                    self.V(lambda: nc.vector.tensor_tensor(out=ac[:], in0=uu[0][:], in1=wj[0][:, c0:c0 + HW], op=ALU.mult),
                           rd=[uu[0], wj[0]], wr=[ac])
                    for j in range(1, 4):
                        self.V(lambda: nc.vector.tensor_tensor(out=tm[:], in0=uu[j][:], in1=wj[j][:, c0:c0 + HW], op=ALU.mult),
                               rd=[uu[j], wj[j]], wr=[tm])
                        self.V(lambda: nc.vector.tensor_tensor(out=ac[:], in0=ac[:], in1=tm[:], op=ALU.add), rd=[ac, tm], wr=[ac])
                    self.V(lambda: nc.vector.tensor_tensor(out=ac[:], in0=ac[:], in1=cb[:, c0:c0 + HW], op=ALU.add), rd=[ac, cb], wr=[ac])
                    self.A(lambda: nc.scalar.activation(out=xo[:], in_=ac[:], func=AF.Silu), rd=[ac], wr=[xo])
                    self.st(S["XA"][tt * 128:(tt + 1) * 128, c0:c0 + HW], xo, xo[:])
                    if hf == 1:
                        bt = bct[tt % 2]
                        for g2 in range(2):
                            p = self.ps()
                            self.mm(p, [(p[:, j * 128:(j + 1) * 128],
                                         [(xo[:, 512 + (g2 * 4 + j) * 128:512 + (g2 * 4 + j + 1) * 128], self.identb[:])])
                                        for j in range(4)], [xo, self.identb])
                            self.A(lambda: nc.scalar.copy(out=bt[:, g2 * 4:(g2 + 1) * 4, :],
                                                          in_=p[:].rearrange("p (a b) -> p a b", a=4)), rd=[p], wr=[bt])
                        self.st(S["BCT"][:, tt * 128:(tt + 1) * 128].rearrange("(j p) t -> p j t", p=128), bt, bt[:])

    def seqs(self, pcs=PCS):
        out = [(0, pcs, [c * pcs for c in range(NPR // pcs)])]
        for s in range(NSEQ_S):
            out.append((s + 1, 8, [NPR + s * 8]))
        return out

    def cview(self, c0, n, rows):
        return self.cf[:rows, c0:c0 + n]

    def phase_ssd(self, l):
        nc = self.nc
        I, S, O = self.I, self.S, self.O
        cf = self.cf
        with self.phase():
            Dbc = self.load_bc(I["ssd_d"][l:l + 1, :], 32)
            nw = self.load_bc(I["ssd_norm"][l:l + 1, :], D)
            hTs = [self.sb([128, D], F32) for _ in range(2)]
            hTbs = [self.sb([128, D], BF16) for _ in range(2)]
            nats = [self.sb([128, 16, 128], F32) for _ in range(2)]
            NB = 2
            xs = [self.sb([128, D], BF16) for _ in range(NB)]
            Bt = [self.sb([128, 512], BF16) for _ in range(NB)]
            BT = [self.sb([128, 4, 128], BF16) for _ in range(NB)]
            CT = [self.sb([128, 4, 128], BF16) for _ in range(NB)]
            dta = [self.sb([128, 64], F32) for _ in range(NB)]
            zs = [self.sb([128, D], BF16) for _ in range(NB)]
            ex = self.sb([128, 96], F32)
            R = self.sb([128, 32, 128], F32)
            E = self.sb([128, 32, 128], BF16)
            cbm = self.sb([128, 4, 128], BF16)
            wT = self.sb([128, 32, 128], BF16)
            xdt = self.sb([128, D], BF16)
            xdl = self.sb([128, D], BF16)
            yb = self.sb([128, D], F32)
            tmp = self.sb([128, D], F32)
            ss = self.sb([128, 4], F32)
            yo = [self.sb([128, D], BF16) for _ in range(2)]

            def loads(bi, cs, t0):
                self.ld(xs[bi], xs[bi][:cs, :], S["XA"][t0:t0 + cs, 0:2048])
                self.ld(Bt[bi], Bt[bi][:cs, :], S["XA"][t0:t0 + cs, 2048:2560])
                self.ld(BT[bi], BT[bi][:, :, :cs], S["BCT"][0:512, t0:t0 + cs].rearrange("(g n) t -> n g t", n=128))
                self.ld(CT[bi], CT[bi][:, :, :cs], S["BCT"][512:1024, t0:t0 + cs].rearrange("(g n) t -> n g t", n=128))
                self.ld(dta[bi], dta[bi][:cs, :], S["DTA"][t0:t0 + cs, :])
                self.ld(zs[bi], zs[bi][:cs, :], S["ZS"][t0:t0 + cs, :])

            chunks = []
            for (sq, cs, offs) in self.seqs():
                for ci, t0 in enumerate(offs):
                    chunks.append((sq, cs, t0, ci == 0, ci == len(offs) - 1))
            with nc.allow_non_contiguous_dma(reason="small chunk loads"):
                loads(0, chunks[0][1], chunks[0][2])
                for idx, (sq, cs, t0, first, last) in enumerate(chunks):
                    bi = idx % NB
                    hT, hTb, nat = hTs[sq % 2], hTbs[sq % 2], nats[sq % 2]
                    if idx + 1 < len(chunks):
                        loads((idx + 1) % NB, chunks[idx + 1][1], chunks[idx + 1][2])
                        nsq = chunks[idx + 1][0]
                        if chunks[idx + 1][3] and nsq >= 1:
                            self.ld(nats[nsq % 2], nats[nsq % 2][:], I["st_ssd"][l, nsq - 1].rearrange("(j p) n -> p j n", p=128))
                    x_, B_, BT_, CT_, d_, z_ = xs[bi], Bt[bi], BT[bi], CT[bi], dta[bi], zs[bi]
                    tri = cf[:cs, C_TRI:C_TRI + cs]
                    Lm = cf[:cs, C_L:C_L + cs]
                    if first:
                        if sq == 0:
                            self.V(lambda: nc.vector.memset(hT[:], 0.0), wr=[hT])
                            self.V(lambda: nc.vector.memset(hTb[:], 0.0), wr=[hTb])
                        else:
                            for g in range(4):
                                p = self.ps()
                                self.mm(p, [(p[:, j * 128:(j + 1) * 128], [(nat[:, g * 4 + j, :], cf[:, C_ID:C_ID + 128])])
                                            for j in range(4)], [nat, cf])
                                self.V(lambda: nc.vector.tensor_copy(out=hT[:, g * 512:(g + 1) * 512], in_=p[:]), rd=[p], wr=[hT])
                            self.A(lambda: nc.scalar.copy(out=hTb[:], in_=hT[:]), rd=[hT], wr=[hTb])
                    a_ = d_[:cs, 32:64]
                    dt_ = d_[:cs, 0:32]
                    pA = self.ps()
                    self.mm(pA, [(pA[:cs, 0:32], [(tri, a_)]), (pA[:cs, 32:64], [(Lm, a_)]),
                                 (pA[:, 64:96], [(cf[:cs, C_ONES:C_ONES + 128], a_)])], [cf, d_])
                    self.A(lambda: nc.scalar.activation(out=ex[:cs, 0:64], in_=pA[:cs, 0:64], func=AF.Exp), rd=[pA], wr=[ex])
                    self.A(lambda: nc.scalar.activation(out=ex[:, 64:96], in_=pA[:, 64:96], func=AF.Exp), rd=[pA], wr=[ex])
                    self.V(lambda: nc.vector.tensor_tensor(out=R[:cs, :, :cs], in0=self.bc(a_, [cs, 32, cs], 2),
                                                           in1=self.bc(tri, [cs, 32, cs], 1), op=ALU.mult), rd=[d_, cf], wr=[R])
                    hp = min(32, 512 // cs)
                    for q in range(32 // hp):
                        p = self.ps()
                        self.mm(p, [(p[:cs, :hp * cs].rearrange("p (a b) -> p a b", a=hp), [(Lm, R[:cs, q * hp:(q + 1) * hp, :cs])])], [cf, R])
                        self.A(lambda: nc.scalar.activation(out=E[:cs, q * hp:(q + 1) * hp, :cs],
                                                            in_=p[:cs, :hp * cs].rearrange("p (a b) -> p a b", a=hp), func=AF.Exp),
                               rd=[p], wr=[E])
                    pC = self.ps()
                    self.mm(pC, [(pC[:cs, g * cs:(g + 1) * cs], [(BT_[:, g, :cs], CT_[:, g, :cs])]) for g in range(4)], [BT_, CT_])
                    self.V(lambda: nc.vector.tensor_tensor(out=cbm[:cs, :, :cs], in0=pC[:cs, :4 * cs].rearrange("p (a b) -> p a b", a=4),
                                                           in1=self.bc(tri, [cs, 4, cs], 1), op=ALU.mult), rd=[pC, cf], wr=[cbm])
                    for g in range(4):
                        self.V(lambda: nc.vector.tensor_tensor(out=wT[:cs, g * 8:(g + 1) * 8, :cs], in0=E[:cs, g * 8:(g + 1) * 8, :cs],
                                                               in1=self.bc(cbm[:cs, g, :cs], [cs, 8, cs], 1), op=ALU.mult),
                               rd=[E, cbm], wr=[wT])
                    self.V(lambda: nc.vector.tensor_tensor(out=xdt[:cs, :].rearrange("p (h d) -> p h d", h=32),
                                                           in0=x_[:cs, :].rearrange("p (h d) -> p h d", h=32),
                                                           in1=self.bc(dt_, [cs, 32, 64], 2), op=ALU.mult), rd=[x_, d_], wr=[xdt])
                    for g in range(4):
                        pY = self.ps()
                        self.mm(pY, [(pY[:cs, hh * 64:(hh + 1) * 64],
                                      [(wT[:cs, g * 8 + hh, :cs], xdt[:cs, (g * 8 + hh) * 64:(g * 8 + hh + 1) * 64])]) for hh in range(8)],
                                [wT, xdt])
                        pI = self.ps()
                        self.mm(pI, [(pI[:cs, :512], [(CT_[:, g, :cs], hTb[:, g * 512:(g + 1) * 512])])], [CT_, hTb])
                        self.V(lambda: nc.vector.tensor_tensor(out=tmp[:cs, g * 512:(g + 1) * 512].rearrange("p (h d) -> p h d", h=8),
                                                               in0=pI[:cs, :512].rearrange("p (h d) -> p h d", h=8),
                                                               in1=self.bc(ex[:cs, g * 8:(g + 1) * 8], [cs, 8, 64], 2), op=ALU.mult),
                               rd=[pI, ex], wr=[tmp])
                        self.V(lambda: nc.vector.tensor_tensor(out=yb[:cs, g * 512:(g + 1) * 512], in0=pY[:cs, :512],
                                                               in1=tmp[:cs, g * 512:(g + 1) * 512], op=ALU.add), rd=[pY, tmp], wr=[yb])
                    self.V(lambda: nc.vector.tensor_tensor(out=tmp[:cs, :].rearrange("p (h d) -> p h d", h=32),
                                                           in0=x_[:cs, :].rearrange("p (h d) -> p h d", h=32),
                                                           in1=self.bc(Dbc[:cs, :], [cs, 32, 64], 2), op=ALU.mult), rd=[x_, Dbc], wr=[tmp])
                    self.V(lambda: nc.vector.tensor_tensor(out=yb[:cs, :], in0=yb[:cs, :], in1=tmp[:cs, :], op=ALU.add), rd=[yb, tmp], wr=[yb])
                    self.V(lambda: nc.vector.tensor_tensor(out=yb[:cs, :], in0=yb[:cs, :], in1=z_[:cs, :], op=ALU.mult), rd=[yb, z_], wr=[yb])
                    self.A(lambda: nc.scalar.activation(out=tmp[:cs, :], in_=yb[:cs, :], func=AF.Square), rd=[yb], wr=[tmp])
                    self.V(lambda: nc.vector.tensor_reduce(out=ss[:cs, :], in_=tmp[:cs, :].rearrange("p (g d) -> p g d", g=4),
                                                           axis=AX.X, op=ALU.add), rd=[tmp], wr=[ss])
                    self.rstd(ss, cs, 4, 1.0 / 512, 1e-6)
                    for g in range(4):
                        self.A(lambda: nc.scalar.activation(out=yb[:cs, g * 512:(g + 1) * 512], in_=yb[:cs, g * 512:(g + 1) * 512],
                                                            func=AF.Copy, scale=ss[:cs, g:g + 1]), rd=[yb, ss], wr=[yb])
                    yo_ = yo[idx % 2]
                    self.V(lambda: nc.vector.tensor_tensor(out=yo_[:cs, :], in0=yb[:cs, :], in1=nw[:cs, :], op=ALU.mult), rd=[yb, nw], wr=[yo_])
                    self.st(S["YB"][0, t0:t0 + cs, :], yo_, yo_[:cs, :])
                    self.V(lambda: nc.vector.tensor_tensor(out=xdl[:cs, :].rearrange("p (h d) -> p h d", h=32),
                                                           in0=xdt[:cs, :].rearrange("p (h d) -> p h d", h=32),
                                                           in1=self.bc(ex[:cs, 32:64], [cs, 32, 64], 2), op=ALU.mult), rd=[xdt, ex], wr=[xdl])
                    for g in range(4):
                        pU = self.ps()
                        self.mm(pU, [(pU[:, :512], [(B_[:cs, g * 128:(g + 1) * 128], xdl[:cs, g * 512:(g + 1) * 512])])], [B_, xdl])
                        self.V(lambda: nc.vector.tensor_tensor(out=hT[:, g * 512:(g + 1) * 512].rearrange("p (h d) -> p h d", h=8),
                                                               in0=hT[:, g * 512:(g + 1) * 512].rearrange("p (h d) -> p h d", h=8),
                                                               in1=self.bc(ex[:, 64 + g * 8:64 + (g + 1) * 8], [128, 8, 64], 2), op=ALU.mult),
                               rd=[hT, ex], wr=[hT])
                        self.V(lambda: nc.vector.tensor_tensor(out=hT[:, g * 512:(g + 1) * 512], in0=hT[:, g * 512:(g + 1) * 512],
                                                               in1=pU[:, :512], op=ALU.add), rd=[hT, pU], wr=[hT])
                    if not last:
                        self.A(lambda: nc.scalar.copy(out=hTb[:], in_=hT[:]), rd=[hT], wr=[hTb])
                    else:
                        for g in range(4):
                            p = self.ps()
                            self.mm(p, [(p[:, j * 128:(j + 1) * 128], [(hT[:, (g * 4 + j) * 128:(g * 4 + j + 1) * 128], cf[:, C_ID:C_ID + 128])])
                                        for j in range(4)], [hT, cf])
                            self.A(lambda: nc.scalar.copy(out=nat[:, g * 4:(g + 1) * 4, :], in_=p[:].rearrange("p (a b) -> p a b", a=4)),
                                   rd=[p], wr=[nat])
                        self.st(O["o_ssd"][l, sq].rearrange("(j p) n -> p j n", p=128), nat, nat[:])

    def phase_mlstm(self, l):
        nc = self.nc
        I, S, O = self.I, self.S, self.O
        cf = self.cf
        with self.phase():
            nw = self.load_bc(I["ml_norm"][l:l + 1, :], D)
            cS = [self.sb([128, 8, 512], F32) for _ in range(2)]
            cbS = [self.sb([128, 8, 512], BF16) for _ in range(2)]
            nS = [self.sb([128, 8], F32) for _ in range(2)]
            nbS = [self.sb([128, 8], BF16) for _ in range(2)]
            mbcS = [self.sb([128, 4], F32) for _ in range(2)]
            n8S = [self.sb([8, 128], F32) for _ in range(2)]
            NB = 2
            qT = [self.sb([128, 8, 128], BF16) for _ in range(NB)]
            kT = [self.sb([128, 8, 128], BF16) for _ in range(NB)]
            kt = [self.sb([128, 1024], BF16) for _ in range(NB)]
            v = [self.sb([128, D], BF16) for _ in range(NB)]
            mo = [self.sb([128, D], BF16) for _ in range(NB)]
            gif = [self.sb([128, 8], F32) for _ in range(NB)]
            sm = self.sb([128, 64], F32)
            Dg = self.sb([128, 4, 128], F32)
            dm = self.sb([128, 4, 128], F32)
            wts = self.sb([128, 4, 128], F32)
            Pm = self.sb([128, 4, 128], F32)
            Pb = self.sb([128, 4, 128], BF16)
            PTb = self.sb([128, 4, 128], BF16)
            tmpn = self.sb([128, 512], F32)
            hm = self.sb([128, D], F32)
            sqt = self.sb([128, D], F32)
            yo = [self.sb([128, D], BF16) for _ in range(2)]
            kw = self.sb([128, 1024], BF16)

            def loads(bi, cs, t0):
                self.ld(qT[bi], qT[bi][:, :, :cs], S["MQ"][:, t0:t0 + cs].rearrange("(j p) t -> p j t", p=128))
                self.ld(kT[bi], kT[bi][:, :, :cs], S["MKT"][:, t0:t0 + cs].rearrange("(j p) t -> p j t", p=128))
                self.ld(kt[bi], kt[bi][:cs, :], S["MK"][t0:t0 + cs, :])
                self.ld(v[bi], v[bi][:cs, :], S["MV"][t0:t0 + cs, :])
                self.ld(mo[bi], mo[bi][:cs, :], S["MO"][t0:t0 + cs, :])
                self.ld(gif[bi], gif[bi][:cs, :], S["MIF"][t0:t0 + cs, :])

            chunks = []
            for (sq, cs, offs) in self.seqs():
                for ci, t0 in enumerate(offs):
                    chunks.append((sq, cs, t0, ci == 0, ci == len(offs) - 1))
            with nc.allow_non_contiguous_dma(reason="small chunk loads"):
                loads(0, chunks[0][1], chunks[0][2])
                for idx, (sq, cs, t0, first, last) in enumerate(chunks):
                    bi = idx % NB
                    c, cb, n, nb, mbc, n8 = cS[sq % 2], cbS[sq % 2], nS[sq % 2], nbS[sq % 2], mbcS[sq % 2], n8S[sq % 2]
                    if idx + 1 < len(chunks):
                        loads((idx + 1) % NB, chunks[idx + 1][1], chunks[idx + 1][2])
                        nsq = chunks[idx + 1][0]
                        if chunks[idx + 1][3] and nsq >= 1:
                            pp = nsq % 2
                            self.ld(cS[pp], cS[pp][:], I["st_mc"][l, nsq - 1].rearrange("h (dc p) v -> p (h dc) v", p=128))
                            self.ld(n8S[pp], n8S[pp][:], I["st_mn"][l, nsq - 1])
                            self.ld(mbcS[pp], mbcS[pp][:], I["st_mm"][l, nsq - 1:nsq, :].partition_broadcast(128))
                    q_, kT_, kt_, v_, mo_, g_ = qT[bi], kT[bi], kt[bi], v[bi], mo[bi], gif[bi]
                    tri = cf[:cs, C_TRI:C_TRI + cs]
                    Lm = cf[:cs, C_L:C_L + cs]
                    idf = cf[:cs, C_ID:C_ID + cs]
                    sel = C_SEL128 if cs == PCS else C_SEL8
                    if first:
                        if sq == 0:
                            self.V(lambda: nc.vector.memset(c[:], 0.0), wr=[c])
                            self.V(lambda: nc.vector.memset(cb[:], 0.0), wr=[cb])
                            self.V(lambda: nc.vector.memset(n[:], 0.0), wr=[n])
                            self.V(lambda: nc.vector.memset(nb[:], 0.0), wr=[nb])
                            self.V(lambda: nc.vector.memset(mbc[:], 0.0), wr=[mbc])
                        else:
                            p = self.ps()
                            self.mm(p, [(p[:, 0:8], [(n8[:, :], cf[:8, C_ID:C_ID + 8])])], [n8, cf])
                            self.V(lambda: nc.vector.tensor_copy(out=n[:], in_=p[:, 0:8]), rd=[p], wr=[n])
                            self.A(lambda: nc.scalar.copy(out=nb[:], in_=n[:]), rd=[n], wr=[nb])
                            self.A(lambda: nc.scalar.copy(out=cb[:], in_=c[:]), rd=[c], wr=[cb])
                    self.A(lambda: nc.scalar.activation(out=sm[:cs, 0:4], in_=g_[:cs, 4:8], func=AF.Exp, scale=-1.0), rd=[g_], wr=[sm])
                    self.A(lambda: nc.scalar.activation(out=sm[:cs, 0:4], in_=sm[:cs, 0:4], func=AF.Ln, bias=1.0, scale=1.0), rd=[sm], wr=[sm])
                    self.V(lambda: nc.vector.tensor_scalar(out=sm[:cs, 0:4], in0=sm[:cs, 0:4], scalar1=-1.0, scalar2=None, op0=ALU.mult),
                           rd=[sm], wr=[sm])
                    pA = self.ps()
                    self.mm(pA, [(pA[:cs, 0:4], [(tri, sm[:cs, 0:4])]), (pA[:cs, 4:8], [(Lm, sm[:cs, 0:4])])], [cf, sm])
                    self.V(lambda: nc.vector.tensor_copy(out=sm[:cs, 4:12], in_=pA[:cs, 0:8]), rd=[pA], wr=[sm])
                    self.V(lambda: nc.vector.tensor_tensor(out=sm[:cs, 12:16], in0=g_[:cs, 0:4], in1=sm[:cs, 4:8], op=ALU.subtract),
                           rd=[g_, sm], wr=[sm])
                    self.V(lambda: nc.vector.tensor_tensor(out=Dg[:cs, :, :cs], in0=self.bc(idf, [cs, 4, cs], 1),
                                                           in1=self.bc(sm[:cs, 12:16], [cs, 4, cs], 2), op=ALU.mult), rd=[cf, sm], wr=[Dg])
                    pR = self.ps()
                    self.mm(pR, [(pR[:cs, :4 * cs].rearrange("p (a b) -> p a b", a=4), [(cf[:cs, C_ONES:C_ONES + cs], Dg[:cs, :, :cs])])], [cf, Dg])
                    self.V(lambda: nc.vector.tensor_tensor(out=dm[:cs, :, :cs], in0=pR[:cs, :4 * cs].rearrange("p (a b) -> p a b", a=4),
                                                           in1=self.bc(sm[:cs, 4:8], [cs, 4, cs], 2), op=ALU.add), rd=[pR, sm], wr=[dm])
                    self.V(lambda: nc.vector.tensor_tensor(out=dm[:cs, :, :cs], in0=dm[:cs, :, :cs],
                                                           in1=self.bc(cf[:cs, C_MNEG:C_MNEG + cs], [cs, 4, cs], 1), op=ALU.add), rd=[dm, cf], wr=[dm])
                    self.V(lambda: nc.vector.tensor_reduce(out=sm[:cs, 16:20], in_=dm[:cs, :, :cs], axis=AX.X, op=ALU.max), rd=[dm], wr=[sm])
                    self.V(lambda: nc.vector.tensor_tensor(out=sm[:cs, 20:24], in0=sm[:cs, 4:8], in1=mbc[:cs, :], op=ALU.add), rd=[sm, mbc], wr=[sm])
                    self.V(lambda: nc.vector.tensor_tensor(out=sm[:cs, 24:28], in0=sm[:cs, 20:24], in1=sm[:cs, 16:20], op=ALU.max), rd=[sm], wr=[sm])
                    self.V(lambda: nc.vector.tensor_scalar(out=sm[:cs, 28:32], in0=sm[:cs, 24:28], scalar1=-1.0, scalar2=None, op0=ALU.mult),
                           rd=[sm], wr=[sm])
                    for h in range(4):
                        self.A(lambda: nc.scalar.activation(out=wts[:cs, h, :cs], in_=dm[:cs, h, :cs], func=AF.Exp,
                                                            bias=sm[:cs, 28 + h:29 + h], scale=1.0), rd=[dm, sm], wr=[wts])
                    self.V(lambda: nc.vector.tensor_tensor(out=sm[:cs, 32:36], in0=sm[:cs, 20:24], in1=sm[:cs, 24:28], op=ALU.subtract),
                           rd=[sm], wr=[sm])
                    self.A(lambda: nc.scalar.activation(out=sm[:cs, 32:36], in_=sm[:cs, 32:36], func=AF.Exp), rd=[sm], wr=[sm])
                    self.A(lambda: nc.scalar.activation(out=sm[:cs, 36:40], in_=sm[:cs, 28:32], func=AF.Exp), rd=[sm], wr=[sm])
                    pQ = self.ps()
                    self.mm(pQ, [(pQ[:cs, h * cs:(h + 1) * cs], [(q_[:, 2 * h + dc, :cs], kT_[:, 2 * h + dc, :cs]) for dc in range(2)])
                                 for h in range(4)], [q_, kT_])
                    self.V(lambda: nc.vector.tensor_tensor(out=Pm[:cs, :, :cs], in0=pQ[:cs, :4 * cs].rearrange("p (a b) -> p a b", a=4),
                                                           in1=wts[:cs, :, :cs], op=ALU.mult), rd=[pQ, wts], wr=[Pm])
                    self.V(lambda: nc.vector.tensor_reduce(out=sm[:cs, 40:44], in_=Pm[:cs, :, :cs], axis=AX.X, op=ALU.add), rd=[Pm], wr=[sm])
                    self.A(lambda: nc.scalar.copy(out=Pb[:cs, :, :cs], in_=Pm[:cs, :, :cs]), rd=[Pm], wr=[Pb])
                    pP = self.ps()
                    self.mm(pP, [(pP[:cs, h * cs:(h + 1) * cs], [(Pb[:cs, h, :cs], self.identb[:cs, :cs])]) for h in range(4)], [Pb, self.identb])
                    self.A(lambda: nc.scalar.copy(out=PTb[:cs, :, :cs], in_=pP[:cs, :4 * cs].rearrange("p (a b) -> p a b", a=4)), rd=[pP], wr=[PTb])
                    pD = self.ps()
                    self.mm(pD, [(pD[:cs, h:h + 1], [(q_[:, 2 * h + dc, :cs], nb[:, 2 * h + dc:2 * h + dc + 1]) for dc in range(2)])
                                 for h in range(4)], [q_, nb])
                    self.V(lambda: nc.vector.tensor_tensor(out=sm[:cs, 44:48], in0=pD[:cs, 0:4], in1=sm[:cs, 32:36], op=ALU.mult), rd=[pD, sm], wr=[sm])
                    self.V(lambda: nc.vector.tensor_tensor(out=sm[:cs, 40:44], in0=sm[:cs, 40:44], in1=sm[:cs, 44:48], op=ALU.add), rd=[sm], wr=[sm])
                    self.V(lambda: nc.vector.tensor_scalar(out=sm[:cs, 44:48], in0=sm[:cs, 40:44], scalar1=-1.0, scalar2=None, op0=ALU.mult), rd=[sm], wr=[sm])
                    self.V(lambda: nc.vector.tensor_tensor(out=sm[:cs, 40:44], in0=sm[:cs, 40:44], in1=sm[:cs, 44:48], op=ALU.max), rd=[sm], wr=[sm])
                    self.V(lambda: nc.vector.tensor_tensor(out=sm[:cs, 40:44], in0=sm[:cs, 40:44], in1=sm[:cs, 36:40], op=ALU.max), rd=[sm], wr=[sm])
                    self.V(lambda: nc.vector.reciprocal(out=sm[:cs, 44:48], in_=sm[:cs, 40:44]), rd=[sm], wr=[sm])
                    for h in range(4):
                        pN = self.ps()
                        self.mm(pN, [(pN[:cs, :512], [(PTb[:cs, h, :cs], v_[:cs, h * 512:(h + 1) * 512])])], [PTb, v_])
                        pNi = self.ps()
                        self.mm(pNi, [(pNi[:cs, :512], [(q_[:, 2 * h + dc, :cs], cb[:, 2 * h + dc, :]) for dc in range(2)])], [q_, cb])
                        self.A(lambda: nc.scalar.activation(out=tmpn[:cs, :], in_=pNi[:cs, :512], func=AF.Copy, scale=sm[:cs, 32 + h:33 + h]),
                               rd=[pNi, sm], wr=[tmpn])
                        self.V(lambda: nc.vector.tensor_tensor(out=tmpn[:cs, :], in0=tmpn[:cs, :], in1=pN[:cs, :512], op=ALU.add), rd=[tmpn, pN], wr=[tmpn])
                        self.V(lambda: nc.vector.tensor_scalar(out=hm[:cs, h * 512:(h + 1) * 512], in0=tmpn[:cs, :], scalar1=sm[:cs, 44 + h:45 + h],
                                                               scalar2=None, op0=ALU.mult), rd=[tmpn, sm], wr=[hm])
                    self.A(lambda: nc.scalar.activation(out=sqt[:cs, :], in_=hm[:cs, :], func=AF.Square), rd=[hm], wr=[sqt])
                    self.V(lambda: nc.vector.tensor_reduce(out=sm[:cs, 60:64], in_=sqt[:cs, :].rearrange("p (g d) -> p g d", g=4),
                                                           axis=AX.X, op=ALU.add), rd=[sqt], wr=[sm])
                    a_ss = sm[:cs, 60:64]
                    self.V(lambda: nc.vector.tensor_scalar(out=a_ss, in0=a_ss, scalar1=1.0 / 512, scalar2=1e-6, op0=ALU.mult, op1=ALU.add), rd=[sm], wr=[sm])
                    self.A(lambda: nc.scalar.activation(out=a_ss, in_=a_ss, func=AF.Ln), rd=[sm], wr=[sm])
                    self.A(lambda: nc.scalar.activation(out=a_ss, in_=a_ss, func=AF.Exp, scale=-0.5), rd=[sm], wr=[sm])
                    for h in range(4):
                        self.A(lambda: nc.scalar.activation(out=hm[:cs, h * 512:(h + 1) * 512], in_=hm[:cs, h * 512:(h + 1) * 512],
                                                            func=AF.Copy, scale=sm[:cs, 60 + h:61 + h]), rd=[hm, sm], wr=[hm])
                    self.V(lambda: nc.vector.tensor_tensor(out=hm[:cs, :], in0=hm[:cs, :], in1=nw[:cs, :], op=ALU.mult), rd=[hm, nw], wr=[hm])
                    yo_ = yo[idx % 2]
                    self.V(lambda: nc.vector.tensor_tensor(out=yo_[:cs, :], in0=hm[:cs, :], in1=mo_[:cs, :], op=ALU.mult), rd=[hm, mo_], wr=[yo_])
                    self.st(S["YB"][1, t0:t0 + cs, :], yo_, yo_[:cs, :])
                    pM = self.ps()
                    self.mm(pM, [(pM[:cs, 0:4], [(cf[:cs, sel:sel + cs], sm[:cs, 24:28])]),
                                 (pM[:, 4:8], [(cf[:cs, sel:sel + 128], sm[:cs, 32:36])]),
                                 (pM[:, 8:12], [(cf[:cs, sel:sel + 128], sm[:cs, 24:28])])], [cf, sm])
                    self.V(lambda: nc.vector.tensor_copy(out=sm[:cs, 48:52], in_=pM[:cs, 0:4]), rd=[pM], wr=[sm])
                    self.V(lambda: nc.vector.tensor_copy(out=sm[:, 52:56], in_=pM[:, 4:8]), rd=[pM], wr=[sm])
                    self.V(lambda: nc.vector.tensor_copy(out=mbc[:], in_=pM[:, 8:12]), rd=[pM], wr=[mbc])
                    self.V(lambda: nc.vector.tensor_tensor(out=sm[:cs, 56:60], in0=sm[:cs, 8:12], in1=g_[:cs, 0:4], op=ALU.add), rd=[sm, g_], wr=[sm])
                    self.V(lambda: nc.vector.tensor_tensor(out=sm[:cs, 56:60], in0=sm[:cs, 56:60], in1=sm[:cs, 48:52], op=ALU.subtract), rd=[sm], wr=[sm])
                    self.A(lambda: nc.scalar.activation(out=sm[:cs, 56:60], in_=sm[:cs, 56:60], func=AF.Exp), rd=[sm], wr=[sm])
                    self.V(lambda: nc.vector.tensor_tensor(out=kw[:cs, :].rearrange("p (h d) -> p h d", h=4),
                                                           in0=kt_[:cs, :].rearrange("p (h d) -> p h d", h=4),
                                                           in1=self.bc(sm[:cs, 56:60], [cs, 4, 256], 2), op=ALU.mult), rd=[kt_, sm], wr=[kw])
                    pNn = self.ps()
                    self.mm(pNn, [(pNn[:, j:j + 1], [(kw[:cs, j * 128:(j + 1) * 128], self.onesb[:cs, 0:1])]) for j in range(8)], [kw, self.onesb])
                    self.V(lambda: nc.vector.tensor_tensor(out=n[:].rearrange("p (h d) -> p h d", h=4), in0=n[:].rearrange("p (h d) -> p h d", h=4),
                                                           in1=self.bc(sm[:, 52:56], [128, 4, 2], 2), op=ALU.mult), rd=[n, sm], wr=[n])
                    self.V(lambda: nc.vector.tensor_tensor(out=n[:], in0=n[:], in1=pNn[:, 0:8], op=ALU.add), rd=[n, pNn], wr=[n])
                    for j in range(8):
                        h = j // 2
                        pU = self.ps()
                        self.mm(pU, [(pU[:, :512], [(kw[:cs, j * 128:(j + 1) * 128], v_[:cs, h * 512:(h + 1) * 512])])], [kw, v_])
                        self.V(lambda: nc.vector.scalar_tensor_tensor(out=c[:, j, :], in0=c[:, j, :], scalar=sm[:, 52 + h:53 + h], in1=pU[:, :512],
                                                                      op0=ALU.mult, op1=ALU.add), rd=[c, sm, pU], wr=[c])
                    if not last:
                        self.A(lambda: nc.scalar.copy(out=cb[:], in_=c[:]), rd=[c], wr=[cb])
                        self.A(lambda: nc.scalar.copy(out=nb[:], in_=n[:]), rd=[n], wr=[nb])
                    else:
                        self.st(O["o_mc"][l, sq].rearrange("h (dc p) v -> p (h dc) v", p=128), c, c[:])
                        p = self.ps()
                        self.mm(p, [(p[:8, 0:128], [(n[:, :], cf[:, C_ID:C_ID + 128])])], [n, cf])
                        self.V(lambda: nc.vector.tensor_copy(out=n8[:], in_=p[:8, 0:128]), rd=[p], wr=[n8])
                        self.st(O["o_mn"][l, sq], n8, n8[:])
                        self.st(O["o_mm"][l, sq:sq + 1, :], mbc, mbc[0:1, :])

    def phase_hgrn(self, l):
        nc = self.nc
        I, S, O = self.I, self.S, self.O
        cf = self.cf
        with self.phase():
            nw = self.load_bc(I["hg_norm"][l:l + 1, :], D)
            StS = [self.sb([128, 16, 128], F32) for _ in range(2)]
            SbS = [self.sb([128, 16, 128], BF16) for _ in range(2)]
            NB = 2
            lf = [self.sb([128, D], F32) for _ in range(NB)]
            qT = [self.sb([128, 16, 128], BF16) for _ in range(NB)]
            kT = [self.sb([128, 16, 128], BF16) for _ in range(NB)]
            v = [self.sb([128, D], BF16) for _ in range(NB)]
            g = [self.sb([128, D], BF16) for _ in range(NB)]
            erem = self.sb([128, D], F32)
            ef = self.sb([128, D], F32)
            khat = self.sb([128, D], BF16)
            bT = self.sb([128, 16, 128], F32)
            dl = self.sb([128, 16, 128], F32)
            e1 = self.sb([128, 16, 128], BF16)
            e3 = self.sb([128, 16, 128], F32)
            qt = self.sb([128, 16, 128], BF16)
            ktl = self.sb([128, 16, 128], BF16)
            qb = self.sb([128, 16, 128], BF16)
            attm = self.sb([128, 16, 128], BF16)
            y = self.sb([128, D], F32)
            sq = self.sb([128, D], F32)
            ss = self.sb([128, 16], F32)
            yo = [self.sb([128, D], BF16) for _ in range(2)]

            def loads(bi, cs, t0):
                self.ld(lf[bi], lf[bi][:cs, :], S["LF"][t0:t0 + cs, :])
                self.ld(qT[bi], qT[bi][:, :, :cs], S["HQ"][:, t0:t0 + cs].rearrange("(h p) t -> p h t", p=128))
                self.ld(kT[bi], kT[bi][:, :, :cs], S["KT"][:, t0:t0 + cs].rearrange("(h p) t -> p h t", p=128))
                self.ld(v[bi], v[bi][:cs, :], S["HI"][t0:t0 + cs, :])
                self.ld(g[bi], g[bi][:cs, :], S["HG"][t0:t0 + cs, :])

            chunks = []
            for (sq_, cs, offs) in self.seqs(64):
                for ci, t0 in enumerate(offs):
                    chunks.append((sq_, cs, t0, ci == 0, ci == len(offs) - 1))
            with nc.allow_non_contiguous_dma(reason="small chunk loads"):
                loads(0, chunks[0][1], chunks[0][2])
                for idx, (sqi, cs, t0, first, last) in enumerate(chunks):
                    bi = idx % NB
                    St, Sb = StS[sqi % 2], SbS[sqi % 2]
                    if idx + 1 < len(chunks):
                        loads((idx + 1) % NB, chunks[idx + 1][1], chunks[idx + 1][2])
                        nsq = chunks[idx + 1][0]
                        if chunks[idx + 1][3] and nsq >= 1:
                            self.ld(StS[nsq % 2], StS[nsq % 2][:], I["st_hg"][l, nsq - 1].rearrange("h d v -> d h v"))
                    lf_, q_, k_, v_, g_ = lf[bi], qT[bi], kT[bi], v[bi], g[bi]
                    tri = cf[:cs, C_TRI:C_TRI + cs]
                    Lm = cf[:cs, C_L:C_L + cs]
                    mid = cs // 2 - 1
                    if first:
                        if sqi == 0:
                            self.V(lambda: nc.vector.memset(St[:], 0.0), wr=[St])
                            self.V(lambda: nc.vector.memset(Sb[:], 0.0), wr=[Sb])
                        else:
                            self.A(lambda: nc.scalar.copy(out=Sb[:], in_=St[:]), rd=[St], wr=[Sb])
                    for q4 in range(4):
                        p = self.ps()
                        self.mm(p, [(p[:cs, :512], [(Lm, lf_[:cs, q4 * 512:(q4 + 1) * 512])])], [cf, lf_])
                        self.A(lambda: nc.scalar.activation(out=erem[:cs, q4 * 512:(q4 + 1) * 512], in_=p[:cs, :512], func=AF.Exp), rd=[p], wr=[erem])
                    self.A(lambda: nc.scalar.activation(out=ef[:cs, :], in_=lf_[:cs, :], func=AF.Exp), rd=[lf_], wr=[ef])
                    self.V(lambda: nc.vector.tensor_scalar(out=ef[:cs, :], in0=ef[:cs, :], scalar1=-1.0, scalar2=1.0, op0=ALU.mult, op1=ALU.add),
                           rd=[ef], wr=[ef])
                    self.V(lambda: nc.vector.tensor_tensor(out=khat[:cs, :], in0=ef[:cs, :], in1=erem[:cs, :], op=ALU.mult), rd=[ef, erem], wr=[khat])
                    hpb = min(16, 512 // cs)
                    for q2 in range(16 // hpb):
                        p = self.ps()
                        self.mm(p, [(p[:, hh * cs:(hh + 1) * cs], [(lf_[:cs, (q2 * hpb + hh) * 128:(q2 * hpb + hh + 1) * 128], tri)]) for hh in range(hpb)],
                                [lf_, cf])
                        self.V(lambda: nc.vector.tensor_copy(out=bT[:, q2 * hpb:(q2 + 1) * hpb, :cs],
                                                             in_=p[:, :hpb * cs].rearrange("p (a b) -> p a b", a=hpb)), rd=[p], wr=[bT])
                    self.V(lambda: nc.vector.tensor_tensor(out=dl[:, :, :cs], in0=bT[:, :, :cs],
                                                           in1=bT[:, :, mid:mid + 1].to_broadcast([128, 16, cs]), op=ALU.subtract), rd=[bT], wr=[dl])
                    self.A(lambda: nc.scalar.activation(out=e1[:, :, :cs], in_=dl[:, :, :cs], func=AF.Exp), rd=[dl], wr=[e1])
                    self.V(lambda: nc.vector.tensor_tensor(out=qt[:, :, :cs], in0=q_[:, :, :cs], in1=e1[:, :, :cs], op=ALU.mult), rd=[q_, e1], wr=[qt])
                    self.A(lambda: nc.scalar.activation(out=e1[:, :, :cs], in_=dl[:, :, :cs], func=AF.Exp, scale=-1.0), rd=[dl], wr=[e1])
                    self.V(lambda: nc.vector.tensor_tensor(out=ktl[:, :, :cs], in0=k_[:, :, :cs], in1=e1[:, :, :cs], op=ALU.mult), rd=[k_, e1], wr=[ktl])
                    self.A(lambda: nc.scalar.activation(out=e3[:, :, :cs], in_=bT[:, :, :cs], func=AF.Exp), rd=[bT], wr=[e3])
                    self.V(lambda: nc.vector.tensor_tensor(out=qb[:, :, :cs], in0=q_[:, :, :cs], in1=e3[:, :, :cs], op=ALU.mult), rd=[q_, e3], wr=[qb])
                    for q2 in range(16 // hpb):
                        p = self.ps()
                        self.mm(p, [(p[:cs, hh * cs:(hh + 1) * cs], [(ktl[:, q2 * hpb + hh, :cs], qt[:, q2 * hpb + hh, :cs])]) for hh in range(hpb)],
                                [ktl, qt])
                        self.V(lambda: nc.vector.tensor_tensor(out=attm[:cs, q2 * hpb:(q2 + 1) * hpb, :cs],
                                                               in0=p[:cs, :hpb * cs].rearrange("p (a b) -> p a b", a=hpb),
                                                               in1=self.bc(tri, [cs, hpb, cs], 1), op=ALU.mult), rd=[p, cf], wr=[attm])
                    for q4 in range(4):
                        p = self.ps()
                        self.mm(p, [(p[:cs, hh * 128:(hh + 1) * 128],
                                     [(attm[:cs, q4 * 4 + hh, :cs], v_[:cs, (q4 * 4 + hh) * 128:(q4 * 4 + hh + 1) * 128]),
                                      (qb[:, q4 * 4 + hh, :cs], Sb[:, q4 * 4 + hh, :])]) for hh in range(4)], [attm, v_, qb, Sb])
                        self.A(lambda: nc.scalar.copy(out=y[:cs, q4 * 512:(q4 + 1) * 512], in_=p[:cs, :512]), rd=[p], wr=[y])
                    self.A(lambda: nc.scalar.activation(out=sq[:cs, :], in_=y[:cs, :], func=AF.Square), rd=[y], wr=[sq])
                    self.V(lambda: nc.vector.tensor_reduce(out=ss[:cs, :], in_=sq[:cs, :].rearrange("p (g d) -> p g d", g=16), axis=AX.X, op=ALU.add),
                           rd=[sq], wr=[ss])
                    self.rstd(ss, cs, 16, 1.0 / 128, 1e-6)
                    self.V(lambda: nc.vector.tensor_tensor(out=y[:cs, :].rearrange("p (g d) -> p g d", g=16), in0=y[:cs, :].rearrange("p (g d) -> p g d", g=16),
                                                           in1=self.bc(ss[:cs, :], [cs, 16, 128], 2), op=ALU.mult), rd=[y, ss], wr=[y])
                    self.V(lambda: nc.vector.tensor_tensor(out=y[:cs, :], in0=y[:cs, :], in1=nw[:cs, :], op=ALU.mult), rd=[y, nw], wr=[y])
                    yo_ = yo[idx % 2]
                    self.V(lambda: nc.vector.tensor_tensor(out=yo_[:cs, :], in0=y[:cs, :], in1=g_[:cs, :], op=ALU.mult), rd=[y, g_], wr=[yo_])
                    self.st(S["YB"][2, t0:t0 + cs, :], yo_, yo_[:cs, :])
                    for q4 in range(4):
                        p = self.ps()
                        self.mm(p, [(p[:, hh * 128:(hh + 1) * 128],
                                     [(khat[:cs, (q4 * 4 + hh) * 128:(q4 * 4 + hh + 1) * 128], v_[:cs, (q4 * 4 + hh) * 128:(q4 * 4 + hh + 1) * 128])])
                                    for hh in range(4)], [khat, v_])
                        for hh in range(4):
                            h = q4 * 4 + hh
                            self.V(lambda: nc.vector.scalar_tensor_tensor(out=St[:, h, :], in0=St[:, h, :], scalar=e3[:, h, cs - 1:cs],
                                                                          in1=p[:, hh * 128:(hh + 1) * 128], op0=ALU.mult, op1=ALU.add),
                                   rd=[St, e3, p], wr=[St])
                    if not last:
                        self.A(lambda: nc.scalar.copy(out=Sb[:], in_=St[:]), rd=[St], wr=[Sb])
                    else:
                        self.st(O["o_hg"][l, sqi].rearrange("h d v -> d h v"), St, St[:])

    def fill_aT_from_T(self, src):
        with self.phase():
            tb = [self.sb([128, D], BF16) for _ in range(2)]
            for tt in range(NTILE):
                t = tb[tt % 2]
                self.ld(t, t[:], src[tt * 128:(tt + 1) * 128, :])
                self.transpose_to_aT(t, tt)

    def phase_branch(self, l, b):
        nc = self.nc
        I, S = self.I, self.S
        self.fill_aT_from_T(S["YB"][b])
        Wb = I["w_branch"][l, b]
        with self.phase():
            gt = [self.sb([128, NT], BF16) for _ in range(2)]
            mo = [self.sb([128, NT], BF16) for _ in range(2)]
            cnt = [0]

            def mk(cb):
                def fn(slots):
                    s = slots[0]
                    for j in range(4):
                        ch = cb * 4 + j
                        g_ = gt[cnt[0] % 2]
                        m_ = mo[cnt[0] % 2]
                        cnt[0] += 1
                        self.ld(g_, g_[:], S["GT"][b * D + ch * 128:b * D + (ch + 1) * 128, :])
                        for (t0, tn) in TBS:
                            p = self.ps()
                            self.mm(p, [(p[:, :tn], [(s[:, kc, j * 128:(j + 1) * 128], self.aTh[:, kc, t0:t0 + tn]) for kc in range(16)])],
                                    [s] + self.aT_tiles(t0, tn))
                            self.V(lambda: nc.vector.tensor_tensor(out=m_[:, t0:t0 + tn], in0=p[:, :tn], in1=g_[:, t0:t0 + tn], op=ALU.mult),
                                   rd=[p, g_], wr=[m_])
                        self.st(S["MB"][b, ch * 128:(ch + 1) * 128, :], m_, m_[:])
                return fn
            jobs = [([Wb[:, cb * 512:(cb + 1) * 512]], mk(cb)) for cb in range(4)]
            self.run_jobs(jobs, nslots=4)

    def store_Y(self, cb):
        nc = self.nc

        def evac(tt, p, nco):
            o = self.yo[self.yoc % 3]
            self.yoc += 1
            self.ev(lambda: nc.scalar.copy(out=o[:, :nco], in_=p[:, :nco]),
                    lambda: nc.vector.tensor_copy(out=o[:, :nco], in_=p[:, :nco]), rd=[p], wr=[o])
            self.st(self.S["Y"][tt * 128:(tt + 1) * 128, cb * 512:cb * 512 + nco], o, o[:, :nco])
        return evac

    def dense_to_Y(self, Wm):
        with self.phase():
            self.yo = [self.sb([128, 512], F32) for _ in range(3)]
            self.yoc = 0
            jobs = [self.job_T(Wm[:, cb * 512:(cb + 1) * 512], self.store_Y(cb)) for cb in range(4)]
            self.run_jobs(jobs, nslots=4)

    def phase_mixout(self, l):
        nc = self.nc
        S = self.S
        with self.phase():
            m = [[self.sb([128, NT], BF16) for _ in range(3)] for _ in range(2)]
            tf = [self.sb([128, NT], F32) for _ in range(2)]
            for ch in range(16):
                mm_ = m[ch % 2]
                t = tf[ch % 2]
                for b in range(3):
                    self.ld(mm_[b], mm_[b][:], S["MB"][b, ch * 128:(ch + 1) * 128, :])
                self.V(lambda: nc.vector.tensor_tensor(out=t[:], in0=mm_[0][:], in1=mm_[1][:], op=ALU.add), rd=[mm_[0], mm_[1]], wr=[t])
                self.V(lambda: nc.vector.tensor_tensor(out=self.aTh[:, ch, :], in0=t[:], in1=mm_[2][:], op=ALU.add), rd=[t, mm_[2]], wr=self.aT)
        self.dense_to_Y(self.I["w_mix_out"][l])

    def phase_q(self, l):
        nc = self.nc
        S = self.S
        Wq = self.I["x_wq"][l]
        with self.phase():
            ob = [self.sb([128, 512], BF16) for _ in range(3)]
            cnt = [0]

            def mk(cb):
                def evac(j, t0, tn, p):
                    o = ob[cnt[0] % 3]
                    cnt[0] += 1
                    sc = 512 ** -0.5
                    self.ev(lambda: nc.scalar.mul(out=o[:, :tn], in_=p[:, :tn], mul=sc),
                            lambda: nc.vector.tensor_scalar(out=o[:, :tn], in0=p[:, :tn], scalar1=sc, scalar2=None, op0=ALU.mult), rd=[p], wr=[o])
                    r0 = cb * 512 + j * 128
                    self.st(S["QT"][r0:r0 + 128, t0:t0 + tn], o, o[:, :tn])
                return evac
            jobs = [self.job_F(Wq[:, cb * 512:(cb + 1) * 512], mk(cb)) for cb in range(4)]
            self.run_jobs(jobs, nslots=4)

    def softmax_pt(self, np_, psA, psB, pe, pn, st):
        nc = self.nc
        for i, p in enumerate((psA, psB)):
            self.V(lambda: nc.vector.tensor_reduce(out=st[:np_, 2 * i:2 * i + 2], in_=p[:np_, :].rearrange("p (a b) -> p a b", a=2),
                                                   axis=AX.X, op=ALU.max), rd=[p], wr=[st])
        self.V(lambda: nc.vector.tensor_scalar(out=st[:np_, 4:8], in0=st[:np_, 0:4], scalar1=-1.0, scalar2=None, op0=ALU.mult), rd=[st], wr=[st])
        for h in range(4):
            p = (psA, psB)[h // 2]
            self.A(lambda: nc.scalar.activation(out=pe[:np_, h, :], in_=p[:np_, (h % 2) * 256:(h % 2 + 1) * 256], func=AF.Exp,
                                                bias=st[:np_, 4 + h:5 + h], scale=1.0), rd=[p, st], wr=[pe])
        self.V(lambda: nc.vector.tensor_reduce(out=st[:np_, 8:12], in_=pe[:np_, :, :], axis=AX.X, op=ALU.add), rd=[pe], wr=[st])
        self.V(lambda: nc.vector.reciprocal(out=st[:np_, 12:16], in_=st[:np_, 8:12]), rd=[st], wr=[st])
        self.V(lambda: nc.vector.tensor_tensor(out=pn[:np_, :, :], in0=pe[:np_, :, :], in1=self.bc(st[:np_, 12:16], [np_, 4, 256], 2), op=ALU.mult),
               rd=[pe, st], wr=[pn])

    def phase_attn(self, l):
        nc = self.nc
        I, S = self.I, self.S
        with self.phase():
            ktm = self.sb([128, 16, 256], BF16)
            vm = self.sb([128, 2, D], BF16)
            self.ld(ktm, ktm[:], S["KTM"][l].rearrange("(c p) m -> p c m", p=128))
            self.ld(vm, vm[:], S["VM"][l].rearrange("(mc p) v -> p mc v", p=128))
            qb = [self.sb([128, 16, 512], BF16) for _ in range(2)]
            pe = self.sb([128, 4, 256], F32)
            pn = self.sb([128, 4, 256], BF16)
            st = self.sb([128, 16], F32)
            pT = self.sb([128, 8, 128], BF16)

            def ldq(bi):
                t0, tn = TBS[bi]
                t = qb[bi % 2]
                self.ld(t, t[:, :, :tn], S["QT"][:, t0:t0 + tn].rearrange("(c p) t -> p c t", p=128))
                return t
            cur = ldq(0)
            for bi in range(4):
                nxt = ldq(bi + 1)
                for ti in range(4):
                    tt = bi * 4 + ti
                    tsl = slice(ti * 128, (ti + 1) * 128)
                    pss = [self.ps(), self.ps()]
                    for i in range(2):
                        self.mm(pss[i], [(pss[i][:, hh * 256:(hh + 1) * 256],
                                          [(cur[:, 4 * (2 * i + hh) + c, tsl], ktm[:, 4 * (2 * i + hh) + c, :]) for c in range(4)]) for hh in range(2)],
                                [cur, ktm])
                    self.softmax_pt(128, pss[0], pss[1], pe, pn, st)
                    for i in range(2):
                        p = self.ps()
                        self.mm(p, [(p[:, k * 128:(k + 1) * 128], [(pn[:, (i * 4 + k) // 2, ((i * 4 + k) % 2) * 128:((i * 4 + k) % 2 + 1) * 128], self.identb[:])])
                                    for k in range(4)], [pn, self.identb])
                        self.A(lambda: nc.scalar.copy(out=pT[:, i * 4:(i + 1) * 4, :], in_=p[:].rearrange("p (a b) -> p a b", a=4)), rd=[p], wr=[pT])
                    for h in range(4):
                        p = self.ps()
                        self.mm(p, [(p[:, c * 128:(c + 1) * 128],
                                     [(vm[:, mc, (4 * h + c) * 128:(4 * h + c + 1) * 128], pT[:, h * 2 + mc, :]) for mc in range(2)]) for c in range(4)],
                                [vm, pT])
                        o = self.aTh[:, 4 * h:4 * h + 4, tt * 128:(tt + 1) * 128]
                        i_ = p[:].rearrange("p (a b) -> p a b", a=4)
                        self.ev(lambda: nc.scalar.copy(out=o, in_=i_), lambda: nc.vector.tensor_copy(out=o, in_=i_), rd=[p], wr=[self.aT[tt]])
                cur = nxt
            qs = cur
            ks = [self.sb([128, 2, D], BF16) for _ in range(2)]
            vs = [self.sb([128, 2, D], BF16) for _ in range(2)]
            kts = self.sb([128, 16, 256], BF16)
            pTs = self.sb([128, 8, 8], BF16)

            def ldkv(s):
                self.ldc(ks[s % 2], ks[s % 2][:], I["ck"][l, s].rearrange("(mc p) d -> p mc d", p=128))
                self.ldc(vs[s % 2], vs[s % 2][:], I["cv"][l, s].rearrange("(mc p) d -> p mc d", p=128))
            ldkv(0)
            for s in range(NSEQ_S):
                if s + 1 < NSEQ_S:
                    ldkv(s + 1)
                k_, v_ = ks[s % 2], vs[s % 2]
                for c2 in range(8):
                    p = self.ps()
                    self.mm(p, [(p[:, (cc * 2 + mc) * 128:(cc * 2 + mc + 1) * 128],
                                 [(k_[:, mc, (c2 * 2 + cc) * 128:(c2 * 2 + cc + 1) * 128], self.identb[:])]) for cc in range(2) for mc in range(2)],
                            [k_, self.identb])
                    o = kts[:, c2 * 2:c2 * 2 + 2, :]
                    i_ = p[:].rearrange("p (a b) -> p a b", a=2)
                    self.ev(lambda: nc.scalar.copy(out=o, in_=i_), lambda: nc.vector.tensor_copy(out=o, in_=i_), rd=[p], wr=[kts])
                pss = [self.ps(), self.ps()]
                for i in range(2):
                    self.mm(pss[i], [(pss[i][:8, hh * 256:(hh + 1) * 256],
                                      [(qs[:, 4 * (2 * i + hh) + c, s * 8:(s + 1) * 8], kts[:, 4 * (2 * i + hh) + c, :]) for c in range(4)]) for hh in range(2)],
                            [qs, kts])
                self.softmax_pt(8, pss[0], pss[1], pe, pn, st)
                p = self.ps()
                self.mm(p, [(p[:, k * 8:(k + 1) * 8], [(pn[:8, k // 2, (k % 2) * 128:(k % 2 + 1) * 128], self.identb[:8, :8])]) for k in range(8)],
                        [pn, self.identb])
                self.A(lambda: nc.scalar.copy(out=pTs[:], in_=p[:, :64].rearrange("p (a b) -> p a b", a=8)), rd=[p], wr=[pTs])
                p2 = self.ps()
                self.mm(p2, [(p2[:, c * 8:(c + 1) * 8], [(v_[:, mc, c * 128:(c + 1) * 128], pTs[:, (c // 4) * 2 + mc, :]) for mc in range(2)]) for c in range(16)],
                        [v_, pTs])
                o = self.aTh[:, :, NPR + s * 8:NPR + (s + 1) * 8]
                self.V(lambda: nc.vector.tensor_copy(out=o, in_=p2[:, :128].rearrange("p (a b) -> p a b", a=16)), rd=[p2], wr=[self.aT[16]])

    def phase_wo(self, l):
        self.dense_to_Y(self.I["x_wo"][l])

    def phase_memkv(self):
        nc = self.nc
        I, S, O = self.I, self.S, self.O
        with self.phase():
            mT = self.sb([128, 16, 256], BF16)
            mf = self.sb([128, D], F32)
            mb = self.sb([128, D], BF16)
            for mc in range(2):
                self.ld(mf, mf[:], I["memp"][mc * 128:(mc + 1) * 128, :])
                self.A(lambda: nc.scalar.copy(out=mb[:], in_=mf[:]), rd=[mf], wr=[mb])
                for g in range(4):
                    p = self.ps()
                    self.mm(p, [(p[:, j * 128:(j + 1) * 128], [(mb[:, (g * 4 + j) * 128:(g * 4 + j + 1) * 128], self.identb[:])]) for j in range(4)],
                            [mb, self.identb])
                    self.V(lambda: nc.vector.tensor_copy(out=mT[:, g * 4:(g + 1) * 4, mc * 128:(mc + 1) * 128],
                                                         in_=p[:].rearrange("p (a b) -> p a b", a=4)), rd=[p], wr=[mT])
            of = [self.sb([128, 512], F32) for _ in range(3)]
            ob = [self.sb([128, 512], BF16) for _ in range(3)]
            cnt = [0]

            def mk(l, cb, is_k):
                def fn(slots):
                    s = slots[0]
                    for mc in range(2):
                        p = self.ps()
                        self.mm(p, [(p[:, :], [(mT[:, kc, mc * 128:(mc + 1) * 128], s[:, kc, :]) for kc in range(16)])], [mT, s])
                        o = of[cnt[0] % 3]
                        cnt[0] += 1
                        self.V(lambda: nc.vector.tensor_copy(out=o[:], in_=p[:]), rd=[p], wr=[o])
                        dst = O["o_mk"] if is_k else O["o_mv"]
                        self.st(dst[l, mc * 128:(mc + 1) * 128, cb * 512:(cb + 1) * 512], o, o[:])
                        if not is_k:
                            o2 = ob[cnt[0] % 3]
                            self.A(lambda: nc.scalar.copy(out=o2[:], in_=o[:]), rd=[o], wr=[o2])
                            self.st(S["VM"][l, mc * 128:(mc + 1) * 128, cb * 512:(cb + 1) * 512], o2, o2[:])
                    if is_k:
                        for j in range(4):
                            p = self.ps()
                            self.mm(p, [(p[:, :256], [(s[:, kc, j * 128:(j + 1) * 128], mT[:, kc, :]) for kc in range(16)])], [mT, s])
                            o2 = ob[cnt[0] % 3]
                            cnt[0] += 1
                            self.A(lambda: nc.scalar.copy(out=o2[:, :256], in_=p[:, :256]), rd=[p], wr=[o2])
                            r0 = cb * 512 + j * 128
                            self.st(S["KTM"][l, r0:r0 + 128, :], o2, o2[:, :256])
                return fn
            jobs = []
            for l in range(DEPTH):
                for cb in range(4):
                    jobs.append(([I["x_wk"][l][:, cb * 512:(cb + 1) * 512]], mk(l, cb, True)))
                for cb in range(4):
                    jobs.append(([I["x_wv"][l][:, cb * 512:(cb + 1) * 512]], mk(l, cb, False)))
            self.run_jobs(jobs, nslots=4)


_CACHE = {}


def get_program(debug=False, stop=None):
    key = (debug, stop)
    if key not in _CACHE:
        k = K(debug=debug, stop=stop)
        k.build()
        _CACHE[key] = k
    return _CACHE[key]


def make_in_maps(inputs):
    f = lambda a: np.ascontiguousarray(np.asarray(a, dtype=np.float32))
    g = {k: f(v) for k, v in inputs.items()}
    consts = make_consts()
    maps = []
    for c in range(8):
        ps = c % 4
        sl = slice(c * NSEQ_S, (c + 1) * NSEQ_S)
        m = {
            "xp": g["x_prompt"][ps], "xsm": g["x_sample"][sl].reshape(NSM, D), "memp": g["mem_prompt"][ps],
            "st_conv": f(g["state_ssd_conv"][:, sl]), "st_ssd": f(g["state_ssd"][:, sl]).reshape(DEPTH, NSEQ_S, 2048, 128),
            "st_mc": f(g["state_mlstm_c"][:, sl]), "st_mn": f(g["state_mlstm_n"][:, sl]).reshape(DEPTH, NSEQ_S, 8, 128),
            "st_mm": f(g["state_mlstm_m"][:, sl]), "st_hg": f(g["state_hgrn"][:, sl]),
            "ck": f(g["cache_mem_k"][:, sl]).reshape(DEPTH, NSEQ_S, 256, D), "cv": f(g["cache_mem_v"][:, sl]).reshape(DEPTH, NSEQ_S, 256, D),
            "ml_gate_bias": g["ml_gate_bias"].reshape(DEPTH, 8), "consts": consts,
        }
        for nm in ("ln_g", "ln_b", "ffn_w1", "ffn_w3", "ffn_w2", "w_in", "ssd_conv_w", "ssd_conv_b", "ssd_dt_bias", "ssd_a_log",
                   "ssd_d", "ssd_norm", "ml_norm", "hg_lb_logits", "hg_norm", "w_branch", "w_mix_out", "x_wq", "x_wk", "x_wv", "x_wo"):
            m[nm] = g[nm]
        maps.append(m)
    return maps


def assemble(res):
    R = res
    y_prompt = np.stack([R[c]["y"][:NPR] for c in range(4)])
    y_sample = np.concatenate([R[c]["y"][NPR:].reshape(NSEQ_S, LS, D) for c in range(8)])

    def pstate(nm, shp):
        return np.stack([R[c][nm][:, 0] for c in range(4)], axis=1).reshape(shp)

    def sstate(nm, shp):
        return np.concatenate([R[c][nm][:, 1:] for c in range(8)], axis=1).reshape(shp)

    p_conv = pstate("o_conv", (DEPTH, 4, 3, 3072)); s_conv = sstate("o_conv", (DEPTH, 128, 3, 3072))
    p_ssd = pstate("o_ssd", (DEPTH, 4, 32, 64, 128)); s_ssd = sstate("o_ssd", (DEPTH, 128, 32, 64, 128))
    p_mc = pstate("o_mc", (DEPTH, 4, 4, 256, 512)); s_mc = sstate("o_mc", (DEPTH, 128, 4, 256, 512))
    p_mn = pstate("o_mn", (DEPTH, 4, 4, 256)); s_mn = sstate("o_mn", (DEPTH, 128, 4, 256))
    p_mm = pstate("o_mm", (DEPTH, 4, 4)); s_mm = sstate("o_mm", (DEPTH, 128, 4))
    p_hg = pstate("o_hg", (DEPTH, 4, 16, 128, 128)); s_hg = sstate("o_hg", (DEPTH, 128, 16, 128, 128))
    p_mk = np.stack([R[c]["o_mk"] for c in range(4)], axis=1).reshape(DEPTH, 4, 256, 4, 512)
    p_mv = np.stack([R[c]["o_mv"] for c in range(4)], axis=1).reshape(DEPTH, 4, 256, 4, 512)
    outs = (y_prompt, y_sample, p_conv, p_ssd, p_mc, p_mn, p_mm, p_hg, p_mk, p_mv, s_conv, s_ssd, s_mc, s_mn, s_mm, s_hg)
    return tuple(np.ascontiguousarray(o, dtype=np.float32) for o in outs)


def kernel(**inputs):
    k = get_program()
    maps = make_in_maps(inputs)
    res = run_bass_kernel_spmd(k.nc, maps, core_ids=list(range(8)))
    return assemble(res.results)
```

```python
import numpy as np
from contextlib import ExitStack
import concourse.bass as bass
import concourse.mybir as mybir
from concourse.bass_utils import run_bass_kernel_spmd

F32 = mybir.dt.float32
BF16 = mybir.dt.bfloat16
AF = mybir.ActivationFunctionType
ALU = mybir.AluOpType
AX = mybir.AxisListType

D = 2048
DFF = 5504
NIN = 25640
DEPTH = 2
NPR = 2048
NSEQ_S = 16
LS = 8
NSM = NSEQ_S * LS
NT = NPR + NSM
NTILE = NT // 128
TBS = [(0, 512), (512, 512), (1024, 512), (1536, 512), (2048, 128)]
ALPHA = (2.0 * DEPTH) ** 0.25
XBC_ROWS = 3 + NPR + NSEQ_S * (3 + LS)

O_Z, O_XBC, O_DT, O_MQ, O_MK, O_MV, O_MO, O_MI, O_MF, O_HQ, O_HF, O_HI, O_HG, O_GT = (
    0, 2048, 5120, 5152, 6176, 7200, 9248, 11296, 11300, 11304, 13352, 15400, 17448, 19496)

C_ID, C_TRI, C_L, C_MNEG, C_SEL128, C_SEL8, C_ONES, C_END = 0, 128, 256, 384, 512, 640, 768, 896
PCS = 128


def make_consts():
    c = np.zeros((128, C_END), np.float32)
    c[:, C_ID:C_ID + 128] = np.eye(128, dtype=np.float32)
    k = np.arange(128)
    c[:, C_TRI:C_TRI + 128] = (k[:, None] <= k[None, :]).astype(np.float32)
    c[:, C_L:C_L + 128] = (k[:, None] > k[None, :]).astype(np.float32)
    c[:, C_MNEG:C_MNEG + 128] = np.where(k[None, :] <= k[:, None], 0.0, -1e30)
    c[127, C_SEL128:C_SEL128 + 128] = 1.0
    c[7, C_SEL8:C_SEL8 + 128] = 1.0
    c[:, C_ONES:C_ONES + 128] = 1.0
    return c


class Sem:
    __slots__ = ("h", "id", "target")

    def __init__(self, h, i):
        self.h = h
        self.id = i
        self.target = 0


class Tl:
    __slots__ = ("h", "w", "r")

    def __init__(self, h):
        self.h = h
        self.w = None
        self.r = {}

    def __getitem__(self, k):
        return self.h[k]


class Eng:
    def __init__(self, e, sem, name):
        self.e = e
        self.sem = sem
        self.n = 0
        self.known = {}
        self.name = name
        self.dsems = []
        self.di = 0


class K:
    def __init__(self, debug=False, stop=None):
        self.debug = debug
        self.stop = stop
        self.nc = nc = bass.Bass("TRN2", target_bir_lowering=False)
        self.uid = 0
        self.semc = 0
        self.pe = Eng(nc.tensor, self.newsem(), "pe")
        self.act = Eng(nc.scalar, self.newsem(), "act")
        self.dve = Eng(nc.vector, self.newsem(), "dve")
        self.sp = Eng(nc.sync, None, "sp")
        self.gq = Eng(nc.gpsimd, None, "gq")
        self.sp.dsems = [self.newsem() for _ in range(40)]
        self.gq.dsems = [self.newsem() for _ in range(24)]
        self.engs = [self.pe, self.act, self.dve, self.sp, self.gq]
        self.stk = None
        self.psb = [Tl(nc.alloc_psum_tensor(f"psb{i}", [128, 512], F32)) for i in range(8)]
        self.psi = 0
        self.evi = 0

    def newsem(self):
        self.semc += 1
        return Sem(self.nc.alloc_semaphore(f"s{self.semc}"), self.semc)

    def name(self, p="t"):
        self.uid += 1
        return f"{p}{self.uid}"

    def sb(self, shape, dt):
        return Tl(self.stk.enter_context(self.nc.sbuf_tensor(self.name("sb"), list(shape), dt)))

    def dram(self, nm, shape, dt):
        kind = "ExternalOutput" if self.debug else "Internal"
        return self.nc.dram_tensor(nm, list(shape), dt, kind=kind).ap()

    def ps(self):
        t = self.psb[self.psi % 8]
        self.psi += 1
        return t

    def _deps(self, eng, rd, wr):
        need = {}

        def add(ev):
            if ev is None:
                return
            s, v = ev
            o = need.get(s.id)
            if o is None or o[1] < v:
                need[s.id] = (s, v)

        for t in rd:
            add(t.w)
        for t in wr:
            add(t.w)
            for ev in t.r.values():
                add(ev)
        for sid, (s, v) in need.items():
            if eng is self.pe and s is self.pe.sem:
                continue
            if eng.known.get(sid, 0) < v:
                eng.e.wait_ge(s.h, v)
                eng.known[sid] = v

    def _mark(self, ev, rd, wr):
        s, v = ev
        for t in rd:
            t.r[s.id] = ev
        for t in wr:
            t.w = ev
            t.r = {}

    def op(self, eng, fn, rd=(), wr=()):
        self._deps(eng, rd, wr)
        ins = fn()
        eng.n += 1
        ins.then_inc(eng.sem.h, 1)
        self._mark((eng.sem, eng.n), rd, wr)

    def A(self, fn, rd=(), wr=()):
        self.op(self.act, fn, rd, wr)

    def V(self, fn, rd=(), wr=()):
        self.op(self.dve, fn, rd, wr)

    def ev(self, fn_act, fn_dve, rd=(), wr=()):
        self.evi += 1
        if self.evi % 2:
            self.A(fn_act, rd, wr)
        else:
            self.V(fn_dve, rd, wr)

    def mm(self, pst, groups, rd):
        pe = self.pe
        self._deps(pe, rd, [pst])
        last = None
        for out, pairs in groups:
            n = len(pairs)
            for i, (l, r) in enumerate(pairs):
                last = self.nc.tensor.matmul(out, lhsT=l, rhs=r, start=(i == 0), stop=(i == n - 1))
        pe.n += 1
        last.then_inc(pe.sem.h, 1)
        self._mark((pe.sem, pe.n), rd, [pst])

    def dma(self, q, out, in_, rd=(), wr=()):
        self._deps(q, rd, wr)
        s = q.dsems[q.di % len(q.dsems)]
        q.di += 1
        if q.known.get(s.id, 0) < s.target:
            q.e.wait_ge(s.h, s.target)
            q.known[s.id] = s.target
        q.e.dma_start(out=out, in_=in_).then_inc(s.h, 16)
        s.target += 16
        self._mark((s, s.target), rd, wr)

    def ld(self, t, out, in_):
        self.dma(self.sp, out, in_, wr=[t])

    def ldc(self, t, out, in_):
        self.dma(self.gq, out, in_, wr=[t])

    def st(self, out, t, in_):
        self.dma(self.sp, out, in_, rd=[t])

    def barrier(self):
        allv = [(e.sem, e.n) for e in (self.pe, self.act, self.dve)]
        for q in (self.sp, self.gq):
            allv += [(s, s.target) for s in q.dsems]
        for eng in self.engs:
            for s, v in allv:
                if v > 0 and eng.known.get(s.id, 0) < v:
                    eng.e.wait_ge(s.h, v)
                    eng.known[s.id] = v

    class _Phase:
        def __init__(self, k):
            self.k = k

        def __enter__(self):
            self.prev = self.k.stk
            self.k.stk = ExitStack()
            self.k.stk.__enter__()
            return self

        def __exit__(self, *a):
            self.k.barrier()
            self.k.stk.__exit__(None, None, None)
            self.k.stk = self.prev
            return False

    def phase(self):
        return K._Phase(self)

    def load_w(self, wap):
        sl = self.wslots[self.wsi % len(self.wslots)]
        self.wsi += 1
        kc = wap.shape[0] // 128
        nco = wap.shape[1]
        self.ldc(sl, sl[:, :kc, :nco], wap.rearrange("(kc p) n -> p kc n", p=128))
        return sl

    def run_jobs(self, jobs, nslots=4):
        self.wslots = [self.sb([128, 16, 512], BF16) for _ in range(nslots)]
        self.wsi = 0
        n = len(jobs)
        loaded = {}
        if n:
            loaded[0] = [self.load_w(w) for w in jobs[0][0]]
        for i in range(n):
            if i + 1 < n:
                loaded[i + 1] = [self.load_w(w) for w in jobs[i + 1][0]]
            jobs[i][1](loaded.pop(i))

    def bc(self, ap, shape, axis):
        return ap.unsqueeze(axis).to_broadcast(list(shape))

    def load_bc(self, dram_row, n, dt=F32):
        t = self.sb([128, n], dt)
        self.ld(t, t[:], dram_row.partition_broadcast(128))
        return t

    def transpose_to_aT(self, src, tt):
        nc = self.nc
        for g in range(4):
            p = self.ps()
            self.mm(p, [(p[:, j * 128:(j + 1) * 128],
                         [(src[:, (g * 4 + j) * 128:(g * 4 + j + 1) * 128], self.identb[:])]) for j in range(4)],
                    [src, self.identb])
            o = self.aTh[:, g * 4:(g + 1) * 4, tt * 128:(tt + 1) * 128]
            i = p[:].rearrange("p (a b) -> p a b", a=4)
            self.ev(lambda: nc.scalar.copy(out=o, in_=i), lambda: nc.vector.tensor_copy(out=o, in_=i),
                    rd=[p], wr=[self.aT[tt]])

    def alloc_aT(self):
        t = self.stk.enter_context(self.nc.sbuf_tensor(self.name("aT"), [128, 16, NT], BF16))
        self.aTh = t
        self.aT = [Tl(t) for _ in range(NTILE)]

    def aT_tiles(self, t0, n):
        return [self.aT[i] for i in range(t0 // 128, (t0 + n + 127) // 128)]

    def rstd(self, ss, np_, ncol, inv_n, eps):
        nc = self.nc
        a = ss[:np_, :ncol]
        self.V(lambda: nc.vector.tensor_scalar(out=a, in0=a, scalar1=inv_n, scalar2=eps, op0=ALU.mult, op1=ALU.add),
               rd=[ss], wr=[ss])
        self.A(lambda: nc.scalar.activation(out=a, in_=a, func=AF.Ln), rd=[ss], wr=[ss])
        self.A(lambda: nc.scalar.activation(out=a, in_=a, func=AF.Exp, scale=-0.5), rd=[ss], wr=[ss])

    def build(self):
        nc = self.nc
        I = {}

        def inp(nm, shape):
            I[nm] = nc.dram_tensor(nm, list(shape), F32, kind="ExternalInput").ap()

        inp("xp", [NPR, D]); inp("xsm", [NSM, D]); inp("memp", [256, D])
        inp("st_conv", [DEPTH, NSEQ_S, 3, 3072]); inp("st_ssd", [DEPTH, NSEQ_S, 2048, 128])
        inp("st_mc", [DEPTH, NSEQ_S, 4, 256, 512]); inp("st_mn", [DEPTH, NSEQ_S, 8, 128])
        inp("st_mm", [DEPTH, NSEQ_S, 4]); inp("st_hg", [DEPTH, NSEQ_S, 16, 128, 128])
        inp("ck", [DEPTH, NSEQ_S, 256, D]); inp("cv", [DEPTH, NSEQ_S, 256, D])
        inp("ln_g", [DEPTH, 4, D]); inp("ln_b", [DEPTH, 4, D])
        inp("ffn_w1", [DEPTH, 2, D, DFF]); inp("ffn_w3", [DEPTH, 2, D, DFF]); inp("ffn_w2", [DEPTH, 2, DFF, D])
        inp("w_in", [DEPTH, D, NIN]); inp("ssd_conv_w", [DEPTH, 4, 3072]); inp("ssd_conv_b", [DEPTH, 3072])
        inp("ssd_dt_bias", [DEPTH, 32]); inp("ssd_a_log", [DEPTH, 32]); inp("ssd_d", [DEPTH, 32])
        inp("ssd_norm", [DEPTH, D]); inp("ml_gate_bias", [DEPTH, 8]); inp("ml_norm", [DEPTH, D])
        inp("hg_lb_logits", [DEPTH, D]); inp("hg_norm", [DEPTH, D])
        inp("w_branch", [DEPTH, 3, D, D]); inp("w_mix_out", [DEPTH, D, D])
        inp("x_wq", [DEPTH, D, D]); inp("x_wk", [DEPTH, D, D]); inp("x_wv", [DEPTH, D, D]); inp("x_wo", [DEPTH, D, D])
        inp("consts", [128, C_END])
        self.I = I
        O = {}

        def outp(nm, shape):
            O[nm] = nc.dram_tensor(nm, list(shape), F32, kind="ExternalOutput").ap()

        outp("y", [NT, D]); outp("o_conv", [DEPTH, 17, 3, 3072]); outp("o_ssd", [DEPTH, 17, 2048, 128])
        outp("o_mc", [DEPTH, 17, 4, 256, 512]); outp("o_mn", [DEPTH, 17, 8, 128]); outp("o_mm", [DEPTH, 17, 4])
        outp("o_hg", [DEPTH, 17, 16, 128, 128]); outp("o_mk", [DEPTH, 256, D]); outp("o_mv", [DEPTH, 256, D])
        self.O = O
        S = {}
        S["X"] = self.dram("X", [NT, D], F32)
        S["Y"] = self.dram("Y", [NT, D], F32)
        S["G"] = self.dram("G", [DFF, NT], BF16)
        S["ZS"] = self.dram("ZS", [NT, D], BF16)
        S["XBC"] = self.dram("XBC", [XBC_ROWS, 3072], F32)
        S["XA"] = self.dram("XA", [NT, 3072], BF16)
        S["BCT"] = self.dram("BCT", [1024, NT], BF16)
        S["DTA"] = self.dram("DTA", [NT, 64], F32)
        S["MQ"] = self.dram("MQ", [1024, NT], BF16)
        S["MKT"] = self.dram("MKT", [1024, NT], BF16)
        S["MK"] = self.dram("MK", [NT, 1024], BF16)
        S["MV"] = self.dram("MV", [NT, D], BF16)
        S["MO"] = self.dram("MO", [NT, D], BF16)
        S["MIF"] = self.dram("MIF", [NT, 8], F32)
        S["HQ"] = self.dram("HQ", [D, NT], BF16)
        S["KT"] = self.dram("KT", [D, NT], BF16)
        S["LF"] = self.dram("LF", [NT, D], F32)
        S["HI"] = self.dram("HI", [NT, D], BF16)
        S["HG"] = self.dram("HG", [NT, D], BF16)
        S["GT"] = self.dram("GT", [3 * D, NT], BF16)
        S["YB"] = self.dram("YB", [3, NT, D], BF16)
        S["MB"] = self.dram("MB", [3, D, NT], BF16)
        S["QT"] = self.dram("QT", [D, NT], BF16)
        S["KTM"] = self.dram("KTM", [DEPTH, D, 256], BF16)
        S["VM"] = self.dram("VM", [DEPTH, 256, D], BF16)
        self.S = S

        with ExitStack() as gst:
            self.stk = gst
            self.cf = self.sb([128, C_END], F32)
            self.ld(self.cf, self.cf[:], I["consts"])
            self.identb = self.sb([128, 128], BF16)
            self.onesb = self.sb([128, 128], BF16)
            self.V(lambda: nc.vector.tensor_copy(out=self.identb[:], in_=self.cf[:, C_ID:C_ID + 128]),
                   rd=[self.cf], wr=[self.identb])
            self.V(lambda: nc.vector.tensor_copy(out=self.onesb[:], in_=self.cf[:, C_ONES:C_ONES + 128]),
                   rd=[self.cf], wr=[self.onesb])
            self.barrier()
            self.program()
            self.barrier()
        return nc

    def stopped(self, tag):
        return self.stop is not None and tag >= self.stop

    def program(self):
        I, S, O = self.I, self.S, self.O
        if self.stopped(0):
            return
        self.phase_memkv()
        if self.stopped(1):
            return
        for l in range(DEPTH):
            with self.phase():
                self.alloc_aT()
                if l == 0:
                    self.phase_init_xT()
                else:
                    self.phase_ln(l - 1, 3, S["X"], S["X"], final=False)
                self.phase_ffn_up(l, 0)
            if self.stopped(2):
                return
            self.phase_ffn_down(l, 0)
            with self.phase():
                self.alloc_aT()
                self.phase_ln(l, 0, None if l == 0 else S["X"], S["X"])
                if self.stopped(3):
                    return
                self.phase_win(l)
            if self.stopped(4):
                return
            self.phase_conv(l)
            if self.stopped(5):
                return
            self.phase_ssd(l)
            if self.stopped(6):
                return
            self.phase_mlstm(l)
            if self.stopped(7):
                return
            self.phase_hgrn(l)
            if self.stopped(8):
                return
            for b in range(3):
                with self.phase():
                    self.alloc_aT()
                    self.phase_branch(l, b)
            with self.phase():
                self.alloc_aT()
                self.phase_mixout(l)
            with self.phase():
                self.alloc_aT()
                self.phase_ln(l, 1, S["X"], S["X"])
                self.phase_q(l)
            if self.stopped(9):
                return
            with self.phase():
                self.alloc_aT()
                self.phase_attn(l)
                self.phase_wo(l)
            with self.phase():
                self.alloc_aT()
                self.phase_ln(l, 2, S["X"], S["X"])
                self.phase_ffn_up(l, 1)
            self.phase_ffn_down(l, 1)
            if self.stopped(10 + l):
                return
        with self.phase():
            self.phase_ln(DEPTH - 1, 3, S["X"], O["y"], final=True)

    def xsrc(self, tt):
        if tt < 16:
            return self.I["xp"][tt * 128:(tt + 1) * 128, :]
        return self.I["xsm"][:, :]

    def phase_init_xT(self):
        nc = self.nc
        with self.phase():
            xt = [self.sb([128, D], F32) for _ in range(2)]
            xb = [self.sb([128, D], BF16) for _ in range(2)]
            for tt in range(NTILE):
                a, b = xt[tt % 2], xb[tt % 2]
                self.ld(a, a[:], self.xsrc(tt))
                self.A(lambda: nc.scalar.copy(out=b[:], in_=a[:]), rd=[a], wr=[b])
                self.transpose_to_aT(b, tt)

    def phase_ln(self, l, idx, xres, xdst, final=False):
        nc = self.nc
        with self.phase():
            gb = self.load_bc(self.I["ln_g"][l, idx:idx + 1, :], D)
            bb = self.load_bc(self.I["ln_b"][l, idx:idx + 1, :], D)
            xt = [self.sb([128, D], F32) for _ in range(2)]
            yt = [self.sb([128, D], F32) for _ in range(2)]
            vt = [self.sb([128, D], F32) for _ in range(4)]
            sqs = [self.sb([128, D], F32) for _ in range(2)]
            xb = [self.sb([128, D], BF16) for _ in range(2)]
            st = [self.sb([128, 4], F32) for _ in range(4)]

            def S1(tt):
                x, y, v, s = xt[tt % 2], yt[tt % 2], vt[tt % 4], st[tt % 4]
                src = self.xsrc(tt) if xres is None else xres[tt * 128:(tt + 1) * 128, :]
                self.ld(x, x[:], src)
                self.ld(y, y[:], self.S["Y"][tt * 128:(tt + 1) * 128, :])
                self.V(lambda: nc.vector.scalar_tensor_tensor(out=v[:], in0=x[:], scalar=ALPHA, in1=y[:],
                                                              op0=ALU.mult, op1=ALU.add), rd=[x, y], wr=[v])
                self.V(lambda: nc.vector.tensor_reduce(out=s[:, 0:1], in_=v[:], axis=AX.X, op=ALU.add), rd=[v], wr=[s])
                self.V(lambda: nc.vector.tensor_scalar(out=s[:, 1:2], in0=s[:, 0:1], scalar1=-1.0 / D, scalar2=None,
                                                       op0=ALU.mult), rd=[s], wr=[s])

            def S2(tt):
                v, s, sq = vt[tt % 4], st[tt % 4], sqs[tt % 2]
                self.A(lambda: nc.scalar.activation(out=v[:], in_=v[:], func=AF.Identity, bias=s[:, 1:2], scale=1.0),
                       rd=[v, s], wr=[v])
                self.A(lambda: nc.scalar.activation(out=sq[:], in_=v[:], func=AF.Square), rd=[v], wr=[sq])

            def S3(tt):
                s, sq = st[tt % 4], sqs[tt % 2]
                self.V(lambda: nc.vector.tensor_reduce(out=s[:, 2:3], in_=sq[:], axis=AX.X, op=ALU.add), rd=[sq], wr=[s])
                self.V(lambda: nc.vector.tensor_scalar(out=s[:, 2:3], in0=s[:, 2:3], scalar1=1.0 / D, scalar2=1e-5,
                                                       op0=ALU.mult, op1=ALU.add), rd=[s], wr=[s])
                self.A(lambda: nc.scalar.activation(out=s[:, 2:3], in_=s[:, 2:3], func=AF.Ln), rd=[s], wr=[s])
                self.A(lambda: nc.scalar.activation(out=s[:, 3:4], in_=s[:, 2:3], func=AF.Exp, scale=-0.5), rd=[s], wr=[s])

            def S4(tt):
                v, s, b = vt[tt % 4], st[tt % 4], xb[tt % 2]
                self.V(lambda: nc.vector.scalar_tensor_tensor(out=v[:], in0=v[:], scalar=s[:, 3:4], in1=gb[:],
                                                              op0=ALU.mult, op1=ALU.mult), rd=[v, s, gb], wr=[v])
                self.V(lambda: nc.vector.tensor_tensor(out=v[:], in0=v[:], in1=bb[:], op=ALU.add), rd=[v, bb], wr=[v])
                self.st(xdst[tt * 128:(tt + 1) * 128, :], v, v[:])
                if not final:
                    self.A(lambda: nc.scalar.copy(out=b[:], in_=v[:]), rd=[v], wr=[b])
                    self.transpose_to_aT(b, tt)

            for step in range(NTILE + 3):
                if step < NTILE:
                    S1(step)
                if 0 <= step - 1 < NTILE:
                    S2(step - 1)
                if 0 <= step - 2 < NTILE:
                    S3(step - 2)
                if 0 <= step - 3 < NTILE:
                    S4(step - 3)

    def phase_ffn_up(self, l, f):
        nc = self.nc
        W1 = self.I["ffn_w1"][l, f]
        W3 = self.I["ffn_w3"][l, f]
        G = self.S["G"]
        with self.phase():
            gch = [self.sb([128, NT], BF16) for _ in range(2)]
            tmp = [self.sb([128, 512], F32) for _ in range(2)]
            cnt = [0]

            def mk(hs, nco):
                def fn(slots):
                    s1, s3 = slots
                    for j in range(nco // 128):
                        g = gch[cnt[0] % 2]
                        cnt[0] += 1
                        for (t0, tn) in TBS:
                            at = self.aT_tiles(t0, tn)
                            p1 = self.ps()
                            self.mm(p1, [(p1[:, :tn], [(s1[:, kc, j * 128:(j + 1) * 128], self.aTh[:, kc, t0:t0 + tn])
                                                       for kc in range(16)])], [s1] + at)
                            p3 = self.ps()
                            self.mm(p3, [(p3[:, :tn], [(s3[:, kc, j * 128:(j + 1) * 128], self.aTh[:, kc, t0:t0 + tn])
                                                       for kc in range(16)])], [s3] + at)
                            tm = tmp[cnt[0] % 2]
                            cnt[0] += 1
                            self.A(lambda: nc.scalar.activation(out=tm[:, :tn], in_=p1[:, :tn], func=AF.Silu),
                                   rd=[p1], wr=[tm])
                            self.V(lambda: nc.vector.tensor_tensor(out=g[:, t0:t0 + tn], in0=tm[:, :tn], in1=p3[:, :tn],
                                                                   op=ALU.mult), rd=[tm, p3], wr=[g])
                        r0 = hs * 512 + j * 128
                        self.st(G[r0:r0 + 128, :], g, g[:])
                return fn

            jobs = []
            for hs in range(11):
                c0 = hs * 512
                nco = min(512, DFF - c0)
                jobs.append(([W1[:, c0:c0 + nco], W3[:, c0:c0 + nco]], mk(hs, nco)))
            self.run_jobs(jobs, nslots=4)

    def phase_ffn_down(self, l, f):
        nc = self.nc
        W2 = self.I["ffn_w2"][l, f]
        G = self.S["G"]
        Y = self.S["Y"]
        groups = [(i * 256, 256) for i in range(8)] + [(2048, 128)]
        with self.phase():
            gb = [self.sb([128, 43, 256], BF16) for _ in range(2)]
            yo = [self.sb([128, 512], F32) for _ in range(3)]
            cnt = [0]

            def ldg(gi):
                t0, tn = groups[gi]
                t = gb[gi % 2]
                self.ld(t, t[:, :, :tn], G[:, t0:t0 + tn].rearrange("(j p) t -> p j t", p=128))
                return t

            def mk(cb):
                def fn(slots):
                    cur = ldg(0)
                    for gi, (t0, tn) in enumerate(groups):
                        nxt = ldg(gi + 1) if gi + 1 < len(groups) else None
                        for ti in range(tn // 128):
                            p = self.ps()
                            pairs = [(cur[:, j, ti * 128:(ti + 1) * 128], slots[j // 16][:, j % 16, :]) for j in range(43)]
                            self.mm(p, [(p[:, :], pairs)], list(slots) + [cur])
                            o = yo[cnt[0] % 3]
                            cnt[0] += 1
                            self.ev(lambda: nc.scalar.mul(out=o[:], in_=p[:], mul=0.5),
                                    lambda: nc.vector.tensor_scalar(out=o[:], in0=p[:], scalar1=0.5, scalar2=None,
                                                                    op0=ALU.mult), rd=[p], wr=[o])
                            r0 = t0 + ti * 128
                            self.st(Y[r0:r0 + 128, cb * 512:(cb + 1) * 512], o, o[:])
                        cur = nxt
                return fn

            jobs = []
            for cb in range(4):
                cs_ = slice(cb * 512, (cb + 1) * 512)
                jobs.append(([W2[0:2048, cs_], W2[2048:4096, cs_], W2[4096:5504, cs_]], mk(cb)))
            self.run_jobs(jobs, nslots=6)

    def job_T(self, wap, evac):
        nco = wap.shape[1]

        def fn(slots):
            s = slots[0]
            for tt in range(NTILE):
                p = self.ps()
                self.mm(p, [(p[:, :nco], [(self.aTh[:, kc, tt * 128:(tt + 1) * 128], s[:, kc, :nco]) for kc in range(16)])],
                        [s, self.aT[tt]])
                evac(tt, p, nco)
        return ([wap], fn)

    def job_F(self, wap, evac):
        nco = wap.shape[1]

        def fn(slots):
            s = slots[0]
            for j in range(nco // 128):
                for (t0, tn) in TBS:
                    p = self.ps()
                    self.mm(p, [(p[:, :tn], [(s[:, kc, j * 128:(j + 1) * 128], self.aTh[:, kc, t0:t0 + tn])
                                             for kc in range(16)])], [s] + self.aT_tiles(t0, tn))
                    evac(j, t0, tn, p)
        return ([wap], fn)

    def xbc_rows(self, tt):
        if tt < 16:
            return [(3 + tt * 128, 128, 0)]
        return [(3 + NPR + s * 11 + 3, 8, s * 8) for s in range(NSEQ_S)]

    def phase_win(self, l):
        nc = self.nc
        I, S = self.I, self.S
        W = I["w_in"][l]
        with self.phase():
            dtb = self.load_bc(I["ssd_dt_bias"][l:l + 1, :], 32)
            alog = self.load_bc(I["ssd_a_log"][l:l + 1, :], 32)
            aneg = self.sb([128, 32], F32)
            self.A(lambda: nc.scalar.activation(out=aneg[:], in_=alog[:], func=AF.Exp), rd=[alog], wr=[aneg])
            self.V(lambda: nc.vector.tensor_scalar(out=aneg[:], in0=aneg[:], scalar1=-1.0, scalar2=None, op0=ALU.mult),
                   rd=[aneg], wr=[aneg])
            gbias = self.load_bc(I["ml_gate_bias"][l:l + 1, :], 8)
            lb = self.sb([128, D], F32)
            oml = self.sb([128, D], F32)
            lbT = self.sb([128, 16], F32)
            omlT = self.sb([128, 16], F32)
            if l == 0:
                self.V(lambda: nc.vector.memset(lb[:], 0.0), wr=[lb])
                self.V(lambda: nc.vector.memset(lbT[:], 0.0), wr=[lbT])
            else:
                l0 = self.load_bc(I["hg_lb_logits"][0:1, :], D)
                l1 = self.load_bc(I["hg_lb_logits"][1:2, :], D)
                self.V(lambda: nc.vector.tensor_tensor(out=lb[:], in0=l1[:], in1=l0[:], op=ALU.subtract), rd=[l0, l1], wr=[lb])
                self.A(lambda: nc.scalar.activation(out=lb[:], in_=lb[:], func=AF.Sigmoid), rd=[lb], wr=[lb])
                t0_ = self.sb([128, 16], F32)
                t1_ = self.sb([128, 16], F32)
                with nc.allow_non_contiguous_dma(reason="tiny param transpose"):
                    self.ld(t0_, t0_[:], I["hg_lb_logits"][0, :].rearrange("(c p) -> p c", p=128))
                    self.ld(t1_, t1_[:], I["hg_lb_logits"][1, :].rearrange("(c p) -> p c", p=128))
                self.V(lambda: nc.vector.tensor_tensor(out=lbT[:], in0=t1_[:], in1=t0_[:], op=ALU.subtract), rd=[t0_, t1_], wr=[lbT])
                self.A(lambda: nc.scalar.activation(out=lbT[:], in_=lbT[:], func=AF.Sigmoid), rd=[lbT], wr=[lbT])
            self.V(lambda: nc.vector.tensor_scalar(out=oml[:], in0=lb[:], scalar1=-1.0, scalar2=1.0, op0=ALU.mult, op1=ALU.add),
                   rd=[lb], wr=[oml])
            self.V(lambda: nc.vector.tensor_scalar(out=omlT[:], in0=lbT[:], scalar1=-1.0, scalar2=1.0, op0=ALU.mult, op1=ALU.add),
                   rd=[lbT], wr=[omlT])

            ob16 = [self.sb([128, 512], BF16) for _ in range(3)]
            of32 = [self.sb([128, 512], F32) for _ in range(3)]
            tmpf = [self.sb([128, 512], F32) for _ in range(2)]
            cnt = [0]

            def nb16():
                cnt[0] += 1
                return ob16[cnt[0] % 3]

            def nf32():
                cnt[0] += 1
                return of32[cnt[0] % 3]

            def ntmp():
                cnt[0] += 1
                return tmpf[cnt[0] % 2]

            jobs = []

            def T_act(dst, c0, func, scale=1.0):
                def evac(tt, p, nco):
                    o = nb16()
                    self.A(lambda: nc.scalar.activation(out=o[:, :nco], in_=p[:, :nco], func=func, scale=scale), rd=[p], wr=[o])
                    self.st(dst[tt * 128:(tt + 1) * 128, c0:c0 + nco], o, o[:, :nco])
                return evac

            def T_copy(dst, c0, scale=1.0):
                def evac(tt, p, nco):
                    o = nb16()
                    self.ev(lambda: nc.scalar.mul(out=o[:, :nco], in_=p[:, :nco], mul=scale),
                            lambda: nc.vector.tensor_scalar(out=o[:, :nco], in0=p[:, :nco], scalar1=scale, scalar2=None,
                                                            op0=ALU.mult), rd=[p], wr=[o])
                    self.st(dst[tt * 128:(tt + 1) * 128, c0:c0 + nco], o, o[:, :nco])
                return evac

            def F_act(dst, r0, func, scale=1.0):
                def evac(j, t0, tn, p):
                    o = nb16()
                    self.A(lambda: nc.scalar.activation(out=o[:, :tn], in_=p[:, :tn], func=func, scale=scale), rd=[p], wr=[o])
                    self.st(dst[r0 + j * 128:r0 + (j + 1) * 128, t0:t0 + tn], o, o[:, :tn])
                return evac

            def F_copy(dst, r0, scale=1.0):
                def evac(j, t0, tn, p):
                    o = nb16()
                    self.ev(lambda: nc.scalar.mul(out=o[:, :tn], in_=p[:, :tn], mul=scale),
                            lambda: nc.vector.tensor_scalar(out=o[:, :tn], in0=p[:, :tn], scalar1=scale, scalar2=None,
                                                            op0=ALU.mult), rd=[p], wr=[o])
                    self.st(dst[r0 + j * 128:r0 + (j + 1) * 128, t0:t0 + tn], o, o[:, :tn])
                return evac

            def blocks(n):
                return [(c, min(512, n - c)) for c in range(0, n, 512)]

            for c0, n in blocks(2048):
                jobs.append(self.job_T(W[:, O_Z + c0:O_Z + c0 + n], T_act(S["ZS"], c0, AF.Silu)))

            def xbc_evac(c0):
                def evac(tt, p, nco):
                    o = nf32()
                    self.ev(lambda: nc.scalar.copy(out=o[:, :nco], in_=p[:, :nco]),
                            lambda: nc.vector.tensor_copy(out=o[:, :nco], in_=p[:, :nco]), rd=[p], wr=[o])
                    for (r0, nr, p0) in self.xbc_rows(tt):
                        self.st(S["XBC"][r0:r0 + nr, c0:c0 + nco], o, o[p0:p0 + nr, :nco])
                return evac
            for c0, n in blocks(3072):
                jobs.append(self.job_T(W[:, O_XBC + c0:O_XBC + c0 + n], xbc_evac(c0)))

            def dt_evac(tt, p, nco):
                o = nf32()
                self.V(lambda: nc.vector.tensor_tensor(out=o[:, 0:32], in0=p[:, 0:32], in1=dtb[:], op=ALU.add), rd=[p, dtb], wr=[o])
                self.A(lambda: nc.scalar.activation(out=o[:, 0:32], in_=o[:, 0:32], func=AF.Exp), rd=[o], wr=[o])
                self.A(lambda: nc.scalar.activation(out=o[:, 0:32], in_=o[:, 0:32], func=AF.Ln, bias=1.0, scale=1.0), rd=[o], wr=[o])
                self.V(lambda: nc.vector.tensor_tensor(out=o[:, 32:64], in0=o[:, 0:32], in1=aneg[:], op=ALU.mult), rd=[o, aneg], wr=[o])
                self.st(S["DTA"][tt * 128:(tt + 1) * 128, :], o, o[:, 0:64])
            jobs.append(self.job_T(W[:, O_DT:O_DT + 32], dt_evac))

            for c0, n in blocks(1024):
                jobs.append(self.job_F(W[:, O_MQ + c0:O_MQ + c0 + n], F_copy(S["MQ"], c0)))
            for c0, n in blocks(1024):
                jobs.append(self.job_F(W[:, O_MK + c0:O_MK + c0 + n], F_copy(S["MKT"], c0, 256 ** -0.5)))
            for c0, n in blocks(1024):
                jobs.append(self.job_T(W[:, O_MK + c0:O_MK + c0 + n], T_copy(S["MK"], c0, 256 ** -0.5)))
            for c0, n in blocks(2048):
                jobs.append(self.job_T(W[:, O_MV + c0:O_MV + c0 + n], T_copy(S["MV"], c0)))
            for c0, n in blocks(2048):
                jobs.append(self.job_T(W[:, O_MO + c0:O_MO + c0 + n], T_act(S["MO"], c0, AF.Sigmoid)))

            def if_evac(tt, p, nco):
                o = nf32()
                self.V(lambda: nc.vector.tensor_tensor(out=o[:, 0:8], in0=p[:, 0:8], in1=gbias[:], op=ALU.add), rd=[p, gbias], wr=[o])
                self.st(S["MIF"][tt * 128:(tt + 1) * 128, :], o, o[:, 0:8])
            jobs.append(self.job_T(W[:, O_MI:O_MI + 8], if_evac))

            for c0, n in blocks(2048):
                jobs.append(self.job_F(W[:, O_HQ + c0:O_HQ + c0 + n], F_act(S["HQ"], c0, AF.Silu)))

            def kt_evac(c0):
                def evac(j, t0, tn, p):
                    tm = ntmp()
                    o = nb16()
                    cidx = (c0 // 128) + j
                    self.A(lambda: nc.scalar.activation(out=tm[:, :tn], in_=p[:, :tn], func=AF.Sigmoid, scale=-1.0), rd=[p], wr=[tm])
                    self.V(lambda: nc.vector.tensor_scalar(out=o[:, :tn], in0=tm[:, :tn], scalar1=omlT[:, cidx:cidx + 1],
                                                           scalar2=None, op0=ALU.mult), rd=[tm, omlT], wr=[o])
                    self.st(S["KT"][c0 + j * 128:c0 + (j + 1) * 128, t0:t0 + tn], o, o[:, :tn])
                return evac
            for c0, n in blocks(2048):
                jobs.append(self.job_F(W[:, O_HF + c0:O_HF + c0 + n], kt_evac(c0)))

            def lf_evac(c0):
                def evac(tt, p, nco):
                    tm = ntmp()
                    o = nf32()
                    self.A(lambda: nc.scalar.activation(out=tm[:, :nco], in_=p[:, :nco], func=AF.Sigmoid), rd=[p], wr=[tm])
                    self.V(lambda: nc.vector.tensor_tensor(out=tm[:, :nco], in0=tm[:, :nco], in1=oml[:, c0:c0 + nco], op=ALU.mult),
                           rd=[tm, oml], wr=[tm])
                    self.V(lambda: nc.vector.tensor_tensor(out=tm[:, :nco], in0=tm[:, :nco], in1=lb[:, c0:c0 + nco], op=ALU.add),
                           rd=[tm, lb], wr=[tm])
                    self.A(lambda: nc.scalar.activation(out=o[:, :nco], in_=tm[:, :nco], func=AF.Ln), rd=[tm], wr=[o])
                    self.st(S["LF"][tt * 128:(tt + 1) * 128, c0:c0 + nco], o, o[:, :nco])
                return evac
            for c0, n in blocks(2048):
                jobs.append(self.job_T(W[:, O_HF + c0:O_HF + c0 + n], lf_evac(c0)))
            for c0, n in blocks(2048):
                jobs.append(self.job_T(W[:, O_HI + c0:O_HI + c0 + n], T_copy(S["HI"], c0)))
            for c0, n in blocks(2048):
                jobs.append(self.job_T(W[:, O_HG + c0:O_HG + c0 + n], T_act(S["HG"], c0, AF.Sigmoid)))
            for c0, n in blocks(3 * D):
                jobs.append(self.job_F(W[:, O_GT + c0:O_GT + c0 + n], F_act(S["GT"], c0, AF.Sigmoid)))
            self.run_jobs(jobs, nslots=4)

    def phase_conv(self, l):
        nc = self.nc
        I, S, O = self.I, self.S, self.O
        XBC = S["XBC"]
        with self.phase():
            z = self.sb([48, 3072], F32)
            self.V(lambda: nc.vector.memset(z[:3, :], 0.0), wr=[z])
            self.st(XBC[0:3, :], z, z[:3, :])
            hs = self.sb([48, 3072], F32)
            self.ld(hs, hs[:], I["st_conv"][l].rearrange("s r c -> (s r) c"))
            for s in range(NSEQ_S):
                b0 = 3 + NPR + s * 11
                self.st(XBC[b0:b0 + 3, :], hs, hs[s * 3:(s + 1) * 3, :])
        with self.phase():
            t = self.sb([51, 3072], F32)
            self.ld(t, t[0:3, :], XBC[NPR:NPR + 3, :])
            self.st(O["o_conv"][l, 0], t, t[0:3, :])
            for s in range(NSEQ_S):
                b0 = 3 + NPR + s * 11
                self.ld(t, t[3 + s * 3:6 + s * 3, :], XBC[b0 + 8:b0 + 11, :])
            self.st(O["o_conv"][l, 1:17].rearrange("s r c -> (s r) c"), t, t[3:51, :])
            HW = 1536
            wj = [self.load_bc(I["ssd_conv_w"][l, j:j + 1, :], 3072) for j in range(4)]
            cb = self.load_bc(I["ssd_conv_b"][l:l + 1, :], 3072)
            u = [[self.sb([128, HW], F32) for _ in range(4)] for _ in range(2)]
            acc = [self.sb([128, HW], F32) for _ in range(2)]
            t2 = [self.sb([128, HW], F32) for _ in range(2)]
            xa = [self.sb([128, HW], BF16) for _ in range(2)]
            bct = [self.sb([128, 8, 128], BF16) for _ in range(2)]
            it = 0
            for tt in range(NTILE):
                for hf in range(2):
                    c0 = hf * HW
                    uu = u[it % 2]
                    ac, tm, xo = acc[it % 2], t2[it % 2], xa[it % 2]
                    it += 1
                    for j in range(4):
                        for (r0, nr, p0) in self.xbc_rows(tt):
                            self.ld(uu[j], uu[j][p0:p0 + nr, :], XBC[r0 - 3 + j:r0 - 3 + j + nr, c0:c0 + HW])
                    self.V(lambda: nc.vector.tensor_tensor(out=ac[:], in0=uu[0][:], in1=wj[0][:, c0:c0 + HW], op=ALU.mult),
                           rd=[uu[0], wj[0]], wr=[ac])
                    for j in range(1, 4):
                        self.V(lambda: nc.vector.tensor_tensor(out=tm[:], in0=uu[j][:], in1=wj[j][:, c0:c0 + HW], op=ALU.mult),
                               rd=[uu[j], wj[j]], wr=[tm])
                        self.V(lambda: nc.vector.tensor_tensor(out=ac[:], in0=ac[:], in1=tm[:], op=ALU.add), rd=[ac, tm], wr=[ac])
                    self.V(lambda: nc.vector.tensor_tensor(out=ac[:], in0=ac[:], in1=cb[:, c0:c0 + HW], op=ALU.add), rd=[ac, cb], wr=[ac])
                    self.A(lambda: nc.scalar.activation(out=xo[:], in_=ac[:], func=AF.Silu), rd=[ac], wr=[xo])
                    self.st(S["XA"][tt * 128:(tt + 1) * 128, c0:c0 + HW], xo, xo[:])
                    if hf == 1:
                        bt = bct[tt % 2]
                        for g2 in range(2):
                            p = self.ps()
                            self.mm(p, [(p[:, j * 128:(j + 1) * 128],
                                         [(xo[:, 512 + (g2 * 4 + j) * 128:512 + (g2 * 4 + j + 1) * 128], self.identb[:])])
                                        for j in range(4)], [xo, self.identb])
                            self.A(lambda: nc.scalar.copy(out=bt[:, g2 * 4:(g2 + 1) * 4, :],
                                                          in_=p[:].rearrange("p (a b) -> p a b", a=4)), rd=[p], wr=[bt])
                        self.st(S["BCT"][:, tt * 128:(tt + 1) * 128].rearrange("(j p) t -> p j t", p=128), bt, bt[:])

    def seqs(self, pcs=PCS):
        out = [(0, pcs, [c * pcs for c in range(NPR // pcs)])]
        for s in range(NSEQ_S):
            out.append((s + 1, 8, [NPR + s * 8]))
        return out

    def cview(self, c0, n, rows):
        return self.cf[:rows, c0:c0 + n]

    def phase_ssd(self, l):
        nc = self.nc
        I, S, O = self.I, self.S, self.O
        cf = self.cf
        with self.phase():
            Dbc = self.load_bc(I["ssd_d"][l:l + 1, :], 32)
            nw = self.load_bc(I["ssd_norm"][l:l + 1, :], D)
            hTs = [self.sb([128, D], F32) for _ in range(2)]
            hTbs = [self.sb([128, D], BF16) for _ in range(2)]
            nats = [self.sb([128, 16, 128], F32) for _ in range(2)]
            NB = 2
            xs = [self.sb([128, D], BF16) for _ in range(NB)]
            Bt = [self.sb([128, 512], BF16) for _ in range(NB)]
            BT = [self.sb([128, 4, 128], BF16) for _ in range(NB)]
            CT = [self.sb([128, 4, 128], BF16) for _ in range(NB)]
            dta = [self.sb([128, 64], F32) for _ in range(NB)]
            zs = [self.sb([128, D], BF16) for _ in range(NB)]
            ex = self.sb([128, 96], F32)
            R = self.sb([128, 32, 128], F32)
            E = self.sb([128, 32, 128], BF16)
            cbm = self.sb([128, 4, 128], BF16)
            wT = self.sb([128, 32, 128], BF16)
            xdt = self.sb([128, D], BF16)
            xdl = self.sb([128, D], BF16)
            yb = self.sb([128, D], F32)
            tmp = self.sb([128, D], F32)
            ss = self.sb([128, 4], F32)
            yo = [self.sb([128, D], BF16) for _ in range(2)]

            def loads(bi, cs, t0):
                self.ld(xs[bi], xs[bi][:cs, :], S["XA"][t0:t0 + cs, 0:2048])
                self.ld(Bt[bi], Bt[bi][:cs, :], S["XA"][t0:t0 + cs, 2048:2560])
                self.ld(BT[bi], BT[bi][:, :, :cs], S["BCT"][0:512, t0:t0 + cs].rearrange("(g n) t -> n g t", n=128))
                self.ld(CT[bi], CT[bi][:, :, :cs], S["BCT"][512:1024, t0:t0 + cs].rearrange("(g n) t -> n g t", n=128))
                self.ld(dta[bi], dta[bi][:cs, :], S["DTA"][t0:t0 + cs, :])
                self.ld(zs[bi], zs[bi][:cs, :], S["ZS"][t0:t0 + cs, :])

            chunks = []
            for (sq, cs, offs) in self.seqs():
                for ci, t0 in enumerate(offs):
                    chunks.append((sq, cs, t0, ci == 0, ci == len(offs) - 1))
            with nc.allow_non_contiguous_dma(reason="small chunk loads"):
                loads(0, chunks[0][1], chunks[0][2])
                for idx, (sq, cs, t0, first, last) in enumerate(chunks):
                    bi = idx % NB
                    hT, hTb, nat = hTs[sq % 2], hTbs[sq % 2], nats[sq % 2]
                    if idx + 1 < len(chunks):
                        loads((idx + 1) % NB, chunks[idx + 1][1], chunks[idx + 1][2])
                        nsq = chunks[idx + 1][0]
                        if chunks[idx + 1][3] and nsq >= 1:
                            self.ld(nats[nsq % 2], nats[nsq % 2][:], I["st_ssd"][l, nsq - 1].rearrange("(j p) n -> p j n", p=128))
                    x_, B_, BT_, CT_, d_, z_ = xs[bi], Bt[bi], BT[bi], CT[bi], dta[bi], zs[bi]
                    tri = cf[:cs, C_TRI:C_TRI + cs]
                    Lm = cf[:cs, C_L:C_L + cs]
                    if first:
                        if sq == 0:
                            self.V(lambda: nc.vector.memset(hT[:], 0.0), wr=[hT])
                            self.V(lambda: nc.vector.memset(hTb[:], 0.0), wr=[hTb])
                        else:
                            for g in range(4):
                                p = self.ps()
                                self.mm(p, [(p[:, j * 128:(j + 1) * 128], [(nat[:, g * 4 + j, :], cf[:, C_ID:C_ID + 128])])
                                            for j in range(4)], [nat, cf])
                                self.V(lambda: nc.vector.tensor_copy(out=hT[:, g * 512:(g + 1) * 512], in_=p[:]), rd=[p], wr=[hT])
                            self.A(lambda: nc.scalar.copy(out=hTb[:], in_=hT[:]), rd=[hT], wr=[hTb])
                    a_ = d_[:cs, 32:64]
                    dt_ = d_[:cs, 0:32]
                    pA = self.ps()
                    self.mm(pA, [(pA[:cs, 0:32], [(tri, a_)]), (pA[:cs, 32:64], [(Lm, a_)]),
                                 (pA[:, 64:96], [(cf[:cs, C_ONES:C_ONES + 128], a_)])], [cf, d_])
                    self.A(lambda: nc.scalar.activation(out=ex[:cs, 0:64], in_=pA[:cs, 0:64], func=AF.Exp), rd=[pA], wr=[ex])
                    self.A(lambda: nc.scalar.activation(out=ex[:, 64:96], in_=pA[:, 64:96], func=AF.Exp), rd=[pA], wr=[ex])
                    self.V(lambda: nc.vector.tensor_tensor(out=R[:cs, :, :cs], in0=self.bc(a_, [cs, 32, cs], 2),
                                                           in1=self.bc(tri, [cs, 32, cs], 1), op=ALU.mult), rd=[d_, cf], wr=[R])
                    hp = min(32, 512 // cs)
                    for q in range(32 // hp):
                        p = self.ps()
                        self.mm(p, [(p[:cs, :hp * cs].rearrange("p (a b) -> p a b", a=hp), [(Lm, R[:cs, q * hp:(q + 1) * hp, :cs])])], [cf, R])
                        self.A(lambda: nc.scalar.activation(out=E[:cs, q * hp:(q + 1) * hp, :cs],
                                                            in_=p[:cs, :hp * cs].rearrange("p (a b) -> p a b", a=hp), func=AF.Exp),
                               rd=[p], wr=[E])
                    pC = self.ps()
                    self.mm(pC, [(pC[:cs, g * cs:(g + 1) * cs], [(BT_[:, g, :cs], CT_[:, g, :cs])]) for g in range(4)], [BT_, CT_])
                    self.V(lambda: nc.vector.tensor_tensor(out=cbm[:cs, :, :cs], in0=pC[:cs, :4 * cs].rearrange("p (a b) -> p a b", a=4),
                                                           in1=self.bc(tri, [cs, 4, cs], 1), op=ALU.mult), rd=[pC, cf], wr=[cbm])
                    for g in range(4):
                        self.V(lambda: nc.vector.tensor_tensor(out=wT[:cs, g * 8:(g + 1) * 8, :cs], in0=E[:cs, g * 8:(g + 1) * 8, :cs],
                                                               in1=self.bc(cbm[:cs, g, :cs], [cs, 8, cs], 1), op=ALU.mult),
                               rd=[E, cbm], wr=[wT])
                    self.V(lambda: nc.vector.tensor_tensor(out=xdt[:cs, :].rearrange("p (h d) -> p h d", h=32),
                                                           in0=x_[:cs, :].rearrange("p (h d) -> p h d", h=32),
                                                           in1=self.bc(dt_, [cs, 32, 64], 2), op=ALU.mult), rd=[x_, d_], wr=[xdt])
                    for g in range(4):
                        pY = self.ps()
                        self.mm(pY, [(pY[:cs, hh * 64:(hh + 1) * 64],
                                      [(wT[:cs, g * 8 + hh, :cs], xdt[:cs, (g * 8 + hh) * 64:(g * 8 + hh + 1) * 64])]) for hh in range(8)],
                                [wT, xdt])
                        pI = self.ps()
                        self.mm(pI, [(pI[:cs, :512], [(CT_[:, g, :cs], hTb[:, g * 512:(g + 1) * 512])])], [CT_, hTb])
                        self.V(lambda: nc.vector.tensor_tensor(out=tmp[:cs, g * 512:(g + 1) * 512].rearrange("p (h d) -> p h d", h=8),
                                                               in0=pI[:cs, :512].rearrange("p (h d) -> p h d", h=8),
                                                               in1=self.bc(ex[:cs, g * 8:(g + 1) * 8], [cs, 8, 64], 2), op=ALU.mult),
                               rd=[pI, ex], wr=[tmp])
                        self.V(lambda: nc.vector.tensor_tensor(out=yb[:cs, g * 512:(g + 1) * 512], in0=pY[:cs, :512],
                                                               in1=tmp[:cs, g * 512:(g + 1) * 512], op=ALU.add), rd=[pY, tmp], wr=[yb])
                    self.V(lambda: nc.vector.tensor_tensor(out=tmp[:cs, :].rearrange("p (h d) -> p h d", h=32),
                                                           in0=x_[:cs, :].rearrange("p (h d) -> p h d", h=32),
                                                           in1=self.bc(Dbc[:cs, :], [cs, 32, 64], 2), op=ALU.mult), rd=[x_, Dbc], wr=[tmp])
                    self.V(lambda: nc.vector.tensor_tensor(out=yb[:cs, :], in0=yb[:cs, :], in1=tmp[:cs, :], op=ALU.add), rd=[yb, tmp], wr=[yb])
                    self.V(lambda: nc.vector.tensor_tensor(out=yb[:cs, :], in0=yb[:cs, :], in1=z_[:cs, :], op=ALU.mult), rd=[yb, z_], wr=[yb])
                    self.A(lambda: nc.scalar.activation(out=tmp[:cs, :], in_=yb[:cs, :], func=AF.Square), rd=[yb], wr=[tmp])
                    self.V(lambda: nc.vector.tensor_reduce(out=ss[:cs, :], in_=tmp[:cs, :].rearrange("p (g d) -> p g d", g=4),
                                                           axis=AX.X, op=ALU.add), rd=[tmp], wr=[ss])
                    self.rstd(ss, cs, 4, 1.0 / 512, 1e-6)
                    for g in range(4):
                        self.A(lambda: nc.scalar.activation(out=yb[:cs, g * 512:(g + 1) * 512], in_=yb[:cs, g * 512:(g + 1) * 512],
                                                            func=AF.Copy, scale=ss[:cs, g:g + 1]), rd=[yb, ss], wr=[yb])
                    yo_ = yo[idx % 2]
                    self.V(lambda: nc.vector.tensor_tensor(out=yo_[:cs, :], in0=yb[:cs, :], in1=nw[:cs, :], op=ALU.mult), rd=[yb, nw], wr=[yo_])
                    self.st(S["YB"][0, t0:t0 + cs, :], yo_, yo_[:cs, :])
                    self.V(lambda: nc.vector.tensor_tensor(out=xdl[:cs, :].rearrange("p (h d) -> p h d", h=32),
                                                           in0=xdt[:cs, :].rearrange("p (h d) -> p h d", h=32),
                                                           in1=self.bc(ex[:cs, 32:64], [cs, 32, 64], 2), op=ALU.mult), rd=[xdt, ex], wr=[xdl])
                    for g in range(4):
                        pU = self.ps()
                        self.mm(pU, [(pU[:, :512], [(B_[:cs, g * 128:(g + 1) * 128], xdl[:cs, g * 512:(g + 1) * 512])])], [B_, xdl])
                        self.V(lambda: nc.vector.tensor_tensor(out=hT[:, g * 512:(g + 1) * 512].rearrange("p (h d) -> p h d", h=8),
                                                               in0=hT[:, g * 512:(g + 1) * 512].rearrange("p (h d) -> p h d", h=8),
                                                               in1=self.bc(ex[:, 64 + g * 8:64 + (g + 1) * 8], [128, 8, 64], 2), op=ALU.mult),
                               rd=[hT, ex], wr=[hT])
                        self.V(lambda: nc.vector.tensor_tensor(out=hT[:, g * 512:(g + 1) * 512], in0=hT[:, g * 512:(g + 1) * 512],
                                                               in1=pU[:, :512], op=ALU.add), rd=[hT, pU], wr=[hT])
                    if not last:
                        self.A(lambda: nc.scalar.copy(out=hTb[:], in_=hT[:]), rd=[hT], wr=[hTb])
                    else:
                        for g in range(4):
                            p = self.ps()
                            self.mm(p, [(p[:, j * 128:(j + 1) * 128], [(hT[:, (g * 4 + j) * 128:(g * 4 + j + 1) * 128], cf[:, C_ID:C_ID + 128])])
                                        for j in range(4)], [hT, cf])
                            self.A(lambda: nc.scalar.copy(out=nat[:, g * 4:(g + 1) * 4, :], in_=p[:].rearrange("p (a b) -> p a b", a=4)),
                                   rd=[p], wr=[nat])
                        self.st(O["o_ssd"][l, sq].rearrange("(j p) n -> p j n", p=128), nat, nat[:])

    def phase_mlstm(self, l):
        nc = self.nc
        I, S, O = self.I, self.S, self.O
        cf = self.cf
        with self.phase():
            nw = self.load_bc(I["ml_norm"][l:l + 1, :], D)
            cS = [self.sb([128, 8, 512], F32) for _ in range(2)]
            cbS = [self.sb([128, 8, 512], BF16) for _ in range(2)]
            nS = [self.sb([128, 8], F32) for _ in range(2)]
            nbS = [self.sb([128, 8], BF16) for _ in range(2)]
            mbcS = [self.sb([128, 4], F32) for _ in range(2)]
            n8S = [self.sb([8, 128], F32) for _ in range(2)]
            NB = 2
            qT = [self.sb([128, 8, 128], BF16) for _ in range(NB)]
            kT = [self.sb([128, 8, 128], BF16) for _ in range(NB)]
            kt = [self.sb([128, 1024], BF16) for _ in range(NB)]
            v = [self.sb([128, D], BF16) for _ in range(NB)]
            mo = [self.sb([128, D], BF16) for _ in range(NB)]
            gif = [self.sb([128, 8], F32) for _ in range(NB)]
            sm = self.sb([128, 64], F32)
            Dg = self.sb([128, 4, 128], F32)
            dm = self.sb([128, 4, 128], F32)
            wts = self.sb([128, 4, 128], F32)
            Pm = self.sb([128, 4, 128], F32)
            Pb = self.sb([128, 4, 128], BF16)
            PTb = self.sb([128, 4, 128], BF16)
            tmpn = self.sb([128, 512], F32)
            hm = self.sb([128, D], F32)
            sqt = self.sb([128, D], F32)
            yo = [self.sb([128, D], BF16) for _ in range(2)]
            kw = self.sb([128, 1024], BF16)

            def loads(bi, cs, t0):
                self.ld(qT[bi], qT[bi][:, :, :cs], S["MQ"][:, t0:t0 + cs].rearrange("(j p) t -> p j t", p=128))
                self.ld(kT[bi], kT[bi][:, :, :cs], S["MKT"][:, t0:t0 + cs].rearrange("(j p) t -> p j t", p=128))
                self.ld(kt[bi], kt[bi][:cs, :], S["MK"][t0:t0 + cs, :])
                self.ld(v[bi], v[bi][:cs, :], S["MV"][t0:t0 + cs, :])
                self.ld(mo[bi], mo[bi][:cs, :], S["MO"][t0:t0 + cs, :])
                self.ld(gif[bi], gif[bi][:cs, :], S["MIF"][t0:t0 + cs, :])

            chunks = []
            for (sq, cs, offs) in self.seqs():
                for ci, t0 in enumerate(offs):
                    chunks.append((sq, cs, t0, ci == 0, ci == len(offs) - 1))
            with nc.allow_non_contiguous_dma(reason="small chunk loads"):
                loads(0, chunks[0][1], chunks[0][2])
                for idx, (sq, cs, t0, first, last) in enumerate(chunks):
                    bi = idx % NB
                    c, cb, n, nb, mbc, n8 = cS[sq % 2], cbS[sq % 2], nS[sq % 2], nbS[sq % 2], mbcS[sq % 2], n8S[sq % 2]
                    if idx + 1 < len(chunks):
                        loads((idx + 1) % NB, chunks[idx + 1][1], chunks[idx + 1][2])
                        nsq = chunks[idx + 1][0]
                        if chunks[idx + 1][3] and nsq >= 1:
                            pp = nsq % 2
                            self.ld(cS[pp], cS[pp][:], I["st_mc"][l, nsq - 1].rearrange("h (dc p) v -> p (h dc) v", p=128))
                            self.ld(n8S[pp], n8S[pp][:], I["st_mn"][l, nsq - 1])
                            self.ld(mbcS[pp], mbcS[pp][:], I["st_mm"][l, nsq - 1:nsq, :].partition_broadcast(128))
                    q_, kT_, kt_, v_, mo_, g_ = qT[bi], kT[bi], kt[bi], v[bi], mo[bi], gif[bi]
                    tri = cf[:cs, C_TRI:C_TRI + cs]
                    Lm = cf[:cs, C_L:C_L + cs]
                    idf = cf[:cs, C_ID:C_ID + cs]
                    sel = C_SEL128 if cs == PCS else C_SEL8
                    if first:
                        if sq == 0:
                            self.V(lambda: nc.vector.memset(c[:], 0.0), wr=[c])
                            self.V(lambda: nc.vector.memset(cb[:], 0.0), wr=[cb])
                            self.V(lambda: nc.vector.memset(n[:], 0.0), wr=[n])
                            self.V(lambda: nc.vector.memset(nb[:], 0.0), wr=[nb])
                            self.V(lambda: nc.vector.memset(mbc[:], 0.0), wr=[mbc])
                        else:
                            p = self.ps()
                            self.mm(p, [(p[:, 0:8], [(n8[:, :], cf[:8, C_ID:C_ID + 8])])], [n8, cf])
                            self.V(lambda: nc.vector.tensor_copy(out=n[:], in_=p[:, 0:8]), rd=[p], wr=[n])
                            self.A(lambda: nc.scalar.copy(out=nb[:], in_=n[:]), rd=[n], wr=[nb])
                            self.A(lambda: nc.scalar.copy(out=cb[:], in_=c[:]), rd=[c], wr=[cb])
                    self.A(lambda: nc.scalar.activation(out=sm[:cs, 0:4], in_=g_[:cs, 4:8], func=AF.Exp, scale=-1.0), rd=[g_], wr=[sm])
                    self.A(lambda: nc.scalar.activation(out=sm[:cs, 0:4], in_=sm[:cs, 0:4], func=AF.Ln, bias=1.0, scale=1.0), rd=[sm], wr=[sm])
                    self.V(lambda: nc.vector.tensor_scalar(out=sm[:cs, 0:4], in0=sm[:cs, 0:4], scalar1=-1.0, scalar2=None, op0=ALU.mult),
                           rd=[sm], wr=[sm])
                    pA = self.ps()
                    self.mm(pA, [(pA[:cs, 0:4], [(tri, sm[:cs, 0:4])]), (pA[:cs, 4:8], [(Lm, sm[:cs, 0:4])])], [cf, sm])
                    self.V(lambda: nc.vector.tensor_copy(out=sm[:cs, 4:12], in_=pA[:cs, 0:8]), rd=[pA], wr=[sm])
                    self.V(lambda: nc.vector.tensor_tensor(out=sm[:cs, 12:16], in0=g_[:cs, 0:4], in1=sm[:cs, 4:8], op=ALU.subtract),
                           rd=[g_, sm], wr=[sm])
                    self.V(lambda: nc.vector.tensor_tensor(out=Dg[:cs, :, :cs], in0=self.bc(idf, [cs, 4, cs], 1),
                                                           in1=self.bc(sm[:cs, 12:16], [cs, 4, cs], 2), op=ALU.mult), rd=[cf, sm], wr=[Dg])
                    pR = self.ps()
                    self.mm(pR, [(pR[:cs, :4 * cs].rearrange("p (a b) -> p a b", a=4), [(cf[:cs, C_ONES:C_ONES + cs], Dg[:cs, :, :cs])])], [cf, Dg])
                    self.V(lambda: nc.vector.tensor_tensor(out=dm[:cs, :, :cs], in0=pR[:cs, :4 * cs].rearrange("p (a b) -> p a b", a=4),
                                                           in1=self.bc(sm[:cs, 4:8], [cs, 4, cs], 2), op=ALU.add), rd=[pR, sm], wr=[dm])
                    self.V(lambda: nc.vector.tensor_tensor(out=dm[:cs, :, :cs], in0=dm[:cs, :, :cs],
                                                           in1=self.bc(cf[:cs, C_MNEG:C_MNEG + cs], [cs, 4, cs], 1), op=ALU.add), rd=[dm, cf], wr=[dm])
                    self.V(lambda: nc.vector.tensor_reduce(out=sm[:cs, 16:20], in_=dm[:cs, :, :cs], axis=AX.X, op=ALU.max), rd=[dm], wr=[sm])
                    self.V(lambda: nc.vector.tensor_tensor(out=sm[:cs, 20:24], in0=sm[:cs, 4:8], in1=mbc[:cs, :], op=ALU.add), rd=[sm, mbc], wr=[sm])
                    self.V(lambda: nc.vector.tensor_tensor(out=sm[:cs, 24:28], in0=sm[:cs, 20:24], in1=sm[:cs, 16:20], op=ALU.max), rd=[sm], wr=[sm])
                    self.V(lambda: nc.vector.tensor_scalar(out=sm[:cs, 28:32], in0=sm[:cs, 24:28], scalar1=-1.0, scalar2=None, op0=ALU.mult),
                           rd=[sm], wr=[sm])
                    for h in range(4):
                        self.A(lambda: nc.scalar.activation(out=wts[:cs, h, :cs], in_=dm[:cs, h, :cs], func=AF.Exp,
                                                            bias=sm[:cs, 28 + h:29 + h], scale=1.0), rd=[dm, sm], wr=[wts])
                    self.V(lambda: nc.vector.tensor_tensor(out=sm[:cs, 32:36], in0=sm[:cs, 20:24], in1=sm[:cs, 24:28], op=ALU.subtract),
                           rd=[sm], wr=[sm])
                    self.A(lambda: nc.scalar.activation(out=sm[:cs, 32:36], in_=sm[:cs, 32:36], func=AF.Exp), rd=[sm], wr=[sm])
                    self.A(lambda: nc.scalar.activation(out=sm[:cs, 36:40], in_=sm[:cs, 28:32], func=AF.Exp), rd=[sm], wr=[sm])
                    pQ = self.ps()
                    self.mm(pQ, [(pQ[:cs, h * cs:(h + 1) * cs], [(q_[:, 2 * h + dc, :cs], kT_[:, 2 * h + dc, :cs]) for dc in range(2)])
                                 for h in range(4)], [q_, kT_])
                    self.V(lambda: nc.vector.tensor_tensor(out=Pm[:cs, :, :cs], in0=pQ[:cs, :4 * cs].rearrange("p (a b) -> p a b", a=4),
                                                           in1=wts[:cs, :, :cs], op=ALU.mult), rd=[pQ, wts], wr=[Pm])
                    self.V(lambda: nc.vector.tensor_reduce(out=sm[:cs, 40:44], in_=Pm[:cs, :, :cs], axis=AX.X, op=ALU.add), rd=[Pm], wr=[sm])
                    self.A(lambda: nc.scalar.copy(out=Pb[:cs, :, :cs], in_=Pm[:cs, :, :cs]), rd=[Pm], wr=[Pb])
                    pP = self.ps()
                    self.mm(pP, [(pP[:cs, h * cs:(h + 1) * cs], [(Pb[:cs, h, :cs], self.identb[:cs, :cs])]) for h in range(4)], [Pb, self.identb])
                    self.A(lambda: nc.scalar.copy(out=PTb[:cs, :, :cs], in_=pP[:cs, :4 * cs].rearrange("p (a b) -> p a b", a=4)), rd=[pP], wr=[PTb])
                    pD = self.ps()
                    self.mm(pD, [(pD[:cs, h:h + 1], [(q_[:, 2 * h + dc, :cs], nb[:, 2 * h + dc:2 * h + dc + 1]) for dc in range(2)])
                                 for h in range(4)], [q_, nb])
                    self.V(lambda: nc.vector.tensor_tensor(out=sm[:cs, 44:48], in0=pD[:cs, 0:4], in1=sm[:cs, 32:36], op=ALU.mult), rd=[pD, sm], wr=[sm])
                    self.V(lambda: nc.vector.tensor_tensor(out=sm[:cs, 40:44], in0=sm[:cs, 40:44], in1=sm[:cs, 44:48], op=ALU.add), rd=[sm], wr=[sm])
                    self.V(lambda: nc.vector.tensor_scalar(out=sm[:cs, 44:48], in0=sm[:cs, 40:44], scalar1=-1.0, scalar2=None, op0=ALU.mult), rd=[sm], wr=[sm])
                    self.V(lambda: nc.vector.tensor_tensor(out=sm[:cs, 40:44], in0=sm[:cs, 40:44], in1=sm[:cs, 44:48], op=ALU.max), rd=[sm], wr=[sm])
                    self.V(lambda: nc.vector.tensor_tensor(out=sm[:cs, 40:44], in0=sm[:cs, 40:44], in1=sm[:cs, 36:40], op=ALU.max), rd=[sm], wr=[sm])
                    self.V(lambda: nc.vector.reciprocal(out=sm[:cs, 44:48], in_=sm[:cs, 40:44]), rd=[sm], wr=[sm])
                    for h in range(4):
                        pN = self.ps()
                        self.mm(pN, [(pN[:cs, :512], [(PTb[:cs, h, :cs], v_[:cs, h * 512:(h + 1) * 512])])], [PTb, v_])
                        pNi = self.ps()
                        self.mm(pNi, [(pNi[:cs, :512], [(q_[:, 2 * h + dc, :cs], cb[:, 2 * h + dc, :]) for dc in range(2)])], [q_, cb])
                        self.A(lambda: nc.scalar.activation(out=tmpn[:cs, :], in_=pNi[:cs, :512], func=AF.Copy, scale=sm[:cs, 32 + h:33 + h]),
                               rd=[pNi, sm], wr=[tmpn])
                        self.V(lambda: nc.vector.tensor_tensor(out=tmpn[:cs, :], in0=tmpn[:cs, :], in1=pN[:cs, :512], op=ALU.add), rd=[tmpn, pN], wr=[tmpn])
                        self.V(lambda: nc.vector.tensor_scalar(out=hm[:cs, h * 512:(h + 1) * 512], in0=tmpn[:cs, :], scalar1=sm[:cs, 44 + h:45 + h],
                                                               scalar2=None, op0=ALU.mult), rd=[tmpn, sm], wr=[hm])
                    self.A(lambda: nc.scalar.activation(out=sqt[:cs, :], in_=hm[:cs, :], func=AF.Square), rd=[hm], wr=[sqt])
                    self.V(lambda: nc.vector.tensor_reduce(out=sm[:cs, 60:64], in_=sqt[:cs, :].rearrange("p (g d) -> p g d", g=4),
                                                           axis=AX.X, op=ALU.add), rd=[sqt], wr=[sm])
                    a_ss = sm[:cs, 60:64]
                    self.V(lambda: nc.vector.tensor_scalar(out=a_ss, in0=a_ss, scalar1=1.0 / 512, scalar2=1e-6, op0=ALU.mult, op1=ALU.add), rd=[sm], wr=[sm])
                    self.A(lambda: nc.scalar.activation(out=a_ss, in_=a_ss, func=AF.Ln), rd=[sm], wr=[sm])
                    self.A(lambda: nc.scalar.activation(out=a_ss, in_=a_ss, func=AF.Exp, scale=-0.5), rd=[sm], wr=[sm])
                    for h in range(4):
                        self.A(lambda: nc.scalar.activation(out=hm[:cs, h * 512:(h + 1) * 512], in_=hm[:cs, h * 512:(h + 1) * 512],
                                                            func=AF.Copy, scale=sm[:cs, 60 + h:61 + h]), rd=[hm, sm], wr=[hm])
                    self.V(lambda: nc.vector.tensor_tensor(out=hm[:cs, :], in0=hm[:cs, :], in1=nw[:cs, :], op=ALU.mult), rd=[hm, nw], wr=[hm])
                    yo_ = yo[idx % 2]
                    self.V(lambda: nc.vector.tensor_tensor(out=yo_[:cs, :], in0=hm[:cs, :], in1=mo_[:cs, :], op=ALU.mult), rd=[hm, mo_], wr=[yo_])
                    self.st(S["YB"][1, t0:t0 + cs, :], yo_, yo_[:cs, :])
                    pM = self.ps()
                    self.mm(pM, [(pM[:cs, 0:4], [(cf[:cs, sel:sel + cs], sm[:cs, 24:28])]),
                                 (pM[:, 4:8], [(cf[:cs, sel:sel + 128], sm[:cs, 32:36])]),
                                 (pM[:, 8:12], [(cf[:cs, sel:sel + 128], sm[:cs, 24:28])])], [cf, sm])
                    self.V(lambda: nc.vector.tensor_copy(out=sm[:cs, 48:52], in_=pM[:cs, 0:4]), rd=[pM], wr=[sm])
                    self.V(lambda: nc.vector.tensor_copy(out=sm[:, 52:56], in_=pM[:, 4:8]), rd=[pM], wr=[sm])
                    self.V(lambda: nc.vector.tensor_copy(out=mbc[:], in_=pM[:, 8:12]), rd=[pM], wr=[mbc])
                    self.V(lambda: nc.vector.tensor_tensor(out=sm[:cs, 56:60], in0=sm[:cs, 8:12], in1=g_[:cs, 0:4], op=ALU.add), rd=[sm, g_], wr=[sm])
                    self.V(lambda: nc.vector.tensor_tensor(out=sm[:cs, 56:60], in0=sm[:cs, 56:60], in1=sm[:cs, 48:52], op=ALU.subtract), rd=[sm], wr=[sm])
                    self.A(lambda: nc.scalar.activation(out=sm[:cs, 56:60], in_=sm[:cs, 56:60], func=AF.Exp), rd=[sm], wr=[sm])
                    self.V(lambda: nc.vector.tensor_tensor(out=kw[:cs, :].rearrange("p (h d) -> p h d", h=4),
                                                           in0=kt_[:cs, :].rearrange("p (h d) -> p h d", h=4),
                                                           in1=self.bc(sm[:cs, 56:60], [cs, 4, 256], 2), op=ALU.mult), rd=[kt_, sm], wr=[kw])
                    pNn = self.ps()
                    self.mm(pNn, [(pNn[:, j:j + 1], [(kw[:cs, j * 128:(j + 1) * 128], self.onesb[:cs, 0:1])]) for j in range(8)], [kw, self.onesb])
                    self.V(lambda: nc.vector.tensor_tensor(out=n[:].rearrange("p (h d) -> p h d", h=4), in0=n[:].rearrange("p (h d) -> p h d", h=4),
                                                           in1=self.bc(sm[:, 52:56], [128, 4, 2], 2), op=ALU.mult), rd=[n, sm], wr=[n])
                    self.V(lambda: nc.vector.tensor_tensor(out=n[:], in0=n[:], in1=pNn[:, 0:8], op=ALU.add), rd=[n, pNn], wr=[n])
                    for j in range(8):
                        h = j // 2
                        pU = self.ps()
                        self.mm(pU, [(pU[:, :512], [(kw[:cs, j * 128:(j + 1) * 128], v_[:cs, h * 512:(h + 1) * 512])])], [kw, v_])
                        self.V(lambda: nc.vector.scalar_tensor_tensor(out=c[:, j, :], in0=c[:, j, :], scalar=sm[:, 52 + h:53 + h], in1=pU[:, :512],
                                                                      op0=ALU.mult, op1=ALU.add), rd=[c, sm, pU], wr=[c])
                    if not last:
                        self.A(lambda: nc.scalar.copy(out=cb[:], in_=c[:]), rd=[c], wr=[cb])
                        self.A(lambda: nc.scalar.copy(out=nb[:], in_=n[:]), rd=[n], wr=[nb])
                    else:
                        self.st(O["o_mc"][l, sq].rearrange("h (dc p) v -> p (h dc) v", p=128), c, c[:])
                        p = self.ps()
                        self.mm(p, [(p[:8, 0:128], [(n[:, :], cf[:, C_ID:C_ID + 128])])], [n, cf])
                        self.V(lambda: nc.vector.tensor_copy(out=n8[:], in_=p[:8, 0:128]), rd=[p], wr=[n8])
                        self.st(O["o_mn"][l, sq], n8, n8[:])
                        self.st(O["o_mm"][l, sq:sq + 1, :], mbc, mbc[0:1, :])

    def phase_hgrn(self, l):
        nc = self.nc
        I, S, O = self.I, self.S, self.O
        cf = self.cf
        with self.phase():
            nw = self.load_bc(I["hg_norm"][l:l + 1, :], D)
            StS = [self.sb([128, 16, 128], F32) for _ in range(2)]
            SbS = [self.sb([128, 16, 128], BF16) for _ in range(2)]
            NB = 2
            lf = [self.sb([128, D], F32) for _ in range(NB)]
            qT = [self.sb([128, 16, 128], BF16) for _ in range(NB)]
            kT = [self.sb([128, 16, 128], BF16) for _ in range(NB)]
            v = [self.sb([128, D], BF16) for _ in range(NB)]
            g = [self.sb([128, D], BF16) for _ in range(NB)]
            erem = self.sb([128, D], F32)
            ef = self.sb([128, D], F32)
            khat = self.sb([128, D], BF16)
            bT = self.sb([128, 16, 128], F32)
            dl = self.sb([128, 16, 128], F32)
            e1 = self.sb([128, 16, 128], BF16)
            e3 = self.sb([128, 16, 128], F32)
            qt = self.sb([128, 16, 128], BF16)
            ktl = self.sb([128, 16, 128], BF16)
            qb = self.sb([128, 16, 128], BF16)
            attm = self.sb([128, 16, 128], BF16)
            y = self.sb([128, D], F32)
            sq = self.sb([128, D], F32)
            ss = self.sb([128, 16], F32)
            yo = [self.sb([128, D], BF16) for _ in range(2)]

            def loads(bi, cs, t0):
                self.ld(lf[bi], lf[bi][:cs, :], S["LF"][t0:t0 + cs, :])
                self.ld(qT[bi], qT[bi][:, :, :cs], S["HQ"][:, t0:t0 + cs].rearrange("(h p) t -> p h t", p=128))
                self.ld(kT[bi], kT[bi][:, :, :cs], S["KT"][:, t0:t0 + cs].rearrange("(h p) t -> p h t", p=128))
                self.ld(v[bi], v[bi][:cs, :], S["HI"][t0:t0 + cs, :])
                self.ld(g[bi], g[bi][:cs, :], S["HG"][t0:t0 + cs, :])

            chunks = []
            for (sq_, cs, offs) in self.seqs(64):
                for ci, t0 in enumerate(offs):
                    chunks.append((sq_, cs, t0, ci == 0, ci == len(offs) - 1))
            with nc.allow_non_contiguous_dma(reason="small chunk loads"):
                loads(0, chunks[0][1], chunks[0][2])
                for idx, (sqi, cs, t0, first, last) in enumerate(chunks):
                    bi = idx % NB
                    St, Sb = StS[sqi % 2], SbS[sqi % 2]
                    if idx + 1 < len(chunks):
                        loads((idx + 1) % NB, chunks[idx + 1][1], chunks[idx + 1][2])
                        nsq = chunks[idx + 1][0]
                        if chunks[idx + 1][3] and nsq >= 1:
                            self.ld(StS[nsq % 2], StS[nsq % 2][:], I["st_hg"][l, nsq - 1].rearrange("h d v -> d h v"))
                    lf_, q_, k_, v_, g_ = lf[bi], qT[bi], kT[bi], v[bi], g[bi]
                    tri = cf[:cs, C_TRI:C_TRI + cs]
                    Lm = cf[:cs, C_L:C_L + cs]
                    mid = cs // 2 - 1
                    if first:
                        if sqi == 0:
                            self.V(lambda: nc.vector.memset(St[:], 0.0), wr=[St])
                            self.V(lambda: nc.vector.memset(Sb[:], 0.0), wr=[Sb])
                        else:
                            self.A(lambda: nc.scalar.copy(out=Sb[:], in_=St[:]), rd=[St], wr=[Sb])
                    for q4 in range(4):
                        p = self.ps()
                        self.mm(p, [(p[:cs, :512], [(Lm, lf_[:cs, q4 * 512:(q4 + 1) * 512])])], [cf, lf_])
                        self.A(lambda: nc.scalar.activation(out=erem[:cs, q4 * 512:(q4 + 1) * 512], in_=p[:cs, :512], func=AF.Exp), rd=[p], wr=[erem])
                    self.A(lambda: nc.scalar.activation(out=ef[:cs, :], in_=lf_[:cs, :], func=AF.Exp), rd=[lf_], wr=[ef])
                    self.V(lambda: nc.vector.tensor_scalar(out=ef[:cs, :], in0=ef[:cs, :], scalar1=-1.0, scalar2=1.0, op0=ALU.mult, op1=ALU.add),
                           rd=[ef], wr=[ef])
                    self.V(lambda: nc.vector.tensor_tensor(out=khat[:cs, :], in0=ef[:cs, :], in1=erem[:cs, :], op=ALU.mult), rd=[ef, erem], wr=[khat])
                    hpb = min(16, 512 // cs)
                    for q2 in range(16 // hpb):
                        p = self.ps()
                        self.mm(p, [(p[:, hh * cs:(hh + 1) * cs], [(lf_[:cs, (q2 * hpb + hh) * 128:(q2 * hpb + hh + 1) * 128], tri)]) for hh in range(hpb)],
                                [lf_, cf])
                        self.V(lambda: nc.vector.tensor_copy(out=bT[:, q2 * hpb:(q2 + 1) * hpb, :cs],
                                                             in_=p[:, :hpb * cs].rearrange("p (a b) -> p a b", a=hpb)), rd=[p], wr=[bT])
                    self.V(lambda: nc.vector.tensor_tensor(out=dl[:, :, :cs], in0=bT[:, :, :cs],
                                                           in1=bT[:, :, mid:mid + 1].to_broadcast([128, 16, cs]), op=ALU.subtract), rd=[bT], wr=[dl])
                    self.A(lambda: nc.scalar.activation(out=e1[:, :, :cs], in_=dl[:, :, :cs], func=AF.Exp), rd=[dl], wr=[e1])
                    self.V(lambda: nc.vector.tensor_tensor(out=qt[:, :, :cs], in0=q_[:, :, :cs], in1=e1[:, :, :cs], op=ALU.mult), rd=[q_, e1], wr=[qt])
                    self.A(lambda: nc.scalar.activation(out=e1[:, :, :cs], in_=dl[:, :, :cs], func=AF.Exp, scale=-1.0), rd=[dl], wr=[e1])
                    self.V(lambda: nc.vector.tensor_tensor(out=ktl[:, :, :cs], in0=k_[:, :, :cs], in1=e1[:, :, :cs], op=ALU.mult), rd=[k_, e1], wr=[ktl])
                    self.A(lambda: nc.scalar.activation(out=e3[:, :, :cs], in_=bT[:, :, :cs], func=AF.Exp), rd=[bT], wr=[e3])
                    self.V(lambda: nc.vector.tensor_tensor(out=qb[:, :, :cs], in0=q_[:, :, :cs], in1=e3[:, :, :cs], op=ALU.mult), rd=[q_, e3], wr=[qb])
                    for q2 in range(16 // hpb):
                        p = self.ps()
                        self.mm(p, [(p[:cs, hh * cs:(hh + 1) * cs], [(ktl[:, q2 * hpb + hh, :cs], qt[:, q2 * hpb + hh, :cs])]) for hh in range(hpb)],
                                [ktl, qt])
                        self.V(lambda: nc.vector.tensor_tensor(out=attm[:cs, q2 * hpb:(q2 + 1) * hpb, :cs],
                                                               in0=p[:cs, :hpb * cs].rearrange("p (a b) -> p a b", a=hpb),
                                                               in1=self.bc(tri, [cs, hpb, cs], 1), op=ALU.mult), rd=[p, cf], wr=[attm])
                    for q4 in range(4):
                        p = self.ps()
                        self.mm(p, [(p[:cs, hh * 128:(hh + 1) * 128],
                                     [(attm[:cs, q4 * 4 + hh, :cs], v_[:cs, (q4 * 4 + hh) * 128:(q4 * 4 + hh + 1) * 128]),
                                      (qb[:, q4 * 4 + hh, :cs], Sb[:, q4 * 4 + hh, :])]) for hh in range(4)], [attm, v_, qb, Sb])
                        self.A(lambda: nc.scalar.copy(out=y[:cs, q4 * 512:(q4 + 1) * 512], in_=p[:cs, :512]), rd=[p], wr=[y])
                    self.A(lambda: nc.scalar.activation(out=sq[:cs, :], in_=y[:cs, :], func=AF.Square), rd=[y], wr=[sq])
                    self.V(lambda: nc.vector.tensor_reduce(out=ss[:cs, :], in_=sq[:cs, :].rearrange("p (g d) -> p g d", g=16), axis=AX.X, op=ALU.add),
                           rd=[sq], wr=[ss])
                    self.rstd(ss, cs, 16, 1.0 / 128, 1e-6)
                    self.V(lambda: nc.vector.tensor_tensor(out=y[:cs, :].rearrange("p (g d) -> p g d", g=16), in0=y[:cs, :].rearrange("p (g d) -> p g d", g=16),
                                                           in1=self.bc(ss[:cs, :], [cs, 16, 128], 2), op=ALU.mult), rd=[y, ss], wr=[y])
                    self.V(lambda: nc.vector.tensor_tensor(out=y[:cs, :], in0=y[:cs, :], in1=nw[:cs, :], op=ALU.mult), rd=[y, nw], wr=[y])
                    yo_ = yo[idx % 2]
                    self.V(lambda: nc.vector.tensor_tensor(out=yo_[:cs, :], in0=y[:cs, :], in1=g_[:cs, :], op=ALU.mult), rd=[y, g_], wr=[yo_])
                    self.st(S["YB"][2, t0:t0 + cs, :], yo_, yo_[:cs, :])
                    for q4 in range(4):
                        p = self.ps()
                        self.mm(p, [(p[:, hh * 128:(hh + 1) * 128],
                                     [(khat[:cs, (q4 * 4 + hh) * 128:(q4 * 4 + hh + 1) * 128], v_[:cs, (q4 * 4 + hh) * 128:(q4 * 4 + hh + 1) * 128])])
                                    for hh in range(4)], [khat, v_])
                        for hh in range(4):
                            h = q4 * 4 + hh
                            self.V(lambda: nc.vector.scalar_tensor_tensor(out=St[:, h, :], in0=St[:, h, :], scalar=e3[:, h, cs - 1:cs],
                                                                          in1=p[:, hh * 128:(hh + 1) * 128], op0=ALU.mult, op1=ALU.add),
                                   rd=[St, e3, p], wr=[St])
                    if not last:
                        self.A(lambda: nc.scalar.copy(out=Sb[:], in_=St[:]), rd=[St], wr=[Sb])
                    else:
                        self.st(O["o_hg"][l, sqi].rearrange("h d v -> d h v"), St, St[:])

    def fill_aT_from_T(self, src):
        with self.phase():
            tb = [self.sb([128, D], BF16) for _ in range(2)]
            for tt in range(NTILE):
                t = tb[tt % 2]
                self.ld(t, t[:], src[tt * 128:(tt + 1) * 128, :])
                self.transpose_to_aT(t, tt)

    def phase_branch(self, l, b):
        nc = self.nc
        I, S = self.I, self.S
        self.fill_aT_from_T(S["YB"][b])
        Wb = I["w_branch"][l, b]
        with self.phase():
            gt = [self.sb([128, NT], BF16) for _ in range(2)]
            mo = [self.sb([128, NT], BF16) for _ in range(2)]
            cnt = [0]

            def mk(cb):
                def fn(slots):
                    s = slots[0]
                    for j in range(4):
                        ch = cb * 4 + j
                        g_ = gt[cnt[0] % 2]
                        m_ = mo[cnt[0] % 2]
                        cnt[0] += 1
                        self.ld(g_, g_[:], S["GT"][b * D + ch * 128:b * D + (ch + 1) * 128, :])
                        for (t0, tn) in TBS:
                            p = self.ps()
                            self.mm(p, [(p[:, :tn], [(s[:, kc, j * 128:(j + 1) * 128], self.aTh[:, kc, t0:t0 + tn]) for kc in range(16)])],
                                    [s] + self.aT_tiles(t0, tn))
                            self.V(lambda: nc.vector.tensor_tensor(out=m_[:, t0:t0 + tn], in0=p[:, :tn], in1=g_[:, t0:t0 + tn], op=ALU.mult),
                                   rd=[p, g_], wr=[m_])
                        self.st(S["MB"][b, ch * 128:(ch + 1) * 128, :], m_, m_[:])
                return fn
            jobs = [([Wb[:, cb * 512:(cb + 1) * 512]], mk(cb)) for cb in range(4)]
            self.run_jobs(jobs, nslots=4)

    def store_Y(self, cb):
        nc = self.nc

        def evac(tt, p, nco):
            o = self.yo[self.yoc % 3]
            self.yoc += 1
            self.ev(lambda: nc.scalar.copy(out=o[:, :nco], in_=p[:, :nco]),
                    lambda: nc.vector.tensor_copy(out=o[:, :nco], in_=p[:, :nco]), rd=[p], wr=[o])
            self.st(self.S["Y"][tt * 128:(tt + 1) * 128, cb * 512:cb * 512 + nco], o, o[:, :nco])
        return evac

    def dense_to_Y(self, Wm):
        with self.phase():
            self.yo = [self.sb([128, 512], F32) for _ in range(3)]
            self.yoc = 0
            jobs = [self.job_T(Wm[:, cb * 512:(cb + 1) * 512], self.store_Y(cb)) for cb in range(4)]
            self.run_jobs(jobs, nslots=4)

    def phase_mixout(self, l):
        nc = self.nc
        S = self.S
        with self.phase():
            m = [[self.sb([128, NT], BF16) for _ in range(3)] for _ in range(2)]
            tf = [self.sb([128, NT], F32) for _ in range(2)]
            for ch in range(16):
                mm_ = m[ch % 2]
                t = tf[ch % 2]
                for b in range(3):
                    self.ld(mm_[b], mm_[b][:], S["MB"][b, ch * 128:(ch + 1) * 128, :])
                self.V(lambda: nc.vector.tensor_tensor(out=t[:], in0=mm_[0][:], in1=mm_[1][:], op=ALU.add), rd=[mm_[0], mm_[1]], wr=[t])
                self.V(lambda: nc.vector.tensor_tensor(out=self.aTh[:, ch, :], in0=t[:], in1=mm_[2][:], op=ALU.add), rd=[t, mm_[2]], wr=self.aT)
        self.dense_to_Y(self.I["w_mix_out"][l])

    def phase_q(self, l):
        nc = self.nc
        S = self.S
        Wq = self.I["x_wq"][l]
        with self.phase():
            ob = [self.sb([128, 512], BF16) for _ in range(3)]
            cnt = [0]

            def mk(cb):
                def evac(j, t0, tn, p):
                    o = ob[cnt[0] % 3]
                    cnt[0] += 1
                    sc = 512 ** -0.5
                    self.ev(lambda: nc.scalar.mul(out=o[:, :tn], in_=p[:, :tn], mul=sc),
                            lambda: nc.vector.tensor_scalar(out=o[:, :tn], in0=p[:, :tn], scalar1=sc, scalar2=None, op0=ALU.mult), rd=[p], wr=[o])
                    r0 = cb * 512 + j * 128
                    self.st(S["QT"][r0:r0 + 128, t0:t0 + tn], o, o[:, :tn])
                return evac
            jobs = [self.job_F(Wq[:, cb * 512:(cb + 1) * 512], mk(cb)) for cb in range(4)]
            self.run_jobs(jobs, nslots=4)

    def softmax_pt(self, np_, psA, psB, pe, pn, st):
        nc = self.nc
        for i, p in enumerate((psA, psB)):
            self.V(lambda: nc.vector.tensor_reduce(out=st[:np_, 2 * i:2 * i + 2], in_=p[:np_, :].rearrange("p (a b) -> p a b", a=2),
                                                   axis=AX.X, op=ALU.max), rd=[p], wr=[st])
        self.V(lambda: nc.vector.tensor_scalar(out=st[:np_, 4:8], in0=st[:np_, 0:4], scalar1=-1.0, scalar2=None, op0=ALU.mult), rd=[st], wr=[st])
        for h in range(4):
            p = (psA, psB)[h // 2]
            self.A(lambda: nc.scalar.activation(out=pe[:np_, h, :], in_=p[:np_, (h % 2) * 256:(h % 2 + 1) * 256], func=AF.Exp,
                                                bias=st[:np_, 4 + h:5 + h], scale=1.0), rd=[p, st], wr=[pe])
        self.V(lambda: nc.vector.tensor_reduce(out=st[:np_, 8:12], in_=pe[:np_, :, :], axis=AX.X, op=ALU.add), rd=[pe], wr=[st])
        self.V(lambda: nc.vector.reciprocal(out=st[:np_, 12:16], in_=st[:np_, 8:12]), rd=[st], wr=[st])
        self.V(lambda: nc.vector.tensor_tensor(out=pn[:np_, :, :], in0=pe[:np_, :, :], in1=self.bc(st[:np_, 12:16], [np_, 4, 256], 2), op=ALU.mult),
               rd=[pe, st], wr=[pn])

    def phase_attn(self, l):
        nc = self.nc
        I, S = self.I, self.S
        with self.phase():
            ktm = self.sb([128, 16, 256], BF16)
            vm = self.sb([128, 2, D], BF16)
            self.ld(ktm, ktm[:], S["KTM"][l].rearrange("(c p) m -> p c m", p=128))
            self.ld(vm, vm[:], S["VM"][l].rearrange("(mc p) v -> p mc v", p=128))
            qb = [self.sb([128, 16, 512], BF16) for _ in range(2)]
            pe = self.sb([128, 4, 256], F32)
            pn = self.sb([128, 4, 256], BF16)
            st = self.sb([128, 16], F32)
            pT = self.sb([128, 8, 128], BF16)

            def ldq(bi):
                t0, tn = TBS[bi]
                t = qb[bi % 2]
                self.ld(t, t[:, :, :tn], S["QT"][:, t0:t0 + tn].rearrange("(c p) t -> p c t", p=128))
                return t
            cur = ldq(0)
            for bi in range(4):
                nxt = ldq(bi + 1)
                for ti in range(4):
                    tt = bi * 4 + ti
                    tsl = slice(ti * 128, (ti + 1) * 128)
                    pss = [self.ps(), self.ps()]
                    for i in range(2):
                        self.mm(pss[i], [(pss[i][:, hh * 256:(hh + 1) * 256],
                                          [(cur[:, 4 * (2 * i + hh) + c, tsl], ktm[:, 4 * (2 * i + hh) + c, :]) for c in range(4)]) for hh in range(2)],
                                [cur, ktm])
                    self.softmax_pt(128, pss[0], pss[1], pe, pn, st)
                    for i in range(2):
                        p = self.ps()
                        self.mm(p, [(p[:, k * 128:(k + 1) * 128], [(pn[:, (i * 4 + k) // 2, ((i * 4 + k) % 2) * 128:((i * 4 + k) % 2 + 1) * 128], self.identb[:])])
                                    for k in range(4)], [pn, self.identb])
                        self.A(lambda: nc.scalar.copy(out=pT[:, i * 4:(i + 1) * 4, :], in_=p[:].rearrange("p (a b) -> p a b", a=4)), rd=[p], wr=[pT])
                    for h in range(4):
                        p = self.ps()
                        self.mm(p, [(p[:, c * 128:(c + 1) * 128],
                                     [(vm[:, mc, (4 * h + c) * 128:(4 * h + c + 1) * 128], pT[:, h * 2 + mc, :]) for mc in range(2)]) for c in range(4)],
                                [vm, pT])
                        o = self.aTh[:, 4 * h:4 * h + 4, tt * 128:(tt + 1) * 128]
                        i_ = p[:].rearrange("p (a b) -> p a b", a=4)
                        self.ev(lambda: nc.scalar.copy(out=o, in_=i_), lambda: nc.vector.tensor_copy(out=o, in_=i_), rd=[p], wr=[self.aT[tt]])
                cur = nxt
            qs = cur
            ks = [self.sb([128, 2, D], BF16) for _ in range(2)]
            vs = [self.sb([128, 2, D], BF16) for _ in range(2)]
            kts = self.sb([128, 16, 256], BF16)
            pTs = self.sb([128, 8, 8], BF16)

            def ldkv(s):
                self.ldc(ks[s % 2], ks[s % 2][:], I["ck"][l, s].rearrange("(mc p) d -> p mc d", p=128))
                self.ldc(vs[s % 2], vs[s % 2][:], I["cv"][l, s].rearrange("(mc p) d -> p mc d", p=128))
            ldkv(0)
            for s in range(NSEQ_S):
                if s + 1 < NSEQ_S:
                    ldkv(s + 1)
                k_, v_ = ks[s % 2], vs[s % 2]
                for c2 in range(8):
                    p = self.ps()
                    self.mm(p, [(p[:, (cc * 2 + mc) * 128:(cc * 2 + mc + 1) * 128],
                                 [(k_[:, mc, (c2 * 2 + cc) * 128:(c2 * 2 + cc + 1) * 128], self.identb[:])]) for cc in range(2) for mc in range(2)],
                            [k_, self.identb])
                    o = kts[:, c2 * 2:c2 * 2 + 2, :]
                    i_ = p[:].rearrange("p (a b) -> p a b", a=2)
                    self.ev(lambda: nc.scalar.copy(out=o, in_=i_), lambda: nc.vector.tensor_copy(out=o, in_=i_), rd=[p], wr=[kts])
                pss = [self.ps(), self.ps()]
                for i in range(2):
                    self.mm(pss[i], [(pss[i][:8, hh * 256:(hh + 1) * 256],
                                      [(qs[:, 4 * (2 * i + hh) + c, s * 8:(s + 1) * 8], kts[:, 4 * (2 * i + hh) + c, :]) for c in range(4)]) for hh in range(2)],
                            [qs, kts])
                self.softmax_pt(8, pss[0], pss[1], pe, pn, st)
                p = self.ps()
                self.mm(p, [(p[:, k * 8:(k + 1) * 8], [(pn[:8, k // 2, (k % 2) * 128:(k % 2 + 1) * 128], self.identb[:8, :8])]) for k in range(8)],
                        [pn, self.identb])
                self.A(lambda: nc.scalar.copy(out=pTs[:], in_=p[:, :64].rearrange("p (a b) -> p a b", a=8)), rd=[p], wr=[pTs])
                p2 = self.ps()
                self.mm(p2, [(p2[:, c * 8:(c + 1) * 8], [(v_[:, mc, c * 128:(c + 1) * 128], pTs[:, (c // 4) * 2 + mc, :]) for mc in range(2)]) for c in range(16)],
                        [v_, pTs])
                o = self.aTh[:, :, NPR + s * 8:NPR + (s + 1) * 8]
                self.V(lambda: nc.vector.tensor_copy(out=o, in_=p2[:, :128].rearrange("p (a b) -> p a b", a=16)), rd=[p2], wr=[self.aT[16]])

    def phase_wo(self, l):
        self.dense_to_Y(self.I["x_wo"][l])

    def phase_memkv(self):
        nc = self.nc
        I, S, O = self.I, self.S, self.O
        with self.phase():
            mT = self.sb([128, 16, 256], BF16)
            mf = self.sb([128, D], F32)
            mb = self.sb([128, D], BF16)
            for mc in range(2):
                self.ld(mf, mf[:], I["memp"][mc * 128:(mc + 1) * 128, :])
                self.A(lambda: nc.scalar.copy(out=mb[:], in_=mf[:]), rd=[mf], wr=[mb])
                for g in range(4):
                    p = self.ps()
                    self.mm(p, [(p[:, j * 128:(j + 1) * 128], [(mb[:, (g * 4 + j) * 128:(g * 4 + j + 1) * 128], self.identb[:])]) for j in range(4)],
                            [mb, self.identb])
                    self.V(lambda: nc.vector.tensor_copy(out=mT[:, g * 4:(g + 1) * 4, mc * 128:(mc + 1) * 128],
                                                         in_=p[:].rearrange("p (a b) -> p a b", a=4)), rd=[p], wr=[mT])
            of = [self.sb([128, 512], F32) for _ in range(3)]
            ob = [self.sb([128, 512], BF16) for _ in range(3)]
            cnt = [0]

            def mk(l, cb, is_k):
                def fn(slots):
                    s = slots[0]
                    for mc in range(2):
                        p = self.ps()
                        self.mm(p, [(p[:, :], [(mT[:, kc, mc * 128:(mc + 1) * 128], s[:, kc, :]) for kc in range(16)])], [mT, s])
                        o = of[cnt[0] % 3]
                        cnt[0] += 1
                        self.V(lambda: nc.vector.tensor_copy(out=o[:], in_=p[:]), rd=[p], wr=[o])
                        dst = O["o_mk"] if is_k else O["o_mv"]
                        self.st(dst[l, mc * 128:(mc + 1) * 128, cb * 512:(cb + 1) * 512], o, o[:])
                        if not is_k:
                            o2 = ob[cnt[0] % 3]
                            self.A(lambda: nc.scalar.copy(out=o2[:], in_=o[:]), rd=[o], wr=[o2])
                            self.st(S["VM"][l, mc * 128:(mc + 1) * 128, cb * 512:(cb + 1) * 512], o2, o2[:])
                    if is_k:
                        for j in range(4):
                            p = self.ps()
                            self.mm(p, [(p[:, :256], [(s[:, kc, j * 128:(j + 1) * 128], mT[:, kc, :]) for kc in range(16)])], [mT, s])
                            o2 = ob[cnt[0] % 3]
                            cnt[0] += 1
                            self.A(lambda: nc.scalar.copy(out=o2[:, :256], in_=p[:, :256]), rd=[p], wr=[o2])
                            r0 = cb * 512 + j * 128
                            self.st(S["KTM"][l, r0:r0 + 128, :], o2, o2[:, :256])
                return fn
            jobs = []
            for l in range(DEPTH):
                for cb in range(4):
                    jobs.append(([I["x_wk"][l][:, cb * 512:(cb + 1) * 512]], mk(l, cb, True)))
                for cb in range(4):
                    jobs.append(([I["x_wv"][l][:, cb * 512:(cb + 1) * 512]], mk(l, cb, False)))
            self.run_jobs(jobs, nslots=4)


_CACHE = {}


def get_program(debug=False, stop=None):
    key = (debug, stop)
    if key not in _CACHE:
        k = K(debug=debug, stop=stop)
        k.build()
        _CACHE[key] = k
    return _CACHE[key]


def make_in_maps(inputs):
    f = lambda a: np.ascontiguousarray(np.asarray(a, dtype=np.float32))
    g = {k: f(v) for k, v in inputs.items()}
    consts = make_consts()
    maps = []
    for c in range(8):
        ps = c % 4
        sl = slice(c * NSEQ_S, (c + 1) * NSEQ_S)
        m = {
            "xp": g["x_prompt"][ps], "xsm": g["x_sample"][sl].reshape(NSM, D), "memp": g["mem_prompt"][ps],
            "st_conv": f(g["state_ssd_conv"][:, sl]), "st_ssd": f(g["state_ssd"][:, sl]).reshape(DEPTH, NSEQ_S, 2048, 128),
            "st_mc": f(g["state_mlstm_c"][:, sl]), "st_mn": f(g["state_mlstm_n"][:, sl]).reshape(DEPTH, NSEQ_S, 8, 128),
            "st_mm": f(g["state_mlstm_m"][:, sl]), "st_hg": f(g["state_hgrn"][:, sl]),
            "ck": f(g["cache_mem_k"][:, sl]).reshape(DEPTH, NSEQ_S, 256, D), "cv": f(g["cache_mem_v"][:, sl]).reshape(DEPTH, NSEQ_S, 256, D),
            "ml_gate_bias": g["ml_gate_bias"].reshape(DEPTH, 8), "consts": consts,
        }
        for nm in ("ln_g", "ln_b", "ffn_w1", "ffn_w3", "ffn_w2", "w_in", "ssd_conv_w", "ssd_conv_b", "ssd_dt_bias", "ssd_a_log",
                   "ssd_d", "ssd_norm", "ml_norm", "hg_lb_logits", "hg_norm", "w_branch", "w_mix_out", "x_wq", "x_wk", "x_wv", "x_wo"):
            m[nm] = g[nm]
        maps.append(m)
    return maps


def assemble(res):
    R = res
    y_prompt = np.stack([R[c]["y"][:NPR] for c in range(4)])
    y_sample = np.concatenate([R[c]["y"][NPR:].reshape(NSEQ_S, LS, D) for c in range(8)])

    def pstate(nm, shp):
        return np.stack([R[c][nm][:, 0] for c in range(4)], axis=1).reshape(shp)

    def sstate(nm, shp):
        return np.concatenate([R[c][nm][:, 1:] for c in range(8)], axis=1).reshape(shp)

    p_conv = pstate("o_conv", (DEPTH, 4, 3, 3072)); s_conv = sstate("o_conv", (DEPTH, 128, 3, 3072))
    p_ssd = pstate("o_ssd", (DEPTH, 4, 32, 64, 128)); s_ssd = sstate("o_ssd", (DEPTH, 128, 32, 64, 128))
    p_mc = pstate("o_mc", (DEPTH, 4, 4, 256, 512)); s_mc = sstate("o_mc", (DEPTH, 128, 4, 256, 512))
    p_mn = pstate("o_mn", (DEPTH, 4, 4, 256)); s_mn = sstate("o_mn", (DEPTH, 128, 4, 256))
    p_mm = pstate("o_mm", (DEPTH, 4, 4)); s_mm = sstate("o_mm", (DEPTH, 128, 4))
    p_hg = pstate("o_hg", (DEPTH, 4, 16, 128, 128)); s_hg = sstate("o_hg", (DEPTH, 128, 16, 128, 128))
    p_mk = np.stack([R[c]["o_mk"] for c in range(4)], axis=1).reshape(DEPTH, 4, 256, 4, 512)
    p_mv = np.stack([R[c]["o_mv"] for c in range(4)], axis=1).reshape(DEPTH, 4, 256, 4, 512)
    outs = (y_prompt, y_sample, p_conv, p_ssd, p_mc, p_mn, p_mm, p_hg, p_mk, p_mv, s_conv, s_ssd, s_mc, s_mn, s_mm, s_hg)
    return tuple(np.ascontiguousarray(o, dtype=np.float32) for o in outs)


def kernel(**inputs):
    k = get_program()
    maps = make_in_maps(inputs)
    res = run_bass_kernel_spmd(k.nc, maps, core_ids=list(range(8)))
    return assemble(res.results)
```
